# Optimizing a Trainium2 kernel written in Bass

```python
import math
import jax, jax.numpy as jnp
from jax import lax
import numpy as np

D_MODEL = 1024
BATCH = 16
SEQ = 2048
DEPTH = 1
DEC_BATCH = 16
DEC_SEQ = 4096
PAST_LEN = 128

D_CONV = D_MODEL // 2
D_SSM = D_MODEL - D_CONV
D_MIX = D_CONV + D_SSM
CONV_HEADS = 8
SSM_GROUP = 16
SSM_GROUPS = D_SSM // SSM_GROUP
SSM_STATE = 64
D_IN = 3 * D_CONV + D_SSM
D_FF = 2816
EPS = 1e-6
DT_MIN = 1e-3
DT_MAX = 1e-1

kernel_name = "hymba_conv_s5_sandwich_encoder"


def rms_norm(x, g):
    xf = x.astype(jnp.float32)
    y = xf * lax.rsqrt(jnp.mean(xf * xf, axis=-1, keepdims=True) + EPS)
    return (y * g.astype(jnp.float32)).astype(x.dtype)


def dwconv3(x, w):
    L = x.shape[1]
    xp = jnp.pad(x, ((0, 0), (1, 1), (0, 0)))
    return xp[:, 0:L] * w[0] + xp[:, 1:L + 1] * w[1] + xp[:, 2:L + 2] * w[2]


def _linear_recurrence_combine(e1, e2):
    a1, b1 = e1
    a2, b2 = e2
    return (a1 * a2, a2 * b1 + b2)


def _s5_direction(u, lam_re, lam_im, log_step, b_re, b_im, c_re, c_im):
    L = u.shape[0]
    f32 = jnp.float32
    lam = lax.complex(lam_re.astype(f32), lam_im.astype(f32))
    dt = jnp.exp(log_step.astype(f32))[:, None]
    lam_bar = jnp.exp(lam * dt)
    bmat = lax.complex(b_re.astype(f32), b_im.astype(f32))
    b_bar = ((lam_bar - 1.0) / lam)[..., None] * bmat
    bu = jnp.einsum('lbgh,gph->lbgp', u.astype(jnp.complex64), b_bar)
    a = jnp.broadcast_to(lam_bar[None, None], (L, 1) + lam_bar.shape)
    _, s = lax.associative_scan(_linear_recurrence_combine, (a, bu), axis=0)
    cmat = lax.complex(c_re.astype(f32), c_im.astype(f32))
    return jnp.real(jnp.einsum('lbgp,ghp->lbgh', s, cmat))


def s5_mixer(u, lam_re, lam_im, log_step, b_re, b_im, c_re, c_im, d_skip, w_glu, b_glu):
    Bt, L, _ = u.shape
    uf = u.astype(jnp.float32)
    ul = uf.reshape(Bt, L, SSM_GROUPS, SSM_GROUP).transpose(1, 0, 2, 3)
    y_fwd = _s5_direction(ul, lam_re[0], lam_im[0], log_step[0], b_re[0], b_im[0], c_re[0], c_im[0])
    y_bwd = jnp.flip(_s5_direction(jnp.flip(ul, 0), lam_re[1], lam_im[1], log_step[1],
                                   b_re[1], b_im[1], c_re[1], c_im[1]), 0)
    y = (y_fwd + y_bwd).transpose(1, 0, 2, 3).reshape(Bt, L, D_SSM) + uf * d_skip.astype(jnp.float32)
    y = jax.nn.gelu(y).astype(u.dtype)
    return y * jax.nn.sigmoid(y @ w_glu + b_glu)


def encoder_layer(x, pre_mix_g, w_in, conv_w, lam_re, lam_im, log_step, b_re, b_im, c_re, c_im,
                  d_skip, w_glu, b_glu, gn_conv, gn_ssm, w_out, post_mix_g,
                  pre_ffn_g, w_up, ffn_conv_w, ffn_conv_b, w_down, post_ffn_g):
    h = rms_norm(x, pre_mix_g)
    z = h @ w_in
    zb = z[..., 0:D_CONV]
    zc = z[..., D_CONV:2 * D_CONV]
    zx = z[..., 2 * D_CONV:3 * D_CONV]
    zu = z[..., 3 * D_CONV:]
    y_conv = zb * dwconv3(zc * zx, conv_w)
    y_ssm = s5_mixer(zu, lam_re, lam_im, log_step, b_re, b_im, c_re, c_im, d_skip, w_glu, b_glu)
    y = jnp.concatenate([rms_norm(y_conv, gn_conv), rms_norm(y_ssm, gn_ssm)], axis=-1) @ w_out
    x = x + rms_norm(y, post_mix_g)
    h = rms_norm(x, pre_ffn_g)
    up = dwconv3(h @ w_up, ffn_conv_w) + ffn_conv_b
    gate = up[..., :D_FF]
    val = up[..., D_FF:]
    f = (jax.nn.silu(gate) * val) @ w_down
    return x + rms_norm(f, post_ffn_g)


def setup_inputs(seed: int = 0) -> dict:
    key = jax.random.key(seed)
    ks = jax.random.split(key, 32)
    f32 = jnp.float32
    nrm = lambda k, shape, s: jax.random.normal(k, shape, f32) * s
    gain = lambda k, n: 1.0 + 0.01 * jax.random.normal(k, (DEPTH, n), f32)
    G, P, H = SSM_GROUPS, SSM_STATE, SSM_GROUP
    n_idx = jnp.arange(P, dtype=f32)
    lam_re = -0.5 + 0.01 * jax.random.normal(ks[2], (DEPTH, 2, G, P), f32)
    lam_im = math.pi * n_idx + 0.01 * jax.random.normal(ks[3], (DEPTH, 2, G, P), f32)
    log_step = jax.random.uniform(ks[4], (DEPTH, 2, G), f32, math.log(DT_MIN), math.log(DT_MAX))
    return {
        "x_prompt": jax.random.normal(ks[0], (BATCH, SEQ, D_MODEL), f32),
        "x_sample": jax.random.normal(ks[1], (DEC_BATCH, DEC_SEQ, D_MODEL), f32),
        "pre_mix_g": gain(ks[5], D_MODEL),
        "w_in": nrm(ks[6], (DEPTH, D_MODEL, D_IN), D_MODEL ** -0.5),
        "conv_w": nrm(ks[7], (DEPTH, 3, D_CONV), 3 ** -0.5),
        "lam_re": lam_re,
        "lam_im": lam_im,
        "log_step": log_step,
        "b_re": nrm(ks[8], (DEPTH, 2, G, P, H), (2 * H) ** -0.5),
        "b_im": nrm(ks[9], (DEPTH, 2, G, P, H), (2 * H) ** -0.5),
        "c_re": nrm(ks[10], (DEPTH, 2, G, H, P), (2 * P) ** -0.5),
        "c_im": nrm(ks[11], (DEPTH, 2, G, H, P), (2 * P) ** -0.5),
        "d_skip": nrm(ks[12], (DEPTH, D_SSM), 1.0),
        "w_glu": nrm(ks[13], (DEPTH, D_SSM, D_SSM), D_SSM ** -0.5),
        "b_glu": nrm(ks[14], (DEPTH, D_SSM), 0.01),
        "gn_conv": gain(ks[15], D_CONV),
        "gn_ssm": gain(ks[16], D_SSM),
        "w_out": nrm(ks[17], (DEPTH, D_MIX, D_MODEL), D_MIX ** -0.5),
        "post_mix_g": gain(ks[18], D_MODEL),
        "pre_ffn_g": gain(ks[19], D_MODEL),
        "w_up": nrm(ks[20], (DEPTH, D_MODEL, 2 * D_FF), D_MODEL ** -0.5),
        "ffn_conv_w": nrm(ks[21], (DEPTH, 3, 2 * D_FF), 3 ** -0.5),
        "ffn_conv_b": nrm(ks[22], (DEPTH, 2 * D_FF), 0.01),
        "w_down": nrm(ks[23], (DEPTH, D_FF, D_MODEL), D_FF ** -0.5),
        "post_ffn_g": gain(ks[24], D_MODEL),
    }


def reference(x_prompt, x_sample, pre_mix_g, w_in, conv_w, lam_re, lam_im, log_step, b_re, b_im,
              c_re, c_im, d_skip, w_glu, b_glu, gn_conv, gn_ssm, w_out, post_mix_g,
              pre_ffn_g, w_up, ffn_conv_w, ffn_conv_b, w_down, post_ffn_g):
    def trunk(x):
        for l in range(DEPTH):
            x = encoder_layer(x, pre_mix_g[l], w_in[l], conv_w[l], lam_re[l], lam_im[l], log_step[l],
                              b_re[l], b_im[l], c_re[l], c_im[l], d_skip[l], w_glu[l], b_glu[l],
                              gn_conv[l], gn_ssm[l], w_out[l], post_mix_g[l],
                              pre_ffn_g[l], w_up[l], ffn_conv_w[l], ffn_conv_b[l], w_down[l], post_ffn_g[l])
        return x
    y_prompt = trunk(x_prompt)
    y_sample = trunk(x_sample)
    return (y_prompt, y_sample)
```

```python
import contextlib
import math
import numpy as np
import concourse.bass as bass
import concourse.mybir as mybir
from concourse.bass_utils import run_bass_kernel_spmd

F32, BF16 = mybir.dt.float32, mybir.dt.bfloat16
AF, ALU = mybir.ActivationFunctionType, mybir.AluOpType

D = 1024
DC = 512
DIN = 2048
DFF = 2816
NM_UP = 44
EPS = 1e-6
MAGIC = 12582912.0
TWO_PI = 2.0 * math.pi


def _nm(ap):
    return ap.name


class Sched:
    def __init__(self, nc, es):
        self.nc, self.es = nc, es
        self.engs = {"pe": nc.tensor, "act": nc.scalar, "dve": nc.vector, "pool": nc.gpsimd, "sp": nc.sync}
        self.sems, self.cnt = {}, {}
        self.seen = {e: {} for e in self.engs}
        self.lastw, self.readers = {}, {}

    def _sem(self, key):
        if key not in self.sems:
            self.sems[key] = self.es.enter_context(self.nc.semaphore("s%d" % len(self.sems)))
            self.cnt[key] = 0
        return self.sems[key]

    def _wait(self, eng, deps):
        for (k, v) in deps:
            if eng == "pe" and k == "pe":
                continue
            if self.seen[eng].get(k, 0) < v:
                self.engs[eng].wait_ge(self._sem(k), v)
                self.seen[eng][k] = v

    def _deps(self, reads, writes):
        deps = []
        for b in reads:
            if b in self.lastw:
                deps.append(self.lastw[b])
        for b in writes:
            if b in self.lastw:
                deps.append(self.lastw[b])
            deps.extend(self.readers.get(b, {}).items())
        return deps

    def _record(self, ev, reads, writes):
        for b in writes:
            self.lastw[b] = ev
            self.readers[b] = {}
        for b in reads:
            r = self.readers.setdefault(b, {})
            if r.get(ev[0], 0) < ev[1]:
                r[ev[0]] = ev[1]

    def op(self, eng, fn, reads=(), writes=(), inc=True):
        self._wait(eng, self._deps(reads, writes))
        ins = fn(self.engs[eng])
        self._sem(eng)
        if inc:
            self.cnt[eng] += 1
            ins.then_inc(self.sems[eng], 1)
            ev = (eng, self.cnt[eng])
        else:
            ev = (eng, self.cnt[eng] + 1)
        self._record(ev, reads, writes)
        return ev

    def dma(self, q, semkey, out, in_, reads=None, writes=None, **kw):
        reads = [_nm(in_)] if reads is None else reads
        writes = [_nm(out)] if writes is None else writes
        self._wait(q, self._deps(reads, writes))
        s = self._sem(semkey)
        self.cnt[semkey] += 16
        self.engs[q].dma_start(out=out, in_=in_, **kw).then_inc(s, 16)
        ev = (semkey, self.cnt[semkey])
        self._record(ev, reads, writes)
        return ev

    def wait_all(self, eng):
        mx = {}
        for (k, v) in self.lastw.values():
            mx[k] = max(mx.get(k, 0), v)
        for r in self.readers.values():
            for k, v in r.items():
                mx[k] = max(mx.get(k, 0), v)
        self._wait(eng, [(k, min(v, self.cnt[k])) for k, v in mx.items() if k != eng or eng != "pe"])

    def barrier(self):
        for e in self.engs:
            self.wait_all(e)
        self.lastw, self.readers = {}, {}

    def tt(self, eng, out, in0, in1, op, rk=None, wk=None):
        return self.op(eng, lambda e: e.tensor_tensor(out=out, in0=in0, in1=in1, op=op),
                       reads=rk if rk is not None else [_nm(in0), _nm(in1)], writes=wk if wk is not None else [_nm(out)])

    def ts(self, eng, out, in0, s1, s2, op0, op1=None, rk=None, wk=None):
        r = [_nm(in0)] + [_nm(s) for s in (s1, s2) if hasattr(s, "name")]
        if op1 is None:
            f = lambda e: e.tensor_scalar(out=out, in0=in0, scalar1=s1, scalar2=None, op0=op0)
        else:
            f = lambda e: e.tensor_scalar(out=out, in0=in0, scalar1=s1, scalar2=s2, op0=op0, op1=op1)
        return self.op(eng, f, reads=rk if rk is not None else r, writes=wk if wk is not None else [_nm(out)])

    def stt(self, out, in0, scalar, in1, op0, op1, rk=None, wk=None):
        r = [_nm(in0), _nm(in1)] + ([_nm(scalar)] if hasattr(scalar, "name") else [])
        return self.op("dve", lambda e: e.scalar_tensor_tensor(out=out, in0=in0, scalar=scalar, in1=in1, op0=op0, op1=op1),
                       reads=rk if rk is not None else r, writes=wk if wk is not None else [_nm(out)])

    def act(self, out, in_, func, bias=None, scale=None, accum_out=None, rk=None, wk=None):
        kw = {}
        r = [_nm(in_)]
        w = [_nm(out)]
        if bias is not None:
            kw["bias"] = bias
            if hasattr(bias, "name"):
                r.append(_nm(bias))
        if scale is not None:
            kw["scale"] = scale
            if hasattr(scale, "name"):
                r.append(_nm(scale))
        if accum_out is not None:
            kw["accum_out"] = accum_out
            w.append(_nm(accum_out))
        return self.op("act", lambda e: e.activation(out=out, in_=in_, func=func, **kw),
                       reads=rk if rk is not None else r, writes=wk if wk is not None else w)

    def copy(self, eng, out, in_, rk=None, wk=None):
        if eng == "act":
            return self.act(out, in_, AF.Copy, rk=rk, wk=wk)
        return self.op(eng, lambda e: e.tensor_copy(out=out, in_=in_),
                       reads=rk if rk is not None else [_nm(in_)], writes=wk if wk is not None else [_nm(out)])

    def memset(self, eng, out, val, wk=None):
        return self.op(eng, lambda e: e.memset(out, val), reads=[], writes=wk if wk is not None else [_nm(out)])

    def mm(self, out, lhsT, rhs, start, stop, inc=None, tp=None, rk=None, wk=None):
        inc = stop if inc is None else inc
        kw = {} if tp is None else {"tile_position": tp}
        return self.op("pe", lambda e: e.matmul(out, lhsT=lhsT, rhs=rhs, start=start, stop=stop, **kw),
                       reads=rk if rk is not None else [_nm(lhsT), _nm(rhs)],
                       writes=wk if wk is not None else [_nm(out)], inc=inc)

    def tr(self, out, in_, ident, inc=True, tp=None, rk=None, wk=None):
        kw = {} if tp is None else {"tile_position": tp}
        return self.op("pe", lambda e: e.transpose(out, in_, ident, **kw),
                       reads=rk if rk is not None else [_nm(in_), _nm(ident)],
                       writes=wk if wk is not None else [_nm(out)], inc=inc)

    def scan(self, out, d0, d1, init=0.0):
        return self.op("dve", lambda e: e.tensor_tensor_scan(out=out, data0=d0, data1=d1, initial=init,
                                                             op0=ALU.mult, op1=ALU.add),
                       reads=[_nm(d0), _nm(d1)], writes=[_nm(out)])


class Ctx:
    pass


def pipeline(n, stages):
    for t in range(n + len(stages) - 1):
        for k, st in enumerate(stages):
            i = t - k
            if 0 <= i < n:
                st(i)


def rsqrt_act(S, out, in_, scale, tmp, eps_ap, rk=None, wk=None, tk=None):
    S.act(tmp, in_, AF.Ln, scale=scale, bias=eps_ap, rk=rk, wk=tk)
    S.act(out, tmp, AF.Exp, scale=-0.5, rk=tk, wk=wk)


def build(seq_lens, debug=False):
    NT = sum(seq_lens)
    assert all(L % 1024 == 0 for L in seq_lens)
    nc = bass.Bass("TRN2", target_bir_lowering=False)
    c = Ctx()
    c.nc, c.NT, c.seq_lens, c.debug = nc, NT, seq_lens, debug
    c.seqs = []
    t = 0
    for L in seq_lens:
        c.seqs.append((t, L))
        t += L

    def din(name, shape):
        return nc.dram_tensor(name, list(shape), F32, kind="ExternalInput").ap()

    c.x = din("x", [NT, D])
    c.pre_mix_g = din("pre_mix_g", [1, D])
    c.w_in = din("w_in", [D, DIN])
    c.conv_w = din("conv_w", [3, DC])
    c.lam_re = din("lam_re", [2, 32, 64])
    c.lam_im = din("lam_im", [2, 32, 64])
    c.log_step = din("log_step", [2, 32])
    c.b_re = din("b_re", [2, 32, 64, 16])
    c.b_im = din("b_im", [2, 32, 64, 16])
    c.c_re = din("c_re", [2, 32, 16, 64])
    c.c_im = din("c_im", [2, 32, 16, 64])
    c.d_skip = din("d_skip", [1, DC])
    c.w_glu = din("w_glu", [DC, DC])
    c.b_glu = din("b_glu", [1, DC])
    c.gn_conv = din("gn_conv", [1, DC])
    c.gn_ssm = din("gn_ssm", [1, DC])
    c.w_out = din("w_out", [D, D])
    c.post_mix_g = din("post_mix_g", [1, D])
    c.pre_ffn_g = din("pre_ffn_g", [1, D])
    c.w_up = din("w_up", [D, 2 * DFF])
    c.ffn_conv_w = din("ffn_conv_w", [3, 2 * DFF])
    c.ffn_conv_b = din("ffn_conv_b", [1, 2 * DFF])
    c.w_down = din("w_down", [DFF, D])
    c.post_ffn_g = din("post_ffn_g", [1, D])
    c.ident = din("ident", [128, 128])
    c.mask32 = din("mask32", [128, 32])
    c.y = nc.dram_tensor("y", [NT, D], F32, kind="ExternalOutput").ap()

    sk = "ExternalOutput" if debug else "Internal"
    c.zuT_d = nc.dram_tensor("zuT_d", [4, 128, NT], BF16, kind=sk).ap()
    c.ycat_d = nc.dram_tensor("ycat_d", [8, 128, NT], BF16, kind=sk).ap()
    c.yg_d = nc.dram_tensor("yg_d", [4, 128, NT], F32, kind=sk).ap()
    c.x1_d = nc.dram_tensor("x1_d", [NT, D], F32, kind=sk).ap()
    c.h2T_d = nc.dram_tensor("h2T_d", [8, 128, NT], BF16, kind=sk).ap()
    c.wup_d = nc.dram_tensor("wup_d", [NM_UP, 128, 8, 128], BF16, kind="Internal").ap()
    c.WX_d = nc.dram_tensor("WX_d", [4, 128, 4096], BF16, kind="Internal").ap()
    c.WY_d = nc.dram_tensor("WY_d", [32, 128, 576], BF16, kind="Internal").ap()
    c.IN_d = nc.dram_tensor("IN_d", [4, 128, 2048], BF16, kind="Internal").ap()

    with contextlib.ExitStack() as es:
        S = Sched(nc, es)
        c.S = S
        c.es = es
        E = es.enter_context
        c.identb = E(nc.sbuf_tensor("identb", [128, 128], BF16))
        c.identf = E(nc.sbuf_tensor("identf", [128, 128], F32))
        c.onesb = E(nc.sbuf_tensor("onesb", [128, 128], BF16))
        S.dma("pool", "c0", c.identb[:], c.ident[:, :])
        S.dma("sp", "c1", c.identf[:], c.ident[:, :])
        S.memset("dve", c.onesb[:], 1.0)
        c.epsc = E(nc.sbuf_tensor("epsc", [128, 1], F32))
        S.memset("dve", c.epsc[:], EPS)
        S.barrier()
        gsb = lambda n, sh: E(nc.sbuf_tensor(n, sh, F32))
        c.lamre, c.lamim, c.lst = gsb("lamre", [128, 32]), gsb("lamim", [128, 32]), gsb("lst", [128, 32])
        c.dsk, c.m32 = gsb("dsk", [128, 4]), gsb("m32", [128, 32])
        S.dma("act", "p0", c.lamre[:].rearrange("p (d r) -> p d r", d=2), c.lam_re.rearrange("d (r q) p -> (q p) d r", q=2),
              allow_slow_non_contiguous=True)
        S.dma("act", "p1", c.lamim[:].rearrange("p (d r) -> p d r", d=2), c.lam_im.rearrange("d (r q) p -> (q p) d r", q=2),
              allow_slow_non_contiguous=True)
        for q in range(2):
            src = c.log_step.rearrange("d (r q) -> q d r", q=2)[q:q + 1]
            S.dma("act", "p2", c.lst[64 * q:64 * q + 64, :].rearrange("p (d r) -> p d r", d=2), src.to_broadcast([64, 2, 16]),
                  allow_slow_non_contiguous=True)
        S.dma("act", "p7", c.dsk[:], c.d_skip[0, :].rearrange("(f p) -> p f", p=128), allow_slow_non_contiguous=True)
        S.dma("act", "p8", c.m32[:], c.mask32[:, :])
        stage = c.stage = ("ALL" if not debug else debug)
        phase_A(c)
        if stage in ("ALL", "B1", "B2", "C1", "C2"):
            phase_B1(c)
        if stage in ("ALL", "B2", "C1", "C2"):
            c.wo = E(nc.sbuf_tensor("wo", [128, 8, D], BF16))
            c.wdn = E(nc.sbuf_tensor("wdn", [128, 22, D], BF16))
            for kt in range(8):
                S.dma("pool", "wold", c.wo[:, kt, :], c.w_out[kt * 128:(kt + 1) * 128, :])
            for kt in range(22):
                S.dma("pool", "wdnld", c.wdn[:, kt, :], c.w_down[kt * 128:(kt + 1) * 128, :])
            phase_B2(c)
        if stage in ("ALL", "C1", "C2"):
            phase_C1(c)
        if stage in ("ALL", "C2"):
            phase_C2(c)
        for e in ("sp", "pool", "act", "dve", "pe"):
            S.wait_all(e)
    return nc


def phase_prep_wup(c):
    nc, S = c.nc, c.S
    with contextlib.ExitStack() as es:
        E = es.enter_context
        st = [E(nc.sbuf_tensor("wupst%d" % i, [128, 2 * DFF], BF16)) for i in range(2)]
        for kt in range(8):
            s = st[kt % 2]
            S.dma("pool", "wupst%d" % (kt % 2), s[:], c.w_up[kt * 128:(kt + 1) * 128, :])
            S.dma("sp", "wupwr%d" % (kt % 2), c.wup_d[:, :, kt, :].rearrange("m p c -> p m c"),
                  s[:].rearrange("p (m c) -> p m c", c=128), writes=["wup_d"])
        S.barrier()


def phase_A(c):
    nc, S = c.nc, c.S
    NT = c.NT
    with contextlib.ExitStack() as es:
        E = es.enter_context
        sb = lambda n, sh, dt: E(nc.sbuf_tensor(n, sh, dt))
        w_in_sb = sb("w_in_sb", [128, 8, DIN], BF16)
        gpre = sb("gpre", [128, D], F32)
        cw = sb("cw", [128, 4, 3], F32)
        gnc = sb("gnc", [128, 4], F32)
        xt = [sb("xt%d" % i, [128, 4, D], F32) for i in range(2)]
        junk = sb("junkA", [128, D], BF16)
        junk2 = sb("junkA2", [128, D], BF16)
        ss = [sb("ssA%d" % i, [128, 4], F32) for i in range(2)]
        var4 = [sb("var4A%d" % i, [128, 4], F32) for i in range(2)]
        rstd = [sb("rstdA%d" % i, [128, 4], F32) for i in range(2)]
        xn = [sb("xn%d" % i, [128, D], BF16) for i in range(4)]
        hT = [sb("hT%d" % i, [128, 8, 512], BF16) for i in range(2)]
        pbuf = [sb("pbuf%d" % i, [128, 4, 514], F32) for i in range(2)]
        zbb = [sb("zbb%d" % i, [128, 4, 512], F32) for i in range(2)]
        zcs = [sb("zcs%d" % i, [128, 512], F32) for i in range(4)]
        zus = [sb("zus%d" % i, [128, 4, 512], BF16) for i in range(2)]
        acc = [sb("accA%d" % i, [128, 512], F32) for i in range(4)]
        yc = sb("ycA", [128, 4, 512], F32)
        sq = [sb("sqA%d" % i, [128, 512], BF16) for i in range(4)]
        var = sb("varA", [128, 512], F32)
        rst = sb("rstA", [128, 512], F32)
        ycn = [sb("ycn%d" % i, [128, 4, 512], BF16) for i in range(2)]
        ps_t = [E(nc.psum_tensor("psA_t%d" % i, [128, D], BF16)) for i in range(2)]
        ps_z = [E(nc.psum_tensor("psA_z%d" % i, [128, 512], F32)) for i in range(5)]
        ps_s = E(nc.psum_tensor("psA_s", [128, 512], F32))

        for cb in range(4):
            S.dma("pool", "winld%d" % cb, w_in_sb[:, :, cb * 512:(cb + 1) * 512],
                  c.w_in[:, cb * 512:(cb + 1) * 512].rearrange("(k p) n -> p k n", p=128), writes=["w_in_sb_%d" % cb])
        S.dma("sp", "cA0", gpre[:], c.pre_mix_g.to_broadcast([128, D]))
        for j in range(3):
            S.dma("sp", "cA1", cw[:, :, j], c.conv_w[j, :].rearrange("(f p) -> p f", p=128), allow_slow_non_contiguous=True)
        S.dma("sp", "cA2", gnc[:], c.gn_conv[0, :].rearrange("(f p) -> p f", p=128), allow_slow_non_contiguous=True)

        tiles = []
        for (s0, L) in c.seqs:
            n = L // 512
            for i in range(n):
                tiles.append((s0 + i * 512, i == 0, i == n - 1))

        def load_x(i):
            t0 = tiles[i][0]
            S.dma("sp", "xld%d" % (i % 2), xt[i % 2][:], c.x[t0:t0 + 512, :].rearrange("(g p) d -> p g d", p=128))

        def fA1(i, g):
            s = i % 2
            if g % 2 == 0:
                S.act(junk[:], xt[s][:, g, :], AF.Square, accum_out=ss[s][:, g:g + 1], wk=["junkA", "ssA%d_%d" % (s, g)])
            else:
                S.op("dve", lambda e: e.scalar_tensor_tensor(out=junk2[:], in0=xt[s][:, g, :], scalar=1.0, in1=xt[s][:, g, :],
                                                               op0=ALU.mult, op1=ALU.mult, accum_out=ss[s][:, g:g + 1]),
                     reads=[_nm(xt[s][:])], writes=["junkA2", "ssA%d_%d" % (s, g)])
            rsqrt_act(S, rstd[s][:, g:g + 1], ss[s][:, g:g + 1], 1.0 / D, var4[s][:, g:g + 1], c.epsc[:],
                      rk=["ssA%d_%d" % (s, g)], tk=["var4A%d_%d" % (s, g)], wk=["rstdA%d_%d" % (s, g)])
            S.stt(xn[g][:], xt[s][:, g, :], rstd[s][:, g:g + 1], gpre[:], ALU.mult, ALU.mult,
                  rk=[_nm(xt[s][:]), "rstdA%d_%d" % (s, g), "gpre"])

        def fA2(i, g):
            pt = ps_t[g % 2]
            for kt in range(8):
                S.tr(pt[:, kt * 128:(kt + 1) * 128], xn[g][:, kt * 128:(kt + 1) * 128], c.identb[:], inc=(kt == 7))

        def fA3(i, g):
            s = i % 2
            S.copy("act", hT[s][:, :, g * 128:(g + 1) * 128], ps_t[g % 2][:].rearrange("p (k t) -> p k t", k=8),
                   wk=["hT%d_%d" % (s, g)])

        def front_items(i):
            it = [(lambda g=g: fA1(i, g)) for g in range(4)]
            for g in range(4):
                it.append(lambda g=g: fA2(i, g))
                it.append(lambda g=g: fA3(i, g))
            return it

        def c1(i, f):
            S.act(acc[f][:], pbuf[i % 2][:, f, 1:513], AF.Identity, scale=cw[:, f, 1:2])

        def c2(i, f):
            s = i % 2
            S.stt(acc[f][:], pbuf[s][:, f, 0:512], cw[:, f, 0:1], acc[f][:], ALU.mult, ALU.add)
            S.stt(acc[f][:], pbuf[s][:, f, 2:514], cw[:, f, 2:3], acc[f][:], ALU.mult, ALU.add)
            S.tt("dve", yc[:, f, :], acc[f][:], zbb[s][:, f, :], ALU.mult, wk=["ycA_%d" % f])

        def c3(i, f):
            S.act(sq[f][:], yc[:, f, :], AF.Square, rk=["ycA_%d" % f])
            S.mm(ps_s[:], c.onesb[:], sq[f][:], start=(f == 0), stop=(f == 3), inc=True)

        def c4(i):
            t0 = tiles[i][0]
            rsqrt_act(S, rst[:], ps_s[:], 1.0 / DC, var[:], c.epsc[:])
            yn = ycn[i % 2]
            for ff in range(4):
                S.stt(yn[:, ff, :], yc[:, ff, :], gnc[:, ff:ff + 1], rst[:], ALU.mult, ALU.mult,
                      rk=["ycA_%d" % ff, "gnc", "rstA"])
            S.dma("pool", "ycnst%d" % (i % 2), c.ycat_d[0:4, :, t0:t0 + 512].rearrange("f p t -> p f t"), yn[:],
                  writes=[("ycat_d", 0, t0)])

        def conv_items(i):
            order = [("c1", 0), ("c1", 1), ("c2", 0), ("c1", 2), ("c2", 1), ("c3", 0), ("c1", 3), ("c2", 2), ("c3", 1),
                     ("c2", 3), ("c3", 2), ("c3", 3)]
            fn = {"c1": c1, "c2": c2, "c3": c3}
            it = [(lambda k=k, f=f: fn[k](i, f)) for (k, f) in order]
            it.append(lambda: c4(i))
            return it

        load_x(0)
        for w_ in front_items(0):
            w_()
        bg = []
        for i, (t0, first, last) in enumerate(tiles):
            s = i % 2
            if i + 1 < len(tiles):
                load_x(i + 1)
                bg.extend(front_items(i + 1))
            def evac(m, s=s, first=first, last=last, i=i):
                pz = ps_z[m % 5]
                f = m % 4
                if m < 4:
                    S.copy("dve", zbb[s][:, f, :], pz[:])
                elif m < 8:
                    S.copy("dve", zcs[f][:], pz[:])
                elif m < 12:
                    S.tt("dve", pbuf[s][:, f, 1:513], pz[:], zcs[f][:], ALU.mult)
                else:
                    S.copy("act", zus[s][:, f, :].rearrange("p (j c) -> p c j", j=8), pz[:].rearrange("p (c j) -> p c j", j=8))
                if m == 11:
                    if first:
                        S.memset("pool", pbuf[s][:, :, 0:1], 0.0)
                    else:
                        S.copy("pool", pbuf[s][:, :, 0:1], pbuf[1 - s][:, :, 512:513])
                        S.copy("pool", pbuf[1 - s][:, :, 513:514], pbuf[s][:, :, 1:2])
                        bg.extend(conv_items(i - 1))
                    if last:
                        S.memset("pool", pbuf[s][:, :, 513:514], 0.0)

            for m in range(16):
                pz = ps_z[m % 5]
                for kt in range(8):
                    S.mm(pz[:], w_in_sb[:, kt, m * 128:(m + 1) * 128], hT[s][:, kt, :], start=(kt == 0), stop=(kt == 7),
                         rk=["w_in_sb_%d" % (m // 4)] + ["hT%d_%d" % (s, g) for g in range(4)])
                if m >= 1:
                    evac(m - 1)
                npop = -(-len(bg) // (16 - m)) if m >= 12 else min(len(bg), 2)
                for _ in range(npop):
                    bg.pop(0)()
            evac(15)
            S.dma("pool", "zust%d" % s, c.zuT_d[:, :, t0:t0 + 512].rearrange("q p t -> p q t"), zus[s][:],
                  writes=[("zuT_d", t0)])
            while bg:
                bg.pop(0)()
            if last:
                bg.extend(conv_items(i))
        while bg:
            bg.pop(0)()
        S.barrier()


def cmul(S, eng, o_re, o_im, x_re, x_im, y_re, y_im, t1, t2, conj_x=False):
    S.tt(eng, t1, x_re, y_re, ALU.mult)
    S.tt(eng, t2, x_im, y_im, ALU.mult)
    S.tt(eng, o_re, t1, t2, ALU.add if conj_x else ALU.subtract)
    S.tt(eng, t1, x_re, y_im, ALU.mult)
    S.tt(eng, t2, x_im, y_re, ALU.mult)
    S.tt(eng, o_im, t1, t2, ALU.subtract if conj_x else ALU.add)


def phase_B1(c):
    nc, S = c.nc, c.S
    with contextlib.ExitStack() as es:
        E = es.enter_context
        sb = lambda n, sh, dt: E(nc.sbuf_tensor(n, sh, dt))
        U8r = sb("U8r", [128, 32], F32)
        U8i = sb("U8i", [128, 32], F32)
        R8 = sb("R8", [128, 32], F32)
        with contextlib.ExitStack() as es1:
            WX = es1.enter_context(nc.sbuf_tensor("WX", [128, 4, 2, 2, 8, 128], BF16))
            WY = es1.enter_context(nc.sbuf_tensor("WY", [128, 32, 2, 9, 32], BF16))
            INTRA = es1.enter_context(nc.sbuf_tensor("INTRA", [128, 4, 64, 32], BF16))
            s5_prep(c, WX, WY, INTRA, U8r, U8i, R8)
            S.dma("sp", "spl0", c.WX_d.rearrange("q p x -> p q x"), WX[:].rearrange("p q a b e n -> p q (a b e n)"), writes=["WX_d"])
            S.dma("sp", "spl1", c.WY_d.rearrange("u p x -> p u x"), WY[:].rearrange("p u a k n -> p u (a k n)"), writes=["WY_d"])
            S.dma("sp", "spl2", c.IN_d.rearrange("q p x -> p q x"), INTRA[:].rearrange("p q j n -> p q (j n)"), writes=["IN_d"])
            S.barrier()
        s5_main(c, U8r, U8i, R8)


def s5_prep(c, WX, WY, INTRA, U8r, U8i, R8):
    nc, S = c.nc, c.S
    with contextlib.ExitStack() as es:
        E = es.enter_context
        sb = lambda n, sh, dt=F32: E(nc.sbuf_tensor(n, sh, dt))
        lamre, lamim, lst, dsk, m32 = c.lamre, c.lamim, c.lst, c.dsk, c.m32
        Bre, Bim = sb("Bre", [128, 32, 16]), sb("Bim", [128, 32, 16])
        S.dma("sp", "p3", Bre[:].rearrange("p (d r) h -> p d r h", d=2), c.b_re.rearrange("d (r q) p h -> (q p) d r h", q=2))
        S.dma("sp", "p4", Bim[:].rearrange("p (d r) h -> p d r h", d=2), c.b_im.rearrange("d (r q) p h -> (q p) d r h", q=2))
        Cre, Cim = sb("Cre", [128, 32, 16]), sb("Cim", [128, 32, 16])
        with contextlib.ExitStack() as es2:
            Cnr = es2.enter_context(nc.sbuf_tensor("Cnr", [16, 64, 64], F32))
            Cni = es2.enter_context(nc.sbuf_tensor("Cni", [16, 64, 64], F32))
            psC = [es2.enter_context(nc.psum_tensor("psC%d" % i, [128, 32, 16], F32)) for i in range(2)]
            S.dma("sp", "p5", Cnr[:], c.c_re.rearrange("d g h p -> h (d g) p"))
            S.dma("sp", "p6", Cni[:], c.c_im.rearrange("d g h p -> h (d g) p"))
            for ri, (Cn, Cst) in enumerate(((Cnr, Cre), (Cni, Cim))):
                for u in range(32):
                    d, pr = divmod(u, 16)
                    a = d * 32 + 2 * pr
                    S.tr(psC[ri][:, u, :], Cn[:, a:a + 2, :].rearrange("h g p -> h (g p)"), c.identf[0:16, 0:16], inc=(u == 31))
                S.copy("dve", Cst[:], psC[ri][:])
            S.barrier()

        dt, lr, th = sb("dtp", [128, 32]), sb("lr", [128, 32]), sb("th", [128, 32])
        mag, kf, thr = sb("mag", [128, 32]), sb("kf", [128, 32]), sb("thr", [128, 32])
        sh, ch, t1, t2 = sb("sh", [128, 32]), sb("ch", [128, 32]), sb("t1p", [128, 32]), sb("t2p", [128, 32])
        cs, sn = sb("cs", [128, 32]), sb("sn", [128, 32])
        hpi = sb("hpi", [128, 1])
        S.memset("dve", hpi[:], math.pi / 2)
        S.act(dt[:], lst[:], AF.Exp)
        S.tt("dve", lr[:], lamre[:], dt[:], ALU.mult)
        S.tt("dve", th[:], lamim[:], dt[:], ALU.mult)
        S.act(mag[:], lr[:], AF.Exp)
        S.act(R8[:], lr[:], AF.Exp, scale=8.0)
        S.ts("dve", kf[:], th[:], 1.0 / TWO_PI, MAGIC, ALU.mult, ALU.add)
        S.ts("dve", kf[:], kf[:], -MAGIC, -TWO_PI, ALU.add, ALU.mult)
        S.tt("dve", thr[:], th[:], kf[:], ALU.add)
        S.act(sh[:], thr[:], AF.Sin, scale=0.5)
        S.act(ch[:], thr[:], AF.Sin, scale=0.5, bias=hpi[:])
        S.tt("dve", t1[:], ch[:], ch[:], ALU.mult)
        S.tt("dve", t2[:], sh[:], sh[:], ALU.mult)
        S.tt("dve", cs[:], t1[:], t2[:], ALU.subtract)
        S.tt("dve", t1[:], sh[:], ch[:], ALU.mult)
        S.ts("dve", sn[:], t1[:], 2.0, None, ALU.mult)
        UPr, UPi = sb("UPr", [128, 32, 9]), sb("UPi", [128, 32, 9])
        APr, APi = sb("APr", [128, 32, 9]), sb("APi", [128, 32, 9])
        T1, T2 = sb("T1p", [128, 32, 4]), sb("T2p", [128, 32, 4])
        S.memset("dve", UPr[:, :, 0:1], 1.0)
        S.memset("dve", UPi[:, :, 0:1], 0.0)
        S.copy("dve", UPr[:, :, 1], cs[:])
        S.copy("dve", UPi[:, :, 1], sn[:])
        cmul(S, "dve", UPr[:, :, 2], UPi[:, :, 2], UPr[:, :, 1], UPi[:, :, 1], UPr[:, :, 1], UPi[:, :, 1], T1[:, :, 0], T2[:, :, 0])
        bc = lambda ap, n: ap.to_broadcast([128, 32, n])
        cmul(S, "dve", UPr[:, :, 3:5], UPi[:, :, 3:5], UPr[:, :, 1:3], UPi[:, :, 1:3], bc(UPr[:, :, 2:3], 2), bc(UPi[:, :, 2:3], 2),
             T1[:, :, 0:2], T2[:, :, 0:2])
        cmul(S, "dve", UPr[:, :, 5:9], UPi[:, :, 5:9], UPr[:, :, 1:5], UPi[:, :, 1:5], bc(UPr[:, :, 4:5], 4), bc(UPi[:, :, 4:5], 4),
             T1[:], T2[:])
        S.copy("dve", U8r[:], UPr[:, :, 8])
        S.copy("dve", U8i[:], UPi[:, :, 8])
        MG = sb("MG", [128, 32, 9])
        S.memset("dve", MG[:, :, 0:1], 1.0)
        for k in range(1, 9):
            S.act(MG[:, :, k], lr[:], AF.Exp, scale=float(k))
        S.tt("dve", APr[:], UPr[:], MG[:], ALU.mult)
        S.tt("dve", APi[:], UPi[:], MG[:], ALU.mult)
        nr, den, inv = sb("nr", [128, 32]), sb("den", [128, 32]), sb("inv", [128, 32])
        qr, qi = sb("qr", [128, 32]), sb("qi", [128, 32])
        S.ts("dve", nr[:], APr[:, :, 1], -1.0, None, ALU.add)
        S.tt("dve", t1[:], lamre[:], lamre[:], ALU.mult)
        S.tt("dve", t2[:], lamim[:], lamim[:], ALU.mult)
        S.tt("dve", den[:], t1[:], t2[:], ALU.add)
        S.op("dve", lambda e: e.reciprocal(out=inv[:], in_=den[:]), reads=["den"], writes=["inv"])
        S.tt("dve", t1[:], nr[:], lamre[:], ALU.mult)
        S.tt("dve", t2[:], APi[:, :, 1], lamim[:], ALU.mult)
        S.tt("dve", t1[:], t1[:], t2[:], ALU.add)
        S.tt("dve", qr[:], t1[:], inv[:], ALU.mult)
        S.tt("dve", t1[:], APi[:, :, 1], lamre[:], ALU.mult)
        S.tt("dve", t2[:], nr[:], lamim[:], ALU.mult)
        S.tt("dve", t1[:], t1[:], t2[:], ALU.subtract)
        S.tt("dve", qi[:], t1[:], inv[:], ALU.mult)
        Bbr, Bbi = sb("Bbr", [128, 32, 16]), sb("Bbi", [128, 32, 16])
        V1, V2 = sb("V1", [128, 32, 16]), sb("V2", [128, 32, 16])
        b16 = lambda ap: ap.unsqueeze(2).to_broadcast([128, 32, 16])
        cmul(S, "dve", Bbr[:], Bbi[:], b16(qr[:]), b16(qi[:]), Bre[:], Bim[:], V1[:], V2[:])
        Gbd = sb("Gbd", [128, 32, 2, 8, 32], BF16)
        Bbd = sb("Bbd", [128, 32, 2, 32], BF16)
        S.memset("pool", Gbd[:], 0.0)
        S.memset("pool", Bbd[:], 0.0)
        S.memset("pool", WY[:], 0.0)
        W1, W2 = sb("W1", [128, 32, 9, 16]), sb("W2", [128, 32, 9, 16])

        def bk(ap, n):
            return ap.unsqueeze(3).to_broadcast([128, 32, n, 16])

        def bh(ap, n):
            return ap.unsqueeze(2).to_broadcast([128, 32, n, 16])

        def halves(dst_fn, src, n):
            for g2 in range(2):
                P = slice(64 * g2, 64 * g2 + 64)
                S.copy("act", dst_fn(P, slice(16 * g2, 16 * g2 + 16)), src[P])

        S.tt("dve", W2[:, :, 0:8, :], bk(APi[:, :, 0:8], 8), bh(Bbi[:], 8), ALU.mult)
        S.tt("dve", W1[:, :, 0:8, :], bk(APr[:, :, 0:8], 8), bh(Bbr[:], 8), ALU.mult)
        S.tt("dve", W1[:, :, 0:8, :], W1[:, :, 0:8, :], W2[:, :, 0:8, :], ALU.subtract)
        halves(lambda P, Cc: Gbd[P, :, 0, :, Cc], W1[:, :, 0:8, :], 8)
        S.tt("dve", W2[:, :, 0:8, :], bk(APi[:, :, 0:8], 8), bh(Bbr[:], 8), ALU.mult)
        S.tt("dve", W1[:, :, 0:8, :], bk(APr[:, :, 0:8], 8), bh(Bbi[:], 8), ALU.mult)
        S.tt("dve", W1[:, :, 0:8, :], W1[:, :, 0:8, :], W2[:, :, 0:8, :], ALU.add)
        halves(lambda P, Cc: Gbd[P, :, 1, :, Cc], W1[:, :, 0:8, :], 8)
        halves(lambda P, Cc: Bbd[P, :, 0, Cc], Bbr[:], 1)
        halves(lambda P, Cc: Bbd[P, :, 1, Cc], Bbi[:], 1)
        S.tt("dve", W2[:], bk(APi[:], 9), bh(Cim[:], 9), ALU.mult)
        S.tt("dve", W1[:], bk(APr[:], 9), bh(Cre[:], 9), ALU.mult)
        S.tt("dve", W1[:], W1[:], W2[:], ALU.subtract)
        halves(lambda P, Cc: WY[P, :, 0, :, Cc], W1[:], 9)
        S.tt("dve", W2[:], bk(APr[:], 9), bh(Cim[:], 9), ALU.mult)
        S.tt("dve", W1[:], bk(APi[:], 9), bh(Cre[:], 9), ALU.mult)
        S.tt("dve", W1[:], W1[:], W2[:], ALU.add)
        S.ts("dve", W1[:], W1[:], -1.0, None, ALU.mult)
        halves(lambda P, Cc: WY[P, :, 1, :, Cc], W1[:], 9)

        Kps = [E(nc.psum_tensor("Kps%d" % d, [128, 4, 8, 32], F32)) for d in range(2)]
        for d in range(2):
            for pr in range(16):
                u = d * 16 + pr
                q, pq = divmod(pr, 4)
                for ri in range(2):
                    S.mm(Kps[d][32 * pq:32 * pq + 32, q, :, :], Bbd[:, u, ri, :], WY[:, u, ri, 0:8, :],
                         start=(ri == 0), stop=(ri == 1), inc=(ri == 1 and pr == 15), tp=(0, 32 * pq))
        Kf, Kb = sb("Kf", [128, 4, 8, 32]), sb("Kb", [128, 4, 8, 32])
        S.copy("dve", Kf[:], Kps[0][:])
        S.copy("dve", Kb[:], Kps[1][:])
        Dd = sb("Dd", [128, 4, 32])
        S.tt("dve", Dd[:], m32[:].unsqueeze(1).to_broadcast([128, 4, 32]), dsk[:].unsqueeze(2).to_broadcast([128, 4, 32]), ALU.mult)
        S.tt("dve", Dd[:], Dd[:], Kf[:, :, 0, :], ALU.add)
        S.tt("dve", Dd[:], Dd[:], Kb[:, :, 0, :], ALU.add)
        S.copy("dve", INTRA[:, :, 0:64:9, :], Dd[:].unsqueeze(2).to_broadcast([128, 4, 8, 32]))
        for k in range(1, 8):
            n = 8 - k
            S.copy("dve", INTRA[:, :, k:k + 9 * (n - 1) + 1:9, :], Kf[:, :, k:k + 1, :].to_broadcast([128, 4, n, 32]))
            S.copy("dve", INTRA[:, :, 8 * k:8 * k + 9 * (n - 1) + 1:9, :], Kb[:, :, k:k + 1, :].to_broadcast([128, 4, n, 32]))

        psT = [E(nc.psum_tensor("psWX%d" % i, [128, 8, 128], BF16)) for i in range(2)]
        n = 0
        for q in range(4):
            for d in range(2):
                for ri in range(2):
                    pt = psT[n % 2]
                    n += 1
                    for e in range(8):
                        for pq in range(4):
                            u = d * 16 + 4 * q + pq
                            S.tr(pt[32 * pq:32 * pq + 32, e, :], Gbd[:, u, ri, e, :], c.identb[:],
                                 inc=(e == 7 and pq == 3), tp=(0, 32 * pq))
                    S.copy("act", WX[:, q, d, ri, :, :], pt[:])
        S.barrier()


def s5_main(c, U8r, U8i, R8):
    nc, S = c.nc, c.S
    Lmax = max(c.seq_lens)
    NCm = Lmax // 8
    with contextlib.ExitStack() as es:
        E = es.enter_context
        sb = lambda n, sh, dt=F32: E(nc.sbuf_tensor(n, sh, dt))
        WXq = sb("WXq", [128, 2, 2, 8, 128], BF16)
        WYq = sb("WYq", [128, 8, 2, 9, 32], BF16)
        INq = sb("INq", [128, 64, 32], BF16)
        Ec, Es = sb("Ec", [128, 8, NCm]), sb("Es", [128, 8, NCm])
        pwr, pwi = sb("pwr", [128, 8]), sb("pwi", [128, 8])
        pt1, pt2, pt3 = sb("pt1", [128, 8]), sb("pt2", [128, 8]), sb("pt3", [128, 8])
        zum = [[sb("zum%d_%d" % (r, i), [128, Lmax], BF16) for i in range(4)] for r in range(2)]
        Sbf = [sb("Sbf%d" % r, [128, 8, 2, NCm + 1], BF16) for r in range(2)]
        tmps = [[sb("tm%s%d" % (nm, i), [128, NCm]) for nm in "ABCDEF"] for i in range(2)]
        yg = sb("ygB", [128, Lmax])
        ET1 = yg[:, 0:Lmax // 2].rearrange("p (u k) -> p u k", u=8)
        ET2 = yg[:, Lmax // 2:Lmax].rearrange("p (u k) -> p u k", u=8)
        Xps = [[E(nc.psum_tensor("Xps%d_%d" % (i, ri), [128, 512], F32)) for ri in range(2)] for i in range(2)]
        Yps = [E(nc.psum_tensor("Yps%d" % i, [128, 512], F32)) for i in range(2)]
        for r in range(2):
            S.memset("pool" if r == 0 else "dve", Sbf[r][:], 0.0)
            for pq in range(4):
                if r == 0:
                    S.memset("pool", zum[r][pq][:], 0.0)
                else:
                    S.op("act", lambda e, t_=zum[r][pq]: e.memzero(t_[:]), reads=[], writes=[_nm(zum[r][pq][:])])
        nseq = len(c.seqs)

        def scan_prologue(q):
            S.dma("sp", "wq0", WXq[:].rearrange("p a b e n -> p (a b e n)"), c.WX_d[q], reads=["WX_d"])
            for d in range(2):
                usl = slice(d * 16 + 4 * q, d * 16 + 4 * q + 4)
                S.copy("dve", pwr[:, 4 * d:4 * d + 4], U8r[:, usl])
                S.copy("dve", pwi[:, 4 * d:4 * d + 4], U8i[:, usl])
            S.memset("dve", Ec[:, :, 0:1], 1.0)
            S.memset("dve", Es[:, :, 0:1], 0.0)
            seg = 1
            while seg < NCm:
                bcs = lambda ap: ap.unsqueeze(2).to_broadcast([128, 8, seg])
                cmul(S, "dve", Ec[:, :, seg:2 * seg], Es[:, :, seg:2 * seg], Ec[:, :, 0:seg], Es[:, :, 0:seg],
                     bcs(pwr[:]), bcs(pwi[:]), ET1[:, :, 0:seg], ET2[:, :, 0:seg])
                seg *= 2
                if seg < NCm:
                    S.tt("dve", pt1[:], pwr[:], pwr[:], ALU.mult)
                    S.tt("dve", pt2[:], pwi[:], pwi[:], ALU.mult)
                    S.tt("dve", pt3[:], pwr[:], pwi[:], ALU.mult)
                    S.tt("dve", pwr[:], pt1[:], pt2[:], ALU.subtract)
                    S.ts("dve", pwi[:], pt3[:], 2.0, None, ALU.mult)

        def out_prologue(q):
            for d in range(2):
                u0 = d * 16 + 4 * q
                S.dma("sp", "wq1", WYq[:, 4 * d:4 * d + 4].rearrange("p u a k n -> p u (a k n)"),
                      c.WY_d[u0:u0 + 4].rearrange("u p x -> p u x"), reads=["WY_d"])
            S.dma("sp", "wq2", INq[:].rearrange("p j n -> p (j n)"), c.IN_d[q], reads=["IN_d"])

        if True:
            NIT = 4 * nseq

            def step(t):
                g, h = t, t - 1
                gv, hv = g < NIT, h >= 0
                if gv:
                    q, k = divmod(g, nseq)
                    if k == 0:
                        scan_prologue(q)
                    s0, L = c.seqs[k]
                    n_c = L // 8
                    r = g % 2
                    for pq in range(4):
                        P = slice(32 * pq, 32 * pq + 32)
                        S.dma("sp" if pq % 2 == 0 else "act", "zuld%d_%d" % (r, pq), zum[r][pq][P, 0:L], c.zuT_d[q, P, s0:s0 + L],
                              reads=[("zuT_d", s0 + 512 * i) for i in range(L // 512)])
                    zdi = [zum[r][pq][:, 0:L].rearrange("p (t j c) -> p t j c", j=8, c=64) for pq in range(4)]
                if hv:
                    qh, kh = divmod(h, nseq)
                    if kh == 0:
                        out_prologue(qh)
                    s0h, Lh = c.seqs[kh]
                    n_ch = Lh // 8
                    rh = h % 2
                    zdh = [zum[rh][pq][:, 0:Lh].rearrange("p (t j c) -> p t j c", j=8, c=64) for pq in range(4)]
                for pair in range(4):
                    if gv:
                        ops = []
                        for uu in (2 * pair, 2 * pair + 1):
                            d, pq = divmod(uu, 4)
                            u = d * 16 + 4 * q + pq
                            xp = Xps[uu % 2]
                            for ri in range(2):
                                for j in range(8):
                                    e = 7 - j if d == 0 else j
                                    S.mm(xp[ri][:, 0:n_c], WXq[:, d, ri, e, :], zdi[pq][:, :, j, :], start=(j == 0), stop=(j == 7))
                            ec, esn = Ec[:, uu, 0:n_c], Es[:, uu, 0:n_c]
                            if d == 0:
                                xr, xi = xp[0][:, 0:n_c], xp[1][:, 0:n_c]
                                o_re, o_im = Sbf[r][:, uu, 0, 1:n_c + 1], Sbf[r][:, uu, 1, 1:n_c + 1]
                            else:
                                xr, xi = xp[0][:, 0:n_c][:, ::-1], xp[1][:, 0:n_c][:, ::-1]
                                o_re, o_im = Sbf[r][:, uu, 0, 0:n_c][:, ::-1], Sbf[r][:, uu, 1, 0:n_c][:, ::-1]
                            a, b, cc, dd, ee, ff = [tt_[:, 0:n_c] for tt_ in tmps[uu % 2]]
                            r8 = R8[:, u:u + 1].to_broadcast([128, n_c])
                            chain = [
                                lambda a=a, xr=xr, ec=ec: S.tt("dve", a, xr, ec, ALU.mult),
                                lambda b=b, xi=xi, esn=esn: S.tt("dve", b, xi, esn, ALU.mult),
                                lambda a=a, b=b: S.tt("dve", a, a, b, ALU.add),
                                lambda cc=cc, xi=xi, ec=ec: S.tt("dve", cc, xi, ec, ALU.mult),
                                lambda dd=dd, xr=xr, esn=esn: S.tt("dve", dd, xr, esn, ALU.mult),
                                lambda cc=cc, dd=dd: S.tt("dve", cc, cc, dd, ALU.subtract),
                                lambda a=a, b=b, r8=r8: S.scan(b, r8, a),
                                lambda cc=cc, dd=dd, r8=r8: S.scan(dd, r8, cc),
                                lambda a=a, b=b, ec=ec: S.tt("dve", a, b, ec, ALU.mult),
                                lambda ee=ee, dd=dd, esn=esn: S.tt("pool", ee, dd, esn, ALU.mult),
                                lambda cc=cc, dd=dd, ec=ec: S.tt("dve", cc, dd, ec, ALU.mult),
                                lambda ff=ff, b=b, esn=esn: S.tt("pool", ff, b, esn, ALU.mult),
                                lambda o_re=o_re, a=a, ee=ee: S.tt("pool", o_re, a, ee, ALU.subtract),
                                lambda o_im=o_im, cc=cc, ff=ff: S.tt("pool", o_im, cc, ff, ALU.add),
                            ]
                            if d == 1:
                                chain.append(lambda uu=uu: S.memset("pool", Sbf[r][:, uu, :, n_c:n_c + 1], 0.0))
                            ops.append(chain)
                        for i_ in range(max(len(ch) for ch in ops)):
                            for ch in ops:
                                if i_ < len(ch):
                                    ch[i_]()
                    if hv:
                        for j in (2 * pair, 2 * pair + 1):
                            yp = Yps[j % 2]
                            for pq in range(4):
                                P = slice(32 * pq, 32 * pq + 32)
                                first = True
                                for d in range(2):
                                    uu = d * 4 + pq
                                    kk = j + 1 if d == 0 else 8 - j
                                    for ri in range(2):
                                        rhs = Sbf[rh][:, uu, ri, 0:n_ch] if d == 0 else Sbf[rh][:, uu, ri, 1:n_ch + 1]
                                        S.mm(yp[P, 0:n_ch], WYq[:, uu, ri, kk, :], rhs, start=first, stop=False, inc=False, tp=(0, 32 * pq))
                                        first = False
                            for pq in range(4):
                                P = slice(32 * pq, 32 * pq + 32)
                                for jp in range(8):
                                    S.mm(yp[P, 0:n_ch], INq[:, jp * 8 + j, :], zdh[pq][:, :, jp, :], start=False, stop=(jp == 7),
                                         inc=(jp == 7 and pq == 3), tp=(0, 32 * pq))
                            S.act(yg[:, j:Lh:8], yp[:, 0:n_ch], AF.Gelu_apprx_tanh)
                if hv:
                    S.dma("pool", "ygst", c.yg_d[qh, :, s0h:s0h + Lh], yg[:, 0:Lh], writes=[("yg_d", qh, s0h)])

            for t in range(NIT + 1):
                step(t)
        S.barrier()


def phase_B2(c):
    nc, S = c.nc, c.S
    with contextlib.ExitStack() as es:
        E = es.enter_context
        sb = lambda n, sh, dt=F32: E(nc.sbuf_tensor(n, sh, dt))
        wg = sb("wg", [128, 4, DC], BF16)
        bg, gns = sb("bg", [128, 4]), sb("gns", [128, 4])
        ygf = [sb("ygf%d" % i, [128, 4, 512]) for i in range(3)]
        ygb = [sb("ygb%d" % i, [128, 4, 512], BF16) for i in range(2)]
        sg = [sb("sg%d" % i, [128, 512]) for i in range(4)]
        y2 = [sb("y2_%d" % i, [128, 4, 512]) for i in range(2)]
        sq = [sb("sqB%d" % i, [128, 512], BF16) for i in range(4)]
        var = [sb("varB%d" % i, [128, 512]) for i in range(2)]
        rst = [sb("rstB%d" % i, [128, 512]) for i in range(2)]
        y2n = [sb("y2n%d" % i, [128, 4, 512], BF16) for i in range(2)]
        ps_g = [E(nc.psum_tensor("psB_g%d" % i, [128, 512], F32)) for i in range(3)]
        ps_s = [E(nc.psum_tensor("psB_s%d" % i, [128, 512], F32)) for i in range(2)]
        for kt in range(4):
            S.dma("pool", "wgld", wg[:, kt, :], c.w_glu[kt * 128:(kt + 1) * 128, :])
        S.dma("sp", "cB0", bg[:], c.b_glu[0, :].rearrange("(f p) -> p f", p=128), allow_slow_non_contiguous=True)
        S.dma("sp", "cB1", gns[:], c.gn_ssm[0, :].rearrange("(f p) -> p f", p=128), allow_slow_non_contiguous=True)
        tiles = []
        for (s0_, L) in c.seqs:
            for i in range(L // 512):
                tiles.append((s0_ + i * 512, s0_))
        NTI = len(tiles)

        def load(i):
            if i >= NTI:
                return
            t0, s0_ = tiles[i]
            S.dma("sp", "ygld%d" % (i % 3), ygf[i % 3][:], c.yg_d[:, :, t0:t0 + 512].rearrange("q p t -> p q t"),
                  reads=[("yg_d", q, s0_) for q in range(4)])

        wst = [sb("wupst%d" % i, [128, 2 * DFF], BF16) for i in range(2)]

        def st0(i):
            if i < 8:
                S.dma("pool", "wupst%d" % (i % 2), wst[i % 2][:], c.w_up[i * 128:(i + 1) * 128, :])
            for kt in range(4):
                S.copy("act" if kt % 2 else "dve", ygb[i % 2][:, kt, :], ygf[i % 3][:, kt, :], wk=["ygb%d_%d" % (i % 2, kt)])

        def st1(i):
            if i < 8:
                S.dma("pool", "wupwr%d" % (i % 2), c.wup_d[:, :, i, :].rearrange("m p c -> p m c"),
                      wst[i % 2][:].rearrange("p (m c) -> p m c", c=128), writes=["wup_d"])
            for m in range(4):
                pg = ps_g[m % 3]
                for kt in range(4):
                    S.mm(pg[:], wg[:, kt, m * 128:(m + 1) * 128], ygb[i % 2][:, kt, :], start=(kt == 0), stop=(kt == 3),
                         rk=["wg", "ygb%d_%d" % (i % 2, kt)])
                S.act(sg[m][:], pg[:], AF.Sigmoid, bias=bg[:, m:m + 1])
            for m in range(4):
                S.tt("dve", y2[i % 2][:, m, :], ygf[i % 3][:, m, :], sg[m][:], ALU.mult, wk=["y2_%d_%d" % (i % 2, m)])
            for m in range(4):
                S.act(sq[m][:], y2[i % 2][:, m, :], AF.Square, rk=["y2_%d_%d" % (i % 2, m)])
                S.mm(ps_s[i % 2][:], c.onesb[:], sq[m][:], start=(m == 0), stop=(m == 3), inc=True)
            load(i + 3)

        def st2(i):
            t0 = tiles[i][0]
            rsqrt_act(S, rst[i % 2][:], ps_s[i % 2][:], 1.0 / DC, var[i % 2][:], c.epsc[:])
            for m in range(4):
                S.stt(y2n[i % 2][:, m, :], y2[i % 2][:, m, :], gns[:, m:m + 1], rst[i % 2][:], ALU.mult, ALU.mult,
                      rk=["y2_%d_%d" % (i % 2, m), "gns", _nm(rst[i % 2][:])])
            S.dma("pool", "y2nst%d" % (i % 2), c.ycat_d[4:8, :, t0:t0 + 512].rearrange("f p t -> p f t"), y2n[i % 2][:],
                  writes=[("ycat_d", 1, t0)])

        for i in range(3):
            load(i)
        pipeline(NTI, [st0, st1, st2])
        for kt in range(NTI, 8):
            S.dma("pool", "wupst%d" % (kt % 2), wst[kt % 2][:], c.w_up[kt * 128:(kt + 1) * 128, :])
            S.dma("pool", "wupwr%d" % (kt % 2), c.wup_d[:, :, kt, :].rearrange("m p c -> p m c"),
                  wst[kt % 2][:].rearrange("p (m c) -> p m c", c=128), writes=["wup_d"])
        S.barrier()


def phase_C1(c):
    nc, S = c.nc, c.S
    with contextlib.ExitStack() as es:
        E = es.enter_context
        sb = lambda n, sh, dt=F32: E(nc.sbuf_tensor(n, sh, dt))
        wo = c.wo
        gpost, gffn = sb("gpost", [128, D]), sb("gffn", [128, D])
        yc = [sb("ycC%d" % i, [128, 8, 512], BF16) for i in range(2)]
        xt = [sb("xtC%d" % i, [128, 4, D]) for i in range(2)]
        junk = sb("junkC", [128, D], BF16)
        st = [sb("stC%d" % i, [128, 8]) for i in range(3)]
        tmp = [sb("tmpC%d" % i, [128, D]) for i in range(2)]
        x1 = [sb("x1C%d" % i, [128, D]) for i in range(3)]
        h2 = [sb("h2C%d" % i, [128, D], BF16) for i in range(3)]
        h2T = [sb("h2TC%d" % i, [128, 8, 512], BF16) for i in range(2)]
        ps_o = [E(nc.psum_tensor("psC_o%d" % i, [128, D], F32)) for i in range(3)]
        ps_t = [E(nc.psum_tensor("psC_t%d" % i, [128, D], BF16)) for i in range(2)]
        S.dma("sp", "cC0", gpost[:], c.post_mix_g.to_broadcast([128, D]))
        S.dma("sp", "cC1", gffn[:], c.pre_ffn_g.to_broadcast([128, D]))
        tiles = []
        for (s0_, L) in c.seqs:
            for i in range(L // 512):
                tiles.append(s0_ + i * 512)
        NTI = len(tiles)

        def load(ti):
            if ti >= NTI:
                return
            t0 = tiles[ti]
            S.dma("sp", "ycld%d" % (ti % 2), yc[ti % 2][:], c.ycat_d[:, :, t0:t0 + 512].rearrange("f p t -> p f t"),
                  reads=[("ycat_d", 0, t0), ("ycat_d", 1, t0)])
            S.dma("sp", "xldC%d" % (ti % 2), xt[ti % 2][:], c.x[t0:t0 + 512, :].rearrange("(g p) d -> p g d", p=128))

        def s0(i):
            ti, g = divmod(i, 4)
            s = ti % 2
            if g == 1:
                load(ti + 1)
            po = ps_o[i % 3]
            for hh in range(2):
                for kt in range(8):
                    S.mm(po[:, hh * 512:(hh + 1) * 512], yc[s][:, kt, g * 128:(g + 1) * 128], wo[:, kt, hh * 512:(hh + 1) * 512],
                         start=(kt == 0), stop=(kt == 7), inc=(kt == 7 and hh == 1))

        def s1(i):
            ti, g = divmod(i, 4)
            s, b3, t0 = ti % 2, i % 3, tiles[ti]
            po, sv = ps_o[b3], st[b3]
            S.act(junk[:], po[:], AF.Square, accum_out=sv[:, 0:1], wk=["junkC", "stC%d_a" % b3])
            rsqrt_act(S, sv[:, 2:3], sv[:, 0:1], 1.0 / D, sv[:, 1:2], c.epsc[:], rk=["stC%d_a" % b3], tk=["stC%d_b" % b3], wk=["stC%d_c" % b3])
            S.stt(tmp[i % 2][:], po[:], sv[:, 2:3], gpost[:], ALU.mult, ALU.mult, rk=[_nm(po[:]), "stC%d_c" % b3, "gpost"])
            S.tt("dve", x1[b3][:], tmp[i % 2][:], xt[s][:, g, :], ALU.add)
            S.dma("pool", "x1st%d" % b3, c.x1_d[t0 + g * 128:t0 + (g + 1) * 128, :], x1[b3][:], writes=[("x1_d", t0 + g * 128)])

        def s2(i):
            b3 = i % 3
            sv = st[b3]
            S.act(junk[:], x1[b3][:], AF.Square, accum_out=sv[:, 3:4], wk=["junkC", "stC%d_d" % b3])
            rsqrt_act(S, sv[:, 5:6], sv[:, 3:4], 1.0 / D, sv[:, 4:5], c.epsc[:], rk=["stC%d_d" % b3], tk=["stC%d_e" % b3], wk=["stC%d_f" % b3])
            S.stt(h2[b3][:], x1[b3][:], sv[:, 5:6], gffn[:], ALU.mult, ALU.mult, rk=[_nm(x1[b3][:]), "stC%d_f" % b3, "gffn"])

        def s3(i):
            ti, g = divmod(i, 4)
            s, b3, t0 = ti % 2, i % 3, tiles[ti]
            pt = ps_t[i % 2]
            for kt in range(8):
                S.tr(pt[:, kt * 128:(kt + 1) * 128], h2[b3][:, kt * 128:(kt + 1) * 128], c.identb[:], inc=(kt == 7))
            S.copy("act", h2T[s][:, :, g * 128:(g + 1) * 128], pt[:].rearrange("p (k t) -> p k t", k=8))
            if g == 3:
                S.dma("pool", "h2Tst%d" % s, c.h2T_d[:, :, t0:t0 + 512].rearrange("f p t -> p f t"), h2T[s][:],
                      writes=[("h2T_d", t0)])

        load(0)
        pipeline(NTI * 4, [s0, s1, s2, s3])
        S.barrier()


def phase_C2(c):
    nc, S = c.nc, c.S
    BLK = 1024
    with contextlib.ExitStack() as es:
        E = es.enter_context
        sb = lambda n, sh, dt=F32: E(nc.sbuf_tensor(n, sh, dt))
        wdn = c.wdn
        fw, fb = sb("fw", [128, NM_UP, 3]), sb("fb", [128, NM_UP])
        gpo = sb("gpo", [128, D])
        wu = [sb("wu%d" % i, [128, 8, 128], BF16) for i in range(4)]
        hT = [sb("hTF%d" % i, [128, 8, BLK + 2], BF16) for i in range(2)]
        actT = sb("actT", [128, 22, BLK], BF16)
        ag = [sb("agF%d" % i, [128, BLK]) for i in range(2)]
        av = [sb("avF%d" % i, [128, BLK]) for i in range(2)]
        sgl = [sb("sgF%d" % i, [128, BLK]) for i in range(2)]
        hv = [sb("hvF%d" % i, [128, 2]) for i in range(2)]
        x1 = [sb("x1F%d" % i, [128, D]) for i in range(2)]
        junk = sb("junkF", [128, D], BF16)
        st = [sb("stF%d" % i, [128, 4]) for i in range(2)]
        tmp = [sb("tmpF%d" % i, [128, D]) for i in range(2)]
        yo = [sb("yoF%d" % i, [128, D]) for i in range(2)]
        PS = [E(nc.psum_tensor("psF%d" % i, [128, 1024], F32)) for i in range(3)]
        PHs = [E(nc.psum_tensor("psFh%d" % i, [128, 512], F32)) for i in range(2)]
        for j in range(3):
            S.dma("sp", "cF0", fw[:, :, j], c.ffn_conv_w[j, :].rearrange("(m p) -> p m", p=128), allow_slow_non_contiguous=True)
        S.dma("sp", "cF1", fb[:], c.ffn_conv_b[0, :].rearrange("(m p) -> p m", p=128), allow_slow_non_contiguous=True)
        S.dma("sp", "cF2", gpo[:], c.post_ffn_g.to_broadcast([128, D]))
        blocks = []
        for (s0, L) in c.seqs:
            nb = L // BLK
            for i in range(nb):
                blocks.append((s0 + i * BLK, i == 0, i == nb - 1))
        nwu = [0]

        def load_wu(m):
            w = wu[nwu[0] % 4]
            S.dma("sp", "wuld%d" % (nwu[0] % 4), w[:], c.wup_d[m, :, :, :], reads=["wup_d"])
            nwu[0] += 1
            return w

        def load_h(bi):
            t0, first, last = blocks[bi]
            h = hT[bi % 2]
            lo = 1 if first else 0
            hi = BLK + 1 if last else BLK + 2
            S.dma("sp", "hld%d" % (bi % 2), h[:, :, lo:hi], c.h2T_d[:, :, t0 - 1 + lo:t0 - 1 + hi].rearrange("f p t -> p f t"),
                  reads=[("h2T_d", t0 + 512 * k) for k in range(-1 if not first else 0, 3 if not last else 2)])
            if first:
                S.memset("pool", h[:, :, 0:1], 0.0)
            if last:
                S.memset("pool", h[:, :, BLK + 1:BLK + 2], 0.0)

        load_h(0)
        nps = 0
        nh = 0
        nd = 0
        for bi, (t0, first, last) in enumerate(blocks):
            h = hT[bi % 2]
            if bi + 1 < len(blocks):
                load_h(bi + 1)
            interior = (not first) or (not last)
            for mp in range(22):
                res = []
                for which, m in enumerate((mp, 22 + mp)):
                    w = load_wu(m)
                    ps = PS[nps % 3]
                    nps += 1
                    for hh in range(2):
                        for kt in range(8):
                            S.mm(ps[:, hh * 512:(hh + 1) * 512], w[:, kt, :], h[:, kt, 1 + hh * 512:1 + (hh + 1) * 512],
                                 start=(kt == 0), stop=(kt == 7), inc=(kt == 7 and hh == 1))
                    PH = PHs[nh % 2]
                    nh += 1
                    for kt in range(8):
                        S.mm(PH[:, 0:2], w[:, kt, :], h[:, kt, 0:BLK + 2:BLK + 1], start=(kt == 0), stop=(kt == 7))
                    a = (ag if which == 0 else av)[mp % 2]
                    S.act(a[:], ps[:], AF.Identity, scale=fw[:, m, 1:2], bias=fb[:, m:m + 1])
                    S.stt(a[:, 1:BLK], ps[:, 0:BLK - 1], fw[:, m, 0:1], a[:, 1:BLK], ALU.mult, ALU.add)
                    S.stt(a[:, 0:BLK - 1], ps[:, 1:BLK], fw[:, m, 2:3], a[:, 0:BLK - 1], ALU.mult, ALU.add)
                    hvv = hv[which]
                    S.tt("dve", hvv[:], PH[:, 0:2], fw[:, m, 0:3:2], ALU.mult)
                    S.tt("dve", a[:, 0:BLK:BLK - 1], a[:, 0:BLK:BLK - 1], hvv[:], ALU.add)
                    res.append(a)
                g = sgl[mp % 2]
                S.act(g[:], res[0][:], AF.Silu)
                S.tt("pool", actT[:, mp, :], g[:], res[1][:], ALU.mult, wk=["actT_%d" % mp])
            for tg in range(BLK // 128):
                b = nd % 2
                nd += 1
                PD = PS[nps % 3]
                nps += 1
                tt0 = t0 + tg * 128
                S.dma("sp", "x1ld%d" % b, x1[b][:], c.x1_d[tt0:tt0 + 128, :], reads=[("x1_d", tt0)])
                for hh in range(2):
                    for kt in range(22):
                        S.mm(PD[:, hh * 512:(hh + 1) * 512], actT[:, kt, tg * 128:(tg + 1) * 128], wdn[:, kt, hh * 512:(hh + 1) * 512],
                             start=(kt == 0), stop=(kt == 21), inc=(kt == 21 and hh == 1), rk=["actT_%d" % kt, "wdn"])
                sv = st[b]
                S.act(junk[:], PD[:], AF.Square, accum_out=sv[:, 0:1], wk=["junkF", "stF%d_a" % b])
                rsqrt_act(S, sv[:, 2:3], sv[:, 0:1], 1.0 / D, sv[:, 1:2], c.epsc[:], rk=["stF%d_a" % b], tk=["stF%d_b" % b], wk=["stF%d_c" % b])
                S.stt(tmp[b][:], PD[:], sv[:, 2:3], gpo[:], ALU.mult, ALU.mult, rk=[_nm(PD[:]), "stF%d_c" % b, "gpo"])
                S.tt("pool", yo[b][:], tmp[b][:], x1[b][:], ALU.add)
                S.dma("pool", "yst%d" % b, c.y[tt0:tt0 + 128, :], yo[b][:], writes=[("y", tt0)])
        S.barrier()


SEQ_LENS = (2048, 2048, 4096, 4096)
_CONSTS = {
    "ident": np.eye(128, dtype=np.float32),
    "mask32": (np.arange(128)[:, None] % 32 == np.arange(32)[None, :]).astype(np.float32),
}
_WKEYS = ["pre_mix_g", "w_in", "conv_w", "lam_re", "lam_im", "log_step", "b_re", "b_im", "c_re", "c_im", "d_skip",
          "w_glu", "b_glu", "gn_conv", "gn_ssm", "w_out", "post_mix_g", "pre_ffn_g", "w_up", "ffn_conv_w",
          "ffn_conv_b", "w_down", "post_ffn_g"]


def _weights_map(inputs):
    m = {}
    for k in _WKEYS:
        a = np.asarray(inputs[k], dtype=np.float32)
        a = a[0]
        if a.ndim == 1:
            a = a[None, :]
        m[k] = np.ascontiguousarray(a)
    m.update(_CONSTS)
    return m


def kernel(**inputs):
    xp = np.asarray(inputs["x_prompt"], dtype=np.float32)
    xs = np.asarray(inputs["x_sample"], dtype=np.float32)
    n = 8
    nc = build(SEQ_LENS)
    wm = _weights_map(inputs)
    in_maps = []
    for i in range(n):
        xc = np.concatenate([xp[2 * i].reshape(-1, D), xp[2 * i + 1].reshape(-1, D),
                             xs[2 * i].reshape(-1, D), xs[2 * i + 1].reshape(-1, D)], axis=0)
        d = dict(wm)
        d["x"] = np.ascontiguousarray(xc)
        in_maps.append(d)
    res = run_bass_kernel_spmd(nc, in_maps, core_ids=list(range(n)))
    yp = np.empty_like(xp)
    ys = np.empty_like(xs)
    for i in range(n):
        y = res.results[i]["y"]
        yp[2 * i] = y[0:2048]
        yp[2 * i + 1] = y[2048:4096]
        ys[2 * i] = y[4096:8192]
        ys[2 * i + 1] = y[8192:12288]
    return (yp, ys)
```

```python
import contextlib
import math
import numpy as np
import concourse.bass as bass
import concourse.mybir as mybir
from concourse.bass_utils import run_bass_kernel_spmd

F32, BF16 = mybir.dt.float32, mybir.dt.bfloat16
AF, ALU = mybir.ActivationFunctionType, mybir.AluOpType

D = 1024
DC = 512
DIN = 2048
DFF = 2816
NM_UP = 44
EPS = 1e-6
MAGIC = 12582912.0
TWO_PI = 2.0 * math.pi


def _nm(ap):
    return ap.name


class Sched:
    def __init__(self, nc, es):
        self.nc, self.es = nc, es
        self.engs = {"pe": nc.tensor, "act": nc.scalar, "dve": nc.vector, "pool": nc.gpsimd, "sp": nc.sync}
        self.sems, self.cnt = {}, {}
        self.seen = {e: {} for e in self.engs}
        self.lastw, self.readers = {}, {}

    def _sem(self, key):
        if key not in self.sems:
            self.sems[key] = self.es.enter_context(self.nc.semaphore("s%d" % len(self.sems)))
            self.cnt[key] = 0
        return self.sems[key]

    def _wait(self, eng, deps):
        for (k, v) in deps:
            if eng == "pe" and k == "pe":
                continue
            if self.seen[eng].get(k, 0) < v:
                self.engs[eng].wait_ge(self._sem(k), v)
                self.seen[eng][k] = v

    def _deps(self, reads, writes):
        deps = []
        for b in reads:
            if b in self.lastw:
                deps.append(self.lastw[b])
        for b in writes:
            if b in self.lastw:
                deps.append(self.lastw[b])
            deps.extend(self.readers.get(b, {}).items())
        return deps

    def _record(self, ev, reads, writes):
        for b in writes:
            self.lastw[b] = ev
            self.readers[b] = {}
        for b in reads:
            r = self.readers.setdefault(b, {})
            if r.get(ev[0], 0) < ev[1]:
                r[ev[0]] = ev[1]

    def op(self, eng, fn, reads=(), writes=(), inc=True):
        self._wait(eng, self._deps(reads, writes))
        ins = fn(self.engs[eng])
        self._sem(eng)
        if inc:
            self.cnt[eng] += 1
            ins.then_inc(self.sems[eng], 1)
            ev = (eng, self.cnt[eng])
        else:
            ev = (eng, self.cnt[eng] + 1)
        self._record(ev, reads, writes)
        return ev

    def dma(self, q, semkey, out, in_, reads=None, writes=None, **kw):
        reads = [_nm(in_)] if reads is None else reads
        writes = [_nm(out)] if writes is None else writes
        self._wait(q, self._deps(reads, writes))
        s = self._sem(semkey)
        self.cnt[semkey] += 16
        self.engs[q].dma_start(out=out, in_=in_, **kw).then_inc(s, 16)
        ev = (semkey, self.cnt[semkey])
        self._record(ev, reads, writes)
        return ev

    def wait_all(self, eng):
        mx = {}
        for (k, v) in self.lastw.values():
            mx[k] = max(mx.get(k, 0), v)
        for r in self.readers.values():
            for k, v in r.items():
                mx[k] = max(mx.get(k, 0), v)
        self._wait(eng, [(k, min(v, self.cnt[k])) for k, v in mx.items() if k != eng or eng != "pe"])

    def barrier(self):
        for e in self.engs:
            self.wait_all(e)
        self.lastw, self.readers = {}, {}

    def tt(self, eng, out, in0, in1, op, rk=None, wk=None):
        return self.op(eng, lambda e: e.tensor_tensor(out=out, in0=in0, in1=in1, op=op),
                       reads=rk if rk is not None else [_nm(in0), _nm(in1)], writes=wk if wk is not None else [_nm(out)])

    def ts(self, eng, out, in0, s1, s2, op0, op1=None, rk=None, wk=None):
        r = [_nm(in0)] + [_nm(s) for s in (s1, s2) if hasattr(s, "name")]
        if op1 is None:
            f = lambda e: e.tensor_scalar(out=out, in0=in0, scalar1=s1, scalar2=None, op0=op0)
        else:
            f = lambda e: e.tensor_scalar(out=out, in0=in0, scalar1=s1, scalar2=s2, op0=op0, op1=op1)
        return self.op(eng, f, reads=rk if rk is not None else r, writes=wk if wk is not None else [_nm(out)])

    def stt(self, out, in0, scalar, in1, op0, op1, rk=None, wk=None):
        r = [_nm(in0), _nm(in1)] + ([_nm(scalar)] if hasattr(scalar, "name") else [])
        return self.op("dve", lambda e: e.scalar_tensor_tensor(out=out, in0=in0, scalar=scalar, in1=in1, op0=op0, op1=op1),
                       reads=rk if rk is not None else r, writes=wk if wk is not None else [_nm(out)])

    def act(self, out, in_, func, bias=None, scale=None, accum_out=None, rk=None, wk=None):
        kw = {}
        r = [_nm(in_)]
        w = [_nm(out)]
        if bias is not None:
            kw["bias"] = bias
            if hasattr(bias, "name"):
                r.append(_nm(bias))
        if scale is not None:
            kw["scale"] = scale
            if hasattr(scale, "name"):
                r.append(_nm(scale))
        if accum_out is not None:
            kw["accum_out"] = accum_out
            w.append(_nm(accum_out))
        return self.op("act", lambda e: e.activation(out=out, in_=in_, func=func, **kw),
                       reads=rk if rk is not None else r, writes=wk if wk is not None else w)

    def copy(self, eng, out, in_, rk=None, wk=None):
        if eng == "act":
            return self.act(out, in_, AF.Copy, rk=rk, wk=wk)
        return self.op(eng, lambda e: e.tensor_copy(out=out, in_=in_),
                       reads=rk if rk is not None else [_nm(in_)], writes=wk if wk is not None else [_nm(out)])

    def memset(self, eng, out, val, wk=None):
        return self.op(eng, lambda e: e.memset(out, val), reads=[], writes=wk if wk is not None else [_nm(out)])

    def mm(self, out, lhsT, rhs, start, stop, inc=None, tp=None, rk=None, wk=None):
        inc = stop if inc is None else inc
        kw = {} if tp is None else {"tile_position": tp}
        return self.op("pe", lambda e: e.matmul(out, lhsT=lhsT, rhs=rhs, start=start, stop=stop, **kw),
                       reads=rk if rk is not None else [_nm(lhsT), _nm(rhs)],
                       writes=wk if wk is not None else [_nm(out)], inc=inc)

    def tr(self, out, in_, ident, inc=True, tp=None, rk=None, wk=None):
        kw = {} if tp is None else {"tile_position": tp}
        return self.op("pe", lambda e: e.transpose(out, in_, ident, **kw),
                       reads=rk if rk is not None else [_nm(in_), _nm(ident)],
                       writes=wk if wk is not None else [_nm(out)], inc=inc)

    def scan(self, out, d0, d1, init=0.0):
        return self.op("dve", lambda e: e.tensor_tensor_scan(out=out, data0=d0, data1=d1, initial=init,
                                                             op0=ALU.mult, op1=ALU.add),
                       reads=[_nm(d0), _nm(d1)], writes=[_nm(out)])


class Ctx:
    pass


def pipeline(n, stages):
    for t in range(n + len(stages) - 1):
        for k, st in enumerate(stages):
            i = t - k
            if 0 <= i < n:
                st(i)


def rsqrt_act(S, out, in_, scale, tmp, eps_ap, rk=None, wk=None, tk=None):
    S.act(tmp, in_, AF.Ln, scale=scale, bias=eps_ap, rk=rk, wk=tk)
    S.act(out, tmp, AF.Exp, scale=-0.5, rk=tk, wk=wk)


def build(seq_lens, debug=False):
    NT = sum(seq_lens)
    assert all(L % 1024 == 0 for L in seq_lens)
    nc = bass.Bass("TRN2", target_bir_lowering=False)
    c = Ctx()
    c.nc, c.NT, c.seq_lens, c.debug = nc, NT, seq_lens, debug
    c.seqs = []
    t = 0
    for L in seq_lens:
        c.seqs.append((t, L))
        t += L

    def din(name, shape):
        return nc.dram_tensor(name, list(shape), F32, kind="ExternalInput").ap()

    c.x = din("x", [NT, D])
    c.pre_mix_g = din("pre_mix_g", [1, D])
    c.w_in = din("w_in", [D, DIN])
    c.conv_w = din("conv_w", [3, DC])
    c.lam_re = din("lam_re", [2, 32, 64])
    c.lam_im = din("lam_im", [2, 32, 64])
    c.log_step = din("log_step", [2, 32])
    c.b_re = din("b_re", [2, 32, 64, 16])
    c.b_im = din("b_im", [2, 32, 64, 16])
    c.c_re = din("c_re", [2, 32, 16, 64])
    c.c_im = din("c_im", [2, 32, 16, 64])
    c.d_skip = din("d_skip", [1, DC])
    c.w_glu = din("w_glu", [DC, DC])
    c.b_glu = din("b_glu", [1, DC])
    c.gn_conv = din("gn_conv", [1, DC])
    c.gn_ssm = din("gn_ssm", [1, DC])
    c.w_out = din("w_out", [D, D])
    c.post_mix_g = din("post_mix_g", [1, D])
    c.pre_ffn_g = din("pre_ffn_g", [1, D])
    c.w_up = din("w_up", [D, 2 * DFF])
    c.ffn_conv_w = din("ffn_conv_w", [3, 2 * DFF])
    c.ffn_conv_b = din("ffn_conv_b", [1, 2 * DFF])
    c.w_down = din("w_down", [DFF, D])
    c.post_ffn_g = din("post_ffn_g", [1, D])
    c.ident = din("ident", [128, 128])
    c.mask32 = din("mask32", [128, 32])
    c.y = nc.dram_tensor("y", [NT, D], F32, kind="ExternalOutput").ap()

    sk = "ExternalOutput" if debug else "Internal"
    c.zuT_d = nc.dram_tensor("zuT_d", [4, 128, NT], BF16, kind=sk).ap()
    c.ycat_d = nc.dram_tensor("ycat_d", [8, 128, NT], BF16, kind=sk).ap()
    c.yg_d = nc.dram_tensor("yg_d", [4, 128, NT], F32, kind=sk).ap()
    c.x1_d = nc.dram_tensor("x1_d", [NT, D], F32, kind=sk).ap()
    c.h2T_d = nc.dram_tensor("h2T_d", [8, 128, NT], BF16, kind=sk).ap()
    c.wup_d = nc.dram_tensor("wup_d", [NM_UP, 128, 8, 128], BF16, kind="Internal").ap()
    c.WX_d = nc.dram_tensor("WX_d", [4, 128, 4096], BF16, kind="Internal").ap()
    c.WY_d = nc.dram_tensor("WY_d", [32, 128, 576], BF16, kind="Internal").ap()
    c.IN_d = nc.dram_tensor("IN_d", [4, 128, 2048], BF16, kind="Internal").ap()

    with contextlib.ExitStack() as es:
        S = Sched(nc, es)
        c.S = S
        c.es = es
        E = es.enter_context
        c.identb = E(nc.sbuf_tensor("identb", [128, 128], BF16))
        c.identf = E(nc.sbuf_tensor("identf", [128, 128], F32))
        c.onesb = E(nc.sbuf_tensor("onesb", [128, 128], BF16))
        S.dma("pool", "c0", c.identb[:], c.ident[:, :])
        S.dma("sp", "c1", c.identf[:], c.ident[:, :])
        S.memset("dve", c.onesb[:], 1.0)
        c.epsc = E(nc.sbuf_tensor("epsc", [128, 1], F32))
        S.memset("dve", c.epsc[:], EPS)
        S.barrier()
        gsb = lambda n, sh: E(nc.sbuf_tensor(n, sh, F32))
        c.lamre, c.lamim, c.lst = gsb("lamre", [128, 32]), gsb("lamim", [128, 32]), gsb("lst", [128, 32])
        c.dsk, c.m32 = gsb("dsk", [128, 4]), gsb("m32", [128, 32])
        S.dma("act", "p0", c.lamre[:].rearrange("p (d r) -> p d r", d=2), c.lam_re.rearrange("d (r q) p -> (q p) d r", q=2),
              allow_slow_non_contiguous=True)
        S.dma("act", "p1", c.lamim[:].rearrange("p (d r) -> p d r", d=2), c.lam_im.rearrange("d (r q) p -> (q p) d r", q=2),
              allow_slow_non_contiguous=True)
        for q in range(2):
            src = c.log_step.rearrange("d (r q) -> q d r", q=2)[q:q + 1]
            S.dma("act", "p2", c.lst[64 * q:64 * q + 64, :].rearrange("p (d r) -> p d r", d=2), src.to_broadcast([64, 2, 16]),
                  allow_slow_non_contiguous=True)
        S.dma("act", "p7", c.dsk[:], c.d_skip[0, :].rearrange("(f p) -> p f", p=128), allow_slow_non_contiguous=True)
        S.dma("act", "p8", c.m32[:], c.mask32[:, :])
        stage = c.stage = ("ALL" if not debug else debug)
        phase_A(c)
        if stage in ("ALL", "B1", "B2", "C1", "C2"):
            phase_B1(c)
        if stage in ("ALL", "B2", "C1", "C2"):
            c.wo = E(nc.sbuf_tensor("wo", [128, 8, D], BF16))
            c.wdn = E(nc.sbuf_tensor("wdn", [128, 22, D], BF16))
            for kt in range(8):
                S.dma("pool", "wold", c.wo[:, kt, :], c.w_out[kt * 128:(kt + 1) * 128, :])
            for kt in range(22):
                S.dma("pool", "wdnld", c.wdn[:, kt, :], c.w_down[kt * 128:(kt + 1) * 128, :])
            phase_B2(c)
        if stage in ("ALL", "C1", "C2"):
            phase_C1(c)
        if stage in ("ALL", "C2"):
            phase_C2(c)
        for e in ("sp", "pool", "act", "dve", "pe"):
            S.wait_all(e)
    return nc


def phase_prep_wup(c):
    nc, S = c.nc, c.S
    with contextlib.ExitStack() as es:
        E = es.enter_context
        st = [E(nc.sbuf_tensor("wupst%d" % i, [128, 2 * DFF], BF16)) for i in range(2)]
        for kt in range(8):
            s = st[kt % 2]
            S.dma("pool", "wupst%d" % (kt % 2), s[:], c.w_up[kt * 128:(kt + 1) * 128, :])
            S.dma("sp", "wupwr%d" % (kt % 2), c.wup_d[:, :, kt, :].rearrange("m p c -> p m c"),
                  s[:].rearrange("p (m c) -> p m c", c=128), writes=["wup_d"])
        S.barrier()


def phase_A(c):
    nc, S = c.nc, c.S
    NT = c.NT
    with contextlib.ExitStack() as es:
        E = es.enter_context
        sb = lambda n, sh, dt: E(nc.sbuf_tensor(n, sh, dt))
        w_in_sb = sb("w_in_sb", [128, 8, DIN], BF16)
        gk = sb("gkA", [128, 8], F32)
        cw = sb("cw", [128, 4, 3], F32)
        gnc = sb("gnc", [128, 4], F32)
        xt = [sb("xt%d" % i, [128, 4, D], F32) for i in range(2)]
        junk = sb("junkA", [128, D], BF16)
        junk2 = sb("junkA2", [128, D], BF16)
        ss = [sb("ssA%d" % i, [128, 4], F32) for i in range(2)]
        var4 = [sb("var4A%d" % i, [128, 4], F32) for i in range(2)]
        rstd = [sb("rstdA%d" % i, [128, 4], F32) for i in range(2)]
        xn = [sb("xn%d" % i, [128, D], BF16) for i in range(4)]
        hT = [sb("hT%d" % i, [128, 8, 512], BF16) for i in range(2)]
        pbuf = [sb("pbuf%d" % i, [128, 4, 514], F32) for i in range(2)]
        zbb = [sb("zbb%d" % i, [128, 4, 512], F32) for i in range(2)]
        zcs = [sb("zcs%d" % i, [128, 512], F32) for i in range(4)]
        zus = [sb("zus%d" % i, [128, 4, 512], BF16) for i in range(2)]
        acc = [sb("accA%d" % i, [128, 512], F32) for i in range(4)]
        yc = sb("ycA", [128, 4, 512], F32)
        sq = [sb("sqA%d" % i, [128, 512], BF16) for i in range(4)]
        var = sb("varA", [128, 512], F32)
        rst = sb("rstA", [128, 512], F32)
        ycn = [sb("ycn%d" % i, [128, 4, 512], BF16) for i in range(2)]
        ps_t = [E(nc.psum_tensor("psA_t%d" % i, [128, D], BF16)) for i in range(2)]
        ps_z = [E(nc.psum_tensor("psA_z%d" % i, [128, 512], F32)) for i in range(5)]
        ps_s = E(nc.psum_tensor("psA_s", [128, 512], F32))

        for cb in range(4):
            S.dma("pool", "winld%d" % cb, w_in_sb[:, :, cb * 512:(cb + 1) * 512],
                  c.w_in[:, cb * 512:(cb + 1) * 512].rearrange("(k p) n -> p k n", p=128), writes=["w_in_sb_%d" % cb])
        S.dma("sp", "cA0", gk[:], c.pre_mix_g[0, :].rearrange("(k p) -> p k", p=128), allow_slow_non_contiguous=True)
        for cb in range(4):
            for kt in range(8):
                S.ts("pool", w_in_sb[:, kt, cb * 512:(cb + 1) * 512], w_in_sb[:, kt, cb * 512:(cb + 1) * 512], gk[:, kt:kt + 1], None,
                     ALU.mult, rk=["w_in_sb_%d" % cb, "gkA"], wk=["w_in_sb_%d" % cb])
        for j in range(3):
            S.dma("sp", "cA1", cw[:, :, j], c.conv_w[j, :].rearrange("(f p) -> p f", p=128), allow_slow_non_contiguous=True)
        S.dma("sp", "cA2", gnc[:], c.gn_conv[0, :].rearrange("(f p) -> p f", p=128), allow_slow_non_contiguous=True)

        tiles = []
        for (s0, L) in c.seqs:
            n = L // 512
            for i in range(n):
                tiles.append((s0 + i * 512, i == 0, i == n - 1))

        def load_x(i):
            t0 = tiles[i][0]
            S.dma("sp", "xld%d" % (i % 2), xt[i % 2][:], c.x[t0:t0 + 512, :].rearrange("(g p) d -> p g d", p=128))

        def fA1(i, g):
            s = i % 2
            if g % 2 == 0:
                S.act(junk[:], xt[s][:, g, :], AF.Square, accum_out=ss[s][:, g:g + 1], wk=["junkA", "ssA%d_%d" % (s, g)])
            else:
                S.op("dve", lambda e: e.scalar_tensor_tensor(out=junk2[:], in0=xt[s][:, g, :], scalar=1.0, in1=xt[s][:, g, :],
                                                               op0=ALU.mult, op1=ALU.mult, accum_out=ss[s][:, g:g + 1]),
                     reads=[_nm(xt[s][:])], writes=["junkA2", "ssA%d_%d" % (s, g)])
            rsqrt_act(S, rstd[s][:, g:g + 1], ss[s][:, g:g + 1], 1.0 / D, var4[s][:, g:g + 1], c.epsc[:],
                      rk=["ssA%d_%d" % (s, g)], tk=["var4A%d_%d" % (s, g)], wk=["rstdA%d_%d" % (s, g)])
            if g % 2 == 0:
                S.ts("dve", xn[g][:], xt[s][:, g, :], rstd[s][:, g:g + 1], None, ALU.mult,
                     rk=[_nm(xt[s][:]), "rstdA%d_%d" % (s, g)])
            else:
                S.act(xn[g][:], xt[s][:, g, :], AF.Identity, scale=rstd[s][:, g:g + 1],
                      rk=[_nm(xt[s][:]), "rstdA%d_%d" % (s, g)])

        def fA2(i, g):
            pt = ps_t[g % 2]
            for kt in range(8):
                S.tr(pt[:, kt * 128:(kt + 1) * 128], xn[g][:, kt * 128:(kt + 1) * 128], c.identb[:], inc=(kt == 7))

        def fA3(i, g):
            s = i % 2
            S.copy("act", hT[s][:, :, g * 128:(g + 1) * 128], ps_t[g % 2][:].rearrange("p (k t) -> p k t", k=8),
                   wk=["hT%d_%d" % (s, g)])

        def front_items(i):
            it = [(lambda g=g: fA1(i, g)) for g in range(4)]
            for g in range(4):
                it.append(lambda g=g: fA2(i, g))
                it.append(lambda g=g: fA3(i, g))
            return it

        def c1(i, f):
            S.act(acc[f][:], pbuf[i % 2][:, f, 1:513], AF.Identity, scale=cw[:, f, 1:2])

        def c2(i, f):
            s = i % 2
            S.stt(acc[f][:], pbuf[s][:, f, 0:512], cw[:, f, 0:1], acc[f][:], ALU.mult, ALU.add)
            S.stt(acc[f][:], pbuf[s][:, f, 2:514], cw[:, f, 2:3], acc[f][:], ALU.mult, ALU.add)
            S.tt("dve", yc[:, f, :], acc[f][:], zbb[s][:, f, :], ALU.mult, wk=["ycA_%d" % f])

        def c3(i, f):
            S.act(sq[f][:], yc[:, f, :], AF.Square, rk=["ycA_%d" % f])
            S.mm(ps_s[:], c.onesb[:], sq[f][:], start=(f == 0), stop=(f == 3), inc=True)

        def c4(i):
            t0 = tiles[i][0]
            rsqrt_act(S, rst[:], ps_s[:], 1.0 / DC, var[:], c.epsc[:])
            yn = ycn[i % 2]
            for ff in range(4):
                S.stt(yn[:, ff, :], yc[:, ff, :], gnc[:, ff:ff + 1], rst[:], ALU.mult, ALU.mult,
                      rk=["ycA_%d" % ff, "gnc", "rstA"])
            S.dma("pool", "ycnst%d" % (i % 2), c.ycat_d[0:4, :, t0:t0 + 512].rearrange("f p t -> p f t"), yn[:],
                  writes=[("ycat_d", 0, t0)])

        def conv_items(i):
            order = [("c1", 0), ("c1", 1), ("c2", 0), ("c1", 2), ("c2", 1), ("c3", 0), ("c1", 3), ("c2", 2), ("c3", 1),
                     ("c2", 3), ("c3", 2), ("c3", 3)]
            fn = {"c1": c1, "c2": c2, "c3": c3}
            it = [(lambda k=k, f=f: fn[k](i, f)) for (k, f) in order]
            it.append(lambda: c4(i))
            return it

        load_x(0)
        for w_ in front_items(0):
            w_()
        bg = []
        for i, (t0, first, last) in enumerate(tiles):
            s = i % 2
            if i + 1 < len(tiles):
                load_x(i + 1)
                bg.extend(front_items(i + 1))
            def evac(m, s=s, first=first, last=last, i=i):
                pz = ps_z[m % 5]
                f = m % 4
                if m < 4:
                    S.copy("dve", zbb[s][:, f, :], pz[:])
                elif m < 8:
                    S.copy("dve", zcs[f][:], pz[:])
                elif m < 12:
                    S.tt("dve", pbuf[s][:, f, 1:513], pz[:], zcs[f][:], ALU.mult)
                else:
                    S.copy("act" if m % 2 == 0 else "dve", zus[s][:, f, :].rearrange("p (j c) -> p j c", j=8),
                           pz[:].rearrange("p (c j) -> p j c", j=8))
                if m == 11:
                    if first:
                        S.memset("pool", pbuf[s][:, :, 0:1], 0.0)
                    else:
                        S.copy("pool", pbuf[s][:, :, 0:1], pbuf[1 - s][:, :, 512:513])
                        S.copy("pool", pbuf[1 - s][:, :, 513:514], pbuf[s][:, :, 1:2])
                        bg.extend(conv_items(i - 1))
                    if last:
                        S.memset("pool", pbuf[s][:, :, 513:514], 0.0)

            for m in range(16):
                pz = ps_z[m % 5]
                for kt in range(8):
                    S.mm(pz[:], w_in_sb[:, kt, m * 128:(m + 1) * 128], hT[s][:, kt, :], start=(kt == 0), stop=(kt == 7),
                         rk=["w_in_sb_%d" % (m // 4)] + ["hT%d_%d" % (s, g) for g in range(4)])
                if m >= 1:
                    evac(m - 1)
                npop = -(-len(bg) // (16 - m)) if m >= 12 else min(len(bg), 2)
                for _ in range(npop):
                    bg.pop(0)()
            evac(15)
            S.dma("pool", "zust%d" % s, c.zuT_d[:, :, t0:t0 + 512].rearrange("q p t -> p q t"), zus[s][:],
                  writes=[("zuT_d", t0)])
            while bg:
                bg.pop(0)()
            if last:
                bg.extend(conv_items(i))
        while bg:
            bg.pop(0)()
        S.barrier()


def cmul(S, eng, o_re, o_im, x_re, x_im, y_re, y_im, t1, t2, conj_x=False):
    S.tt(eng, t1, x_re, y_re, ALU.mult)
    S.tt(eng, t2, x_im, y_im, ALU.mult)
    S.tt(eng, o_re, t1, t2, ALU.add if conj_x else ALU.subtract)
    S.tt(eng, t1, x_re, y_im, ALU.mult)
    S.tt(eng, t2, x_im, y_re, ALU.mult)
    S.tt(eng, o_im, t1, t2, ALU.subtract if conj_x else ALU.add)


def phase_B1(c):
    nc, S = c.nc, c.S
    with contextlib.ExitStack() as es:
        E = es.enter_context
        sb = lambda n, sh, dt: E(nc.sbuf_tensor(n, sh, dt))
        U8r = sb("U8r", [128, 32], F32)
        U8i = sb("U8i", [128, 32], F32)
        R8 = sb("R8", [128, 32], F32)
        with contextlib.ExitStack() as es1:
            WX = es1.enter_context(nc.sbuf_tensor("WX", [128, 4, 2, 2, 8, 128], BF16))
            WY = es1.enter_context(nc.sbuf_tensor("WY", [128, 32, 2, 9, 32], BF16))
            INTRA = es1.enter_context(nc.sbuf_tensor("INTRA", [128, 4, 64, 32], BF16))
            s5_prep(c, WX, WY, INTRA, U8r, U8i, R8)
            S.dma("sp", "spl0", c.WX_d.rearrange("q p x -> p q x"), WX[:].rearrange("p q a b e n -> p q (a b e n)"), writes=["WX_d"])
            S.dma("sp", "spl1", c.WY_d.rearrange("u p x -> p u x"), WY[:].rearrange("p u a k n -> p u (a k n)"), writes=["WY_d"])
            S.dma("sp", "spl2", c.IN_d.rearrange("q p x -> p q x"), INTRA[:].rearrange("p q j n -> p q (j n)"), writes=["IN_d"])
            S.barrier()
        s5_main(c, U8r, U8i, R8)


def s5_prep(c, WX, WY, INTRA, U8r, U8i, R8):
    nc, S = c.nc, c.S
    with contextlib.ExitStack() as es:
        E = es.enter_context
        sb = lambda n, sh, dt=F32: E(nc.sbuf_tensor(n, sh, dt))
        lamre, lamim, lst, dsk, m32 = c.lamre, c.lamim, c.lst, c.dsk, c.m32
        Bre, Bim = sb("Bre", [128, 32, 16]), sb("Bim", [128, 32, 16])
        S.dma("sp", "p3", Bre[:].rearrange("p (d r) h -> p d r h", d=2), c.b_re.rearrange("d (r q) p h -> (q p) d r h", q=2))
        S.dma("sp", "p4", Bim[:].rearrange("p (d r) h -> p d r h", d=2), c.b_im.rearrange("d (r q) p h -> (q p) d r h", q=2))
        Cre, Cim = sb("Cre", [128, 32, 16]), sb("Cim", [128, 32, 16])
        with contextlib.ExitStack() as es2:
            Cnr = es2.enter_context(nc.sbuf_tensor("Cnr", [16, 64, 64], F32))
            Cni = es2.enter_context(nc.sbuf_tensor("Cni", [16, 64, 64], F32))
            psC = [es2.enter_context(nc.psum_tensor("psC%d" % i, [128, 32, 16], F32)) for i in range(2)]
            S.dma("sp", "p5", Cnr[:], c.c_re.rearrange("d g h p -> h (d g) p"))
            S.dma("sp", "p6", Cni[:], c.c_im.rearrange("d g h p -> h (d g) p"))
            for ri, (Cn, Cst) in enumerate(((Cnr, Cre), (Cni, Cim))):
                for u in range(32):
                    d, pr = divmod(u, 16)
                    a = d * 32 + 2 * pr
                    S.tr(psC[ri][:, u, :], Cn[:, a:a + 2, :].rearrange("h g p -> h (g p)"), c.identf[0:16, 0:16], inc=(u == 31))
                S.copy("dve", Cst[:], psC[ri][:])
            S.barrier()

        dt, lr, th = sb("dtp", [128, 32]), sb("lr", [128, 32]), sb("th", [128, 32])
        mag, kf, thr = sb("mag", [128, 32]), sb("kf", [128, 32]), sb("thr", [128, 32])
        sh, ch, t1, t2 = sb("sh", [128, 32]), sb("ch", [128, 32]), sb("t1p", [128, 32]), sb("t2p", [128, 32])
        cs, sn = sb("cs", [128, 32]), sb("sn", [128, 32])
        hpi = sb("hpi", [128, 1])
        S.memset("dve", hpi[:], math.pi / 2)
        S.act(dt[:], lst[:], AF.Exp)
        S.tt("dve", lr[:], lamre[:], dt[:], ALU.mult)
        S.tt("dve", th[:], lamim[:], dt[:], ALU.mult)
        S.act(mag[:], lr[:], AF.Exp)
        S.act(R8[:], lr[:], AF.Exp, scale=8.0)
        S.ts("dve", kf[:], th[:], 1.0 / TWO_PI, MAGIC, ALU.mult, ALU.add)
        S.ts("dve", kf[:], kf[:], -MAGIC, -TWO_PI, ALU.add, ALU.mult)
        S.tt("dve", thr[:], th[:], kf[:], ALU.add)
        S.act(sh[:], thr[:], AF.Sin, scale=0.5)
        S.act(ch[:], thr[:], AF.Sin, scale=0.5, bias=hpi[:])
        S.tt("dve", t1[:], ch[:], ch[:], ALU.mult)
        S.tt("dve", t2[:], sh[:], sh[:], ALU.mult)
        S.tt("dve", cs[:], t1[:], t2[:], ALU.subtract)
        S.tt("dve", t1[:], sh[:], ch[:], ALU.mult)
        S.ts("dve", sn[:], t1[:], 2.0, None, ALU.mult)
        UPr, UPi = sb("UPr", [128, 32, 9]), sb("UPi", [128, 32, 9])
        APr, APi = sb("APr", [128, 32, 9]), sb("APi", [128, 32, 9])
        T1, T2 = sb("T1p", [128, 32, 4]), sb("T2p", [128, 32, 4])
        S.memset("dve", UPr[:, :, 0:1], 1.0)
        S.memset("dve", UPi[:, :, 0:1], 0.0)
        S.copy("dve", UPr[:, :, 1], cs[:])
        S.copy("dve", UPi[:, :, 1], sn[:])
        cmul(S, "dve", UPr[:, :, 2], UPi[:, :, 2], UPr[:, :, 1], UPi[:, :, 1], UPr[:, :, 1], UPi[:, :, 1], T1[:, :, 0], T2[:, :, 0])
        bc = lambda ap, n: ap.to_broadcast([128, 32, n])
        cmul(S, "dve", UPr[:, :, 3:5], UPi[:, :, 3:5], UPr[:, :, 1:3], UPi[:, :, 1:3], bc(UPr[:, :, 2:3], 2), bc(UPi[:, :, 2:3], 2),
             T1[:, :, 0:2], T2[:, :, 0:2])
        cmul(S, "dve", UPr[:, :, 5:9], UPi[:, :, 5:9], UPr[:, :, 1:5], UPi[:, :, 1:5], bc(UPr[:, :, 4:5], 4), bc(UPi[:, :, 4:5], 4),
             T1[:], T2[:])
        S.copy("dve", U8r[:], UPr[:, :, 8])
        S.copy("dve", U8i[:], UPi[:, :, 8])
        MG = sb("MG", [128, 32, 9])
        S.memset("dve", MG[:, :, 0:1], 1.0)
        for k in range(1, 9):
            S.act(MG[:, :, k], lr[:], AF.Exp, scale=float(k))
        S.tt("dve", APr[:], UPr[:], MG[:], ALU.mult)
        S.tt("dve", APi[:], UPi[:], MG[:], ALU.mult)
        nr, den, inv = sb("nr", [128, 32]), sb("den", [128, 32]), sb("inv", [128, 32])
        qr, qi = sb("qr", [128, 32]), sb("qi", [128, 32])
        S.ts("dve", nr[:], APr[:, :, 1], -1.0, None, ALU.add)
        S.tt("dve", t1[:], lamre[:], lamre[:], ALU.mult)
        S.tt("dve", t2[:], lamim[:], lamim[:], ALU.mult)
        S.tt("dve", den[:], t1[:], t2[:], ALU.add)
        S.op("dve", lambda e: e.reciprocal(out=inv[:], in_=den[:]), reads=["den"], writes=["inv"])
        S.tt("dve", t1[:], nr[:], lamre[:], ALU.mult)
        S.tt("dve", t2[:], APi[:, :, 1], lamim[:], ALU.mult)
        S.tt("dve", t1[:], t1[:], t2[:], ALU.add)
        S.tt("dve", qr[:], t1[:], inv[:], ALU.mult)
        S.tt("dve", t1[:], APi[:, :, 1], lamre[:], ALU.mult)
        S.tt("dve", t2[:], nr[:], lamim[:], ALU.mult)
        S.tt("dve", t1[:], t1[:], t2[:], ALU.subtract)
        S.tt("dve", qi[:], t1[:], inv[:], ALU.mult)
        Bbr, Bbi = sb("Bbr", [128, 32, 16]), sb("Bbi", [128, 32, 16])
        V1, V2 = sb("V1", [128, 32, 16]), sb("V2", [128, 32, 16])
        b16 = lambda ap: ap.unsqueeze(2).to_broadcast([128, 32, 16])
        cmul(S, "dve", Bbr[:], Bbi[:], b16(qr[:]), b16(qi[:]), Bre[:], Bim[:], V1[:], V2[:])
        Gbd = sb("Gbd", [128, 32, 2, 8, 32], BF16)
        Bbd = sb("Bbd", [128, 32, 2, 32], BF16)
        S.memset("pool", Gbd[:], 0.0)
        S.memset("pool", Bbd[:], 0.0)
        S.memset("pool", WY[:], 0.0)
        W1, W2 = sb("W1", [128, 32, 9, 16]), sb("W2", [128, 32, 9, 16])

        def bk(ap, n):
            return ap.unsqueeze(3).to_broadcast([128, 32, n, 16])

        def bh(ap, n):
            return ap.unsqueeze(2).to_broadcast([128, 32, n, 16])

        def halves(dst_fn, src, n):
            for g2 in range(2):
                P = slice(64 * g2, 64 * g2 + 64)
                S.copy("act", dst_fn(P, slice(16 * g2, 16 * g2 + 16)), src[P])

        S.tt("dve", W2[:, :, 0:8, :], bk(APi[:, :, 0:8], 8), bh(Bbi[:], 8), ALU.mult)
        S.tt("dve", W1[:, :, 0:8, :], bk(APr[:, :, 0:8], 8), bh(Bbr[:], 8), ALU.mult)
        S.tt("dve", W1[:, :, 0:8, :], W1[:, :, 0:8, :], W2[:, :, 0:8, :], ALU.subtract)
        halves(lambda P, Cc: Gbd[P, :, 0, :, Cc], W1[:, :, 0:8, :], 8)
        S.tt("dve", W2[:, :, 0:8, :], bk(APi[:, :, 0:8], 8), bh(Bbr[:], 8), ALU.mult)
        S.tt("dve", W1[:, :, 0:8, :], bk(APr[:, :, 0:8], 8), bh(Bbi[:], 8), ALU.mult)
        S.tt("dve", W1[:, :, 0:8, :], W1[:, :, 0:8, :], W2[:, :, 0:8, :], ALU.add)
        halves(lambda P, Cc: Gbd[P, :, 1, :, Cc], W1[:, :, 0:8, :], 8)
        halves(lambda P, Cc: Bbd[P, :, 0, Cc], Bbr[:], 1)
        halves(lambda P, Cc: Bbd[P, :, 1, Cc], Bbi[:], 1)
        S.tt("dve", W2[:], bk(APi[:], 9), bh(Cim[:], 9), ALU.mult)
        S.tt("dve", W1[:], bk(APr[:], 9), bh(Cre[:], 9), ALU.mult)
        S.tt("dve", W1[:], W1[:], W2[:], ALU.subtract)
        halves(lambda P, Cc: WY[P, :, 0, :, Cc], W1[:], 9)
        S.tt("dve", W2[:], bk(APr[:], 9), bh(Cim[:], 9), ALU.mult)
        S.tt("dve", W1[:], bk(APi[:], 9), bh(Cre[:], 9), ALU.mult)
        S.tt("dve", W1[:], W1[:], W2[:], ALU.add)
        S.ts("dve", W1[:], W1[:], -1.0, None, ALU.mult)
        halves(lambda P, Cc: WY[P, :, 1, :, Cc], W1[:], 9)

        Kps = [E(nc.psum_tensor("Kps%d" % d, [128, 4, 8, 32], F32)) for d in range(2)]
        for d in range(2):
            for pr in range(16):
                u = d * 16 + pr
                q, pq = divmod(pr, 4)
                for ri in range(2):
                    S.mm(Kps[d][32 * pq:32 * pq + 32, q, :, :], Bbd[:, u, ri, :], WY[:, u, ri, 0:8, :],
                         start=(ri == 0), stop=(ri == 1), inc=(ri == 1 and pr == 15), tp=(0, 32 * pq))
        Kf, Kb = sb("Kf", [128, 4, 8, 32]), sb("Kb", [128, 4, 8, 32])
        S.copy("dve", Kf[:], Kps[0][:])
        S.copy("dve", Kb[:], Kps[1][:])
        Dd = sb("Dd", [128, 4, 32])
        S.tt("dve", Dd[:], m32[:].unsqueeze(1).to_broadcast([128, 4, 32]), dsk[:].unsqueeze(2).to_broadcast([128, 4, 32]), ALU.mult)
        S.tt("dve", Dd[:], Dd[:], Kf[:, :, 0, :], ALU.add)
        S.tt("dve", Dd[:], Dd[:], Kb[:, :, 0, :], ALU.add)
        S.copy("dve", INTRA[:, :, 0:64:9, :], Dd[:].unsqueeze(2).to_broadcast([128, 4, 8, 32]))
        for k in range(1, 8):
            n = 8 - k
            S.copy("dve", INTRA[:, :, k:k + 9 * (n - 1) + 1:9, :], Kf[:, :, k:k + 1, :].to_broadcast([128, 4, n, 32]))
            S.copy("dve", INTRA[:, :, 8 * k:8 * k + 9 * (n - 1) + 1:9, :], Kb[:, :, k:k + 1, :].to_broadcast([128, 4, n, 32]))

        psT = [E(nc.psum_tensor("psWX%d" % i, [128, 8, 128], BF16)) for i in range(2)]
        n = 0
        for q in range(4):
            for d in range(2):
                for ri in range(2):
                    pt = psT[n % 2]
                    n += 1
                    for e in range(8):
                        for pq in range(4):
                            u = d * 16 + 4 * q + pq
                            S.tr(pt[32 * pq:32 * pq + 32, e, :], Gbd[:, u, ri, e, :], c.identb[:],
                                 inc=(e == 7 and pq == 3), tp=(0, 32 * pq))
                    S.copy("act", WX[:, q, d, ri, :, :], pt[:])
        S.barrier()


def s5_main(c, U8r, U8i, R8):
    nc, S = c.nc, c.S
    Lmax = max(c.seq_lens)
    NCm = Lmax // 8
    with contextlib.ExitStack() as es:
        E = es.enter_context
        sb = lambda n, sh, dt=F32: E(nc.sbuf_tensor(n, sh, dt))
        WXq = sb("WXq", [128, 2, 2, 8, 128], BF16)
        WYq = sb("WYq", [128, 8, 2, 9, 32], BF16)
        INq = sb("INq", [128, 64, 32], BF16)
        Ec, Es = sb("Ec", [128, 8, NCm]), sb("Es", [128, 8, NCm])
        pwr, pwi = sb("pwr", [128, 8]), sb("pwi", [128, 8])
        pt1, pt2, pt3 = sb("pt1", [128, 8]), sb("pt2", [128, 8]), sb("pt3", [128, 8])
        zum = [[sb("zum%d_%d" % (r, i), [128, Lmax], BF16) for i in range(4)] for r in range(2)]
        Sbf = [sb("Sbf%d" % r, [128, 8, 2, NCm + 1], BF16) for r in range(2)]
        tmps = [[sb("tm%s%d" % (nm, i), [128, NCm]) for nm in "ABCDEF"] for i in range(2)]
        yg = sb("ygB", [128, Lmax])
        ET1 = yg[:, 0:Lmax // 2].rearrange("p (u k) -> p u k", u=8)
        ET2 = yg[:, Lmax // 2:Lmax].rearrange("p (u k) -> p u k", u=8)
        Xps = [[E(nc.psum_tensor("Xps%d_%d" % (i, ri), [128, 512], F32)) for ri in range(2)] for i in range(2)]
        Yps = [E(nc.psum_tensor("Yps%d" % i, [128, 512], F32)) for i in range(2)]
        for r in range(2):
            S.memset("pool" if r == 0 else "dve", Sbf[r][:], 0.0)
            for pq in range(4):
                if r == 0:
                    S.memset("pool", zum[r][pq][:], 0.0)
                else:
                    S.op("act", lambda e, t_=zum[r][pq]: e.memzero(t_[:]), reads=[], writes=[_nm(zum[r][pq][:])])
        nseq = len(c.seqs)

        def scan_prologue(q):
            S.dma("sp", "wq0", WXq[:].rearrange("p a b e n -> p (a b e n)"), c.WX_d[q], reads=["WX_d"])
            for d in range(2):
                usl = slice(d * 16 + 4 * q, d * 16 + 4 * q + 4)
                S.copy("dve", pwr[:, 4 * d:4 * d + 4], U8r[:, usl])
                S.copy("dve", pwi[:, 4 * d:4 * d + 4], U8i[:, usl])
            S.memset("dve", Ec[:, :, 0:1], 1.0)
            S.memset("dve", Es[:, :, 0:1], 0.0)
            seg = 1
            while seg < NCm:
                bcs = lambda ap: ap.unsqueeze(2).to_broadcast([128, 8, seg])
                cmul(S, "dve", Ec[:, :, seg:2 * seg], Es[:, :, seg:2 * seg], Ec[:, :, 0:seg], Es[:, :, 0:seg],
                     bcs(pwr[:]), bcs(pwi[:]), ET1[:, :, 0:seg], ET2[:, :, 0:seg])
                seg *= 2
                if seg < NCm:
                    S.tt("dve", pt1[:], pwr[:], pwr[:], ALU.mult)
                    S.tt("dve", pt2[:], pwi[:], pwi[:], ALU.mult)
                    S.tt("dve", pt3[:], pwr[:], pwi[:], ALU.mult)
                    S.tt("dve", pwr[:], pt1[:], pt2[:], ALU.subtract)
                    S.ts("dve", pwi[:], pt3[:], 2.0, None, ALU.mult)

        def out_prologue(q):
            for d in range(2):
                u0 = d * 16 + 4 * q
                S.dma("sp", "wq1", WYq[:, 4 * d:4 * d + 4].rearrange("p u a k n -> p u (a k n)"),
                      c.WY_d[u0:u0 + 4].rearrange("u p x -> p u x"), reads=["WY_d"])
            S.dma("sp", "wq2", INq[:].rearrange("p j n -> p (j n)"), c.IN_d[q], reads=["IN_d"])

        if True:
            NIT = 4 * nseq

            def step(t):
                g, h = t, t - 1
                gv, hv = g < NIT, h >= 0
                if gv:
                    q, k = divmod(g, nseq)
                    if k == 0:
                        scan_prologue(q)
                    s0, L = c.seqs[k]
                    n_c = L // 8
                    r = g % 2
                    for pq in range(4):
                        P = slice(32 * pq, 32 * pq + 32)
                        S.dma("sp" if pq % 2 == 0 else "act", "zuld%d_%d" % (r, pq), zum[r][pq][P, 0:L], c.zuT_d[q, P, s0:s0 + L],
                              reads=[("zuT_d", s0 + 512 * i) for i in range(L // 512)])
                    zdi = [zum[r][pq][:, 0:L].rearrange("p (t j c) -> p t j c", j=8, c=64) for pq in range(4)]
                if hv:
                    qh, kh = divmod(h, nseq)
                    if kh == 0:
                        out_prologue(qh)
                    s0h, Lh = c.seqs[kh]
                    n_ch = Lh // 8
                    rh = h % 2
                    zdh = [zum[rh][pq][:, 0:Lh].rearrange("p (t j c) -> p t j c", j=8, c=64) for pq in range(4)]
                for pair in range(4):
                    if gv:
                        ops = []
                        for uu in (2 * pair, 2 * pair + 1):
                            d, pq = divmod(uu, 4)
                            u = d * 16 + 4 * q + pq
                            xp = Xps[uu % 2]
                            for ri in range(2):
                                for j in range(8):
                                    e = 7 - j if d == 0 else j
                                    S.mm(xp[ri][:, 0:n_c], WXq[:, d, ri, e, :], zdi[pq][:, :, j, :], start=(j == 0), stop=(j == 7))
                            ec, esn = Ec[:, uu, 0:n_c], Es[:, uu, 0:n_c]
                            if d == 0:
                                xr, xi = xp[0][:, 0:n_c], xp[1][:, 0:n_c]
                                o_re, o_im = Sbf[r][:, uu, 0, 1:n_c + 1], Sbf[r][:, uu, 1, 1:n_c + 1]
                            else:
                                xr, xi = xp[0][:, 0:n_c][:, ::-1], xp[1][:, 0:n_c][:, ::-1]
                                o_re, o_im = Sbf[r][:, uu, 0, 0:n_c][:, ::-1], Sbf[r][:, uu, 1, 0:n_c][:, ::-1]
                            a, b, cc, dd, ee, ff = [tt_[:, 0:n_c] for tt_ in tmps[uu % 2]]
                            r8 = R8[:, u:u + 1].to_broadcast([128, n_c])
                            chain = [
                                lambda a=a, xr=xr, ec=ec: S.tt("dve", a, xr, ec, ALU.mult),
                                lambda b=b, xi=xi, esn=esn: S.tt("dve", b, xi, esn, ALU.mult),
                                lambda a=a, b=b: S.tt("dve", a, a, b, ALU.add),
                                lambda cc=cc, xi=xi, ec=ec: S.tt("dve", cc, xi, ec, ALU.mult),
                                lambda dd=dd, xr=xr, esn=esn: S.tt("dve", dd, xr, esn, ALU.mult),
                                lambda cc=cc, dd=dd: S.tt("dve", cc, cc, dd, ALU.subtract),
                                lambda a=a, b=b, r8=r8: S.scan(b, r8, a),
                                lambda cc=cc, dd=dd, r8=r8: S.scan(dd, r8, cc),
                                lambda a=a, b=b, ec=ec: S.tt("dve", a, b, ec, ALU.mult),
                                lambda ee=ee, dd=dd, esn=esn: S.tt("pool", ee, dd, esn, ALU.mult),
                                lambda cc=cc, dd=dd, ec=ec: S.tt("dve", cc, dd, ec, ALU.mult),
                                lambda ff=ff, b=b, esn=esn: S.tt("pool", ff, b, esn, ALU.mult),
                                lambda o_re=o_re, a=a, ee=ee: S.tt("pool", o_re, a, ee, ALU.subtract),
                                lambda o_im=o_im, cc=cc, ff=ff: S.tt("pool", o_im, cc, ff, ALU.add),
                            ]
                            if d == 1:
                                chain.append(lambda uu=uu: S.memset("pool", Sbf[r][:, uu, :, n_c:n_c + 1], 0.0))
                            ops.append(chain)
                        for i_ in range(max(len(ch) for ch in ops)):
                            for ch in ops:
                                if i_ < len(ch):
                                    ch[i_]()
                    if hv:
                        for j in (2 * pair, 2 * pair + 1):
                            yp = Yps[j % 2]
                            for pq in range(4):
                                P = slice(32 * pq, 32 * pq + 32)
                                first = True
                                for d in range(2):
                                    uu = d * 4 + pq
                                    kk = j + 1 if d == 0 else 8 - j
                                    for ri in range(2):
                                        rhs = Sbf[rh][:, uu, ri, 0:n_ch] if d == 0 else Sbf[rh][:, uu, ri, 1:n_ch + 1]
                                        S.mm(yp[P, 0:n_ch], WYq[:, uu, ri, kk, :], rhs, start=first, stop=False, inc=False, tp=(0, 32 * pq))
                                        first = False
                            for pq in range(4):
                                P = slice(32 * pq, 32 * pq + 32)
                                for jp in range(8):
                                    S.mm(yp[P, 0:n_ch], INq[:, jp * 8 + j, :], zdh[pq][:, :, jp, :], start=False, stop=(jp == 7),
                                         inc=(jp == 7 and pq == 3), tp=(0, 32 * pq))
                            S.act(yg[:, j:Lh:8], yp[:, 0:n_ch], AF.Gelu_apprx_tanh)
                if hv:
                    S.dma("pool", "ygst", c.yg_d[qh, :, s0h:s0h + Lh], yg[:, 0:Lh], writes=[("yg_d", qh, s0h)])

            for t in range(NIT + 1):
                step(t)
        S.barrier()


def phase_B2(c):
    nc, S = c.nc, c.S
    with contextlib.ExitStack() as es:
        E = es.enter_context
        sb = lambda n, sh, dt=F32: E(nc.sbuf_tensor(n, sh, dt))
        wg = sb("wg", [128, 4, DC], BF16)
        bg, gns = sb("bg", [128, 4]), sb("gns", [128, 4])
        ygf = [sb("ygf%d" % i, [128, 4, 512]) for i in range(3)]
        ygb = [sb("ygb%d" % i, [128, 4, 512], BF16) for i in range(2)]
        sg = [sb("sg%d" % i, [128, 512]) for i in range(4)]
        y2 = [sb("y2_%d" % i, [128, 4, 512]) for i in range(2)]
        sq = [sb("sqB%d" % i, [128, 512], BF16) for i in range(4)]
        var = [sb("varB%d" % i, [128, 512]) for i in range(2)]
        rst = [sb("rstB%d" % i, [128, 512]) for i in range(2)]
        y2n = [sb("y2n%d" % i, [128, 4, 512], BF16) for i in range(2)]
        ps_g = [E(nc.psum_tensor("psB_g%d" % i, [128, 512], F32)) for i in range(3)]
        ps_s = [E(nc.psum_tensor("psB_s%d" % i, [128, 512], F32)) for i in range(2)]
        for kt in range(4):
            S.dma("pool", "wgld", wg[:, kt, :], c.w_glu[kt * 128:(kt + 1) * 128, :])
        S.dma("sp", "cB0", bg[:], c.b_glu[0, :].rearrange("(f p) -> p f", p=128), allow_slow_non_contiguous=True)
        S.dma("sp", "cB1", gns[:], c.gn_ssm[0, :].rearrange("(f p) -> p f", p=128), allow_slow_non_contiguous=True)
        tiles = []
        for (s0_, L) in c.seqs:
            for i in range(L // 512):
                tiles.append((s0_ + i * 512, s0_))
        NTI = len(tiles)

        def load(i):
            if i >= NTI:
                return
            t0, s0_ = tiles[i]
            S.dma("sp", "ygld%d" % (i % 3), ygf[i % 3][:], c.yg_d[:, :, t0:t0 + 512].rearrange("q p t -> p q t"),
                  reads=[("yg_d", q, s0_) for q in range(4)])

        wst = [sb("wupst%d" % i, [128, 2 * DFF], BF16) for i in range(2)]

        def st0(i):
            if i < 8:
                S.dma("pool", "wupst%d" % (i % 2), wst[i % 2][:], c.w_up[i * 128:(i + 1) * 128, :])
            for kt in range(4):
                S.copy("act" if kt % 2 else "dve", ygb[i % 2][:, kt, :], ygf[i % 3][:, kt, :], wk=["ygb%d_%d" % (i % 2, kt)])

        def st1(i):
            if i < 8:
                S.dma("pool", "wupwr%d" % (i % 2), c.wup_d[:, :, i, :].rearrange("m p c -> p m c"),
                      wst[i % 2][:].rearrange("p (m c) -> p m c", c=128), writes=["wup_d"])
            for m in range(4):
                pg = ps_g[m % 3]
                for kt in range(4):
                    S.mm(pg[:], wg[:, kt, m * 128:(m + 1) * 128], ygb[i % 2][:, kt, :], start=(kt == 0), stop=(kt == 3),
                         rk=["wg", "ygb%d_%d" % (i % 2, kt)])
                S.act(sg[m][:], pg[:], AF.Sigmoid, bias=bg[:, m:m + 1])
            for m in range(4):
                S.tt("dve", y2[i % 2][:, m, :], ygf[i % 3][:, m, :], sg[m][:], ALU.mult, wk=["y2_%d_%d" % (i % 2, m)])
            for m in range(4):
                S.act(sq[m][:], y2[i % 2][:, m, :], AF.Square, rk=["y2_%d_%d" % (i % 2, m)])
                S.mm(ps_s[i % 2][:], c.onesb[:], sq[m][:], start=(m == 0), stop=(m == 3), inc=True)
            load(i + 3)

        def st2(i):
            t0 = tiles[i][0]
            rsqrt_act(S, rst[i % 2][:], ps_s[i % 2][:], 1.0 / DC, var[i % 2][:], c.epsc[:])
            for m in range(4):
                S.stt(y2n[i % 2][:, m, :], y2[i % 2][:, m, :], gns[:, m:m + 1], rst[i % 2][:], ALU.mult, ALU.mult,
                      rk=["y2_%d_%d" % (i % 2, m), "gns", _nm(rst[i % 2][:])])
            S.dma("pool", "y2nst%d" % (i % 2), c.ycat_d[4:8, :, t0:t0 + 512].rearrange("f p t -> p f t"), y2n[i % 2][:],
                  writes=[("ycat_d", 1, t0)])

        for i in range(3):
            load(i)
        pipeline(NTI, [st0, st1, st2])
        for kt in range(NTI, 8):
            S.dma("pool", "wupst%d" % (kt % 2), wst[kt % 2][:], c.w_up[kt * 128:(kt + 1) * 128, :])
            S.dma("pool", "wupwr%d" % (kt % 2), c.wup_d[:, :, kt, :].rearrange("m p c -> p m c"),
                  wst[kt % 2][:].rearrange("p (m c) -> p m c", c=128), writes=["wup_d"])
        S.barrier()


def phase_C1(c):
    nc, S = c.nc, c.S
    with contextlib.ExitStack() as es:
        E = es.enter_context
        sb = lambda n, sh, dt=F32: E(nc.sbuf_tensor(n, sh, dt))
        wo = c.wo
        gpost, gffn = sb("gpost", [128, D]), sb("gffn", [128, D])
        yc = [sb("ycC%d" % i, [128, 8, 512], BF16) for i in range(2)]
        xt = [sb("xtC%d" % i, [128, 4, D]) for i in range(2)]
        junk = sb("junkC", [128, D], BF16)
        st = [sb("stC%d" % i, [128, 8]) for i in range(3)]
        tmp = [sb("tmpC%d" % i, [128, D]) for i in range(2)]
        x1 = [sb("x1C%d" % i, [128, D]) for i in range(3)]
        h2 = [sb("h2C%d" % i, [128, D], BF16) for i in range(3)]
        h2T = [sb("h2TC%d" % i, [128, 8, 512], BF16) for i in range(2)]
        ps_o = [E(nc.psum_tensor("psC_o%d" % i, [128, D], F32)) for i in range(3)]
        ps_t = [E(nc.psum_tensor("psC_t%d" % i, [128, D], BF16)) for i in range(2)]
        S.dma("sp", "cC0", gpost[:], c.post_mix_g.to_broadcast([128, D]))
        S.dma("sp", "cC1", gffn[:], c.pre_ffn_g.to_broadcast([128, D]))
        tiles = []
        for (s0_, L) in c.seqs:
            for i in range(L // 512):
                tiles.append(s0_ + i * 512)
        NTI = len(tiles)

        def load(ti):
            if ti >= NTI:
                return
            t0 = tiles[ti]
            S.dma("sp", "ycld%d" % (ti % 2), yc[ti % 2][:], c.ycat_d[:, :, t0:t0 + 512].rearrange("f p t -> p f t"),
                  reads=[("ycat_d", 0, t0), ("ycat_d", 1, t0)])
            S.dma("sp", "xldC%d" % (ti % 2), xt[ti % 2][:], c.x[t0:t0 + 512, :].rearrange("(g p) d -> p g d", p=128))

        def s0(i):
            ti, g = divmod(i, 4)
            s = ti % 2
            if g == 1:
                load(ti + 1)
            po = ps_o[i % 3]
            for hh in range(2):
                for kt in range(8):
                    S.mm(po[:, hh * 512:(hh + 1) * 512], yc[s][:, kt, g * 128:(g + 1) * 128], wo[:, kt, hh * 512:(hh + 1) * 512],
                         start=(kt == 0), stop=(kt == 7), inc=(kt == 7 and hh == 1))

        def s1(i):
            ti, g = divmod(i, 4)
            s, b3, t0 = ti % 2, i % 3, tiles[ti]
            po, sv = ps_o[b3], st[b3]
            S.act(junk[:], po[:], AF.Square, accum_out=sv[:, 0:1], wk=["junkC", "stC%d_a" % b3])
            rsqrt_act(S, sv[:, 2:3], sv[:, 0:1], 1.0 / D, sv[:, 1:2], c.epsc[:], rk=["stC%d_a" % b3], tk=["stC%d_b" % b3], wk=["stC%d_c" % b3])
            S.stt(tmp[i % 2][:], po[:], sv[:, 2:3], gpost[:], ALU.mult, ALU.mult, rk=[_nm(po[:]), "stC%d_c" % b3, "gpost"])
            S.tt("dve", x1[b3][:], tmp[i % 2][:], xt[s][:, g, :], ALU.add)
            S.dma("pool", "x1st%d" % b3, c.x1_d[t0 + g * 128:t0 + (g + 1) * 128, :], x1[b3][:], writes=[("x1_d", t0 + g * 128)])

        def s2(i):
            b3 = i % 3
            sv = st[b3]
            S.act(junk[:], x1[b3][:], AF.Square, accum_out=sv[:, 3:4], wk=["junkC", "stC%d_d" % b3])
            rsqrt_act(S, sv[:, 5:6], sv[:, 3:4], 1.0 / D, sv[:, 4:5], c.epsc[:], rk=["stC%d_d" % b3], tk=["stC%d_e" % b3], wk=["stC%d_f" % b3])
            S.stt(h2[b3][:], x1[b3][:], sv[:, 5:6], gffn[:], ALU.mult, ALU.mult, rk=[_nm(x1[b3][:]), "stC%d_f" % b3, "gffn"])

        def s3(i):
            ti, g = divmod(i, 4)
            s, b3, t0 = ti % 2, i % 3, tiles[ti]
            pt = ps_t[i % 2]
            for kt in range(8):
                S.tr(pt[:, kt * 128:(kt + 1) * 128], h2[b3][:, kt * 128:(kt + 1) * 128], c.identb[:], inc=(kt == 7))
            S.copy("act", h2T[s][:, :, g * 128:(g + 1) * 128], pt[:].rearrange("p (k t) -> p k t", k=8))
            if g == 3:
                S.dma("pool", "h2Tst%d" % s, c.h2T_d[:, :, t0:t0 + 512].rearrange("f p t -> p f t"), h2T[s][:],
                      writes=[("h2T_d", t0)])

        load(0)
        pipeline(NTI * 4, [s0, s1, s2, s3])
        S.barrier()


def phase_C2(c):
    nc, S = c.nc, c.S
    BLK = 1024
    with contextlib.ExitStack() as es:
        E = es.enter_context
        sb = lambda n, sh, dt=F32: E(nc.sbuf_tensor(n, sh, dt))
        wdn = c.wdn
        fw, fb = sb("fw", [128, NM_UP, 3]), sb("fb", [128, NM_UP])
        gpo = sb("gpo", [128, D])
        wu = [sb("wu%d" % i, [128, 8, 128], BF16) for i in range(4)]
        hT = [sb("hTF%d" % i, [128, 8, BLK + 2], BF16) for i in range(2)]
        actT = sb("actT", [128, 22, BLK], BF16)
        ag = [sb("agF%d" % i, [128, BLK]) for i in range(2)]
        av = [sb("avF%d" % i, [128, BLK]) for i in range(2)]
        sgl = [sb("sgF%d" % i, [128, BLK]) for i in range(2)]
        hv = [sb("hvF%d" % i, [128, 2]) for i in range(2)]
        x1 = [sb("x1F%d" % i, [128, D]) for i in range(2)]
        junk = sb("junkF", [128, D], BF16)
        st = [sb("stF%d" % i, [128, 4]) for i in range(2)]
        tmp = [sb("tmpF%d" % i, [128, D]) for i in range(2)]
        yo = [sb("yoF%d" % i, [128, D]) for i in range(2)]
        PS = [E(nc.psum_tensor("psF%d" % i, [128, 1024], F32)) for i in range(3)]
        PHs = [E(nc.psum_tensor("psFh%d" % i, [128, 512], F32)) for i in range(2)]
        for j in range(3):
            S.dma("sp", "cF0", fw[:, :, j], c.ffn_conv_w[j, :].rearrange("(m p) -> p m", p=128), allow_slow_non_contiguous=True)
        S.dma("sp", "cF1", fb[:], c.ffn_conv_b[0, :].rearrange("(m p) -> p m", p=128), allow_slow_non_contiguous=True)
        S.dma("sp", "cF2", gpo[:], c.post_ffn_g.to_broadcast([128, D]))
        blocks = []
        for (s0, L) in c.seqs:
            nb = L // BLK
            for i in range(nb):
                blocks.append((s0 + i * BLK, i == 0, i == nb - 1))
        nwu = [0]

        def load_wu(m):
            w = wu[nwu[0] % 4]
            S.dma("sp", "wuld%d" % (nwu[0] % 4), w[:], c.wup_d[m, :, :, :], reads=["wup_d"])
            nwu[0] += 1
            return w

        def load_h(bi):
            t0, first, last = blocks[bi]
            h = hT[bi % 2]
            lo = 1 if first else 0
            hi = BLK + 1 if last else BLK + 2
            S.dma("sp", "hld%d" % (bi % 2), h[:, :, lo:hi], c.h2T_d[:, :, t0 - 1 + lo:t0 - 1 + hi].rearrange("f p t -> p f t"),
                  reads=[("h2T_d", t0 + 512 * k) for k in range(-1 if not first else 0, 3 if not last else 2)])
            if first:
                S.memset("pool", h[:, :, 0:1], 0.0)
            if last:
                S.memset("pool", h[:, :, BLK + 1:BLK + 2], 0.0)

        load_h(0)
        nps = 0
        nh = 0
        nd = 0
        for bi, (t0, first, last) in enumerate(blocks):
            h = hT[bi % 2]
            if bi + 1 < len(blocks):
                load_h(bi + 1)
            interior = (not first) or (not last)
            for mp in range(22):
                res = []
                for which, m in enumerate((mp, 22 + mp)):
                    w = load_wu(m)
                    ps = PS[nps % 3]
                    nps += 1
                    for hh in range(2):
                        for kt in range(8):
                            S.mm(ps[:, hh * 512:(hh + 1) * 512], w[:, kt, :], h[:, kt, 1 + hh * 512:1 + (hh + 1) * 512],
                                 start=(kt == 0), stop=(kt == 7), inc=(kt == 7 and hh == 1))
                    PH = PHs[nh % 2]
                    nh += 1
                    for kt in range(8):
                        S.mm(PH[:, 0:2], w[:, kt, :], h[:, kt, 0:BLK + 2:BLK + 1], start=(kt == 0), stop=(kt == 7))
                    a = (ag if which == 0 else av)[mp % 2]
                    S.act(a[:], ps[:], AF.Identity, scale=fw[:, m, 1:2], bias=fb[:, m:m + 1])
                    S.stt(a[:, 1:BLK], ps[:, 0:BLK - 1], fw[:, m, 0:1], a[:, 1:BLK], ALU.mult, ALU.add)
                    S.stt(a[:, 0:BLK - 1], ps[:, 1:BLK], fw[:, m, 2:3], a[:, 0:BLK - 1], ALU.mult, ALU.add)
                    hvv = hv[which]
                    S.tt("dve", hvv[:], PH[:, 0:2], fw[:, m, 0:3:2], ALU.mult)
                    S.tt("dve", a[:, 0:BLK:BLK - 1], a[:, 0:BLK:BLK - 1], hvv[:], ALU.add)
                    res.append(a)
                g = sgl[mp % 2]
                S.act(g[:], res[0][:], AF.Silu)
                S.tt("pool", actT[:, mp, :], g[:], res[1][:], ALU.mult, wk=["actT_%d" % mp])
            for tg in range(BLK // 128):
                b = nd % 2
                nd += 1
                PD = PS[nps % 3]
                nps += 1
                tt0 = t0 + tg * 128
                S.dma("sp", "x1ld%d" % b, x1[b][:], c.x1_d[tt0:tt0 + 128, :], reads=[("x1_d", tt0)])
                for hh in range(2):
                    for kt in range(22):
                        S.mm(PD[:, hh * 512:(hh + 1) * 512], actT[:, kt, tg * 128:(tg + 1) * 128], wdn[:, kt, hh * 512:(hh + 1) * 512],
                             start=(kt == 0), stop=(kt == 21), inc=(kt == 21 and hh == 1), rk=["actT_%d" % kt, "wdn"])
                sv = st[b]
                S.act(junk[:], PD[:], AF.Square, accum_out=sv[:, 0:1], wk=["junkF", "stF%d_a" % b])
                rsqrt_act(S, sv[:, 2:3], sv[:, 0:1], 1.0 / D, sv[:, 1:2], c.epsc[:], rk=["stF%d_a" % b], tk=["stF%d_b" % b], wk=["stF%d_c" % b])
                S.stt(tmp[b][:], PD[:], sv[:, 2:3], gpo[:], ALU.mult, ALU.mult, rk=[_nm(PD[:]), "stF%d_c" % b, "gpo"])
                S.tt("pool", yo[b][:], tmp[b][:], x1[b][:], ALU.add)
                S.dma("pool", "yst%d" % b, c.y[tt0:tt0 + 128, :], yo[b][:], writes=[("y", tt0)])
        S.barrier()


SEQ_LENS = (2048, 2048, 4096, 4096)
_CONSTS = {
    "ident": np.eye(128, dtype=np.float32),
    "mask32": (np.arange(128)[:, None] % 32 == np.arange(32)[None, :]).astype(np.float32),
}
_WKEYS = ["pre_mix_g", "w_in", "conv_w", "lam_re", "lam_im", "log_step", "b_re", "b_im", "c_re", "c_im", "d_skip",
          "w_glu", "b_glu", "gn_conv", "gn_ssm", "w_out", "post_mix_g", "pre_ffn_g", "w_up", "ffn_conv_w",
          "ffn_conv_b", "w_down", "post_ffn_g"]


def _weights_map(inputs):
    m = {}
    for k in _WKEYS:
        a = np.asarray(inputs[k], dtype=np.float32)
        a = a[0]
        if a.ndim == 1:
            a = a[None, :]
        m[k] = np.ascontiguousarray(a)
    m.update(_CONSTS)
    return m


def kernel(**inputs):
    xp = np.asarray(inputs["x_prompt"], dtype=np.float32)
    xs = np.asarray(inputs["x_sample"], dtype=np.float32)
    n = 8
    nc = build(SEQ_LENS)
    wm = _weights_map(inputs)
    in_maps = []
    for i in range(n):
        xc = np.concatenate([xp[2 * i].reshape(-1, D), xp[2 * i + 1].reshape(-1, D),
                             xs[2 * i].reshape(-1, D), xs[2 * i + 1].reshape(-1, D)], axis=0)
        d = dict(wm)
        d["x"] = np.ascontiguousarray(xc)
        in_maps.append(d)
    res = run_bass_kernel_spmd(nc, in_maps, core_ids=list(range(n)))
    yp = np.empty_like(xp)
    ys = np.empty_like(xs)
    for i in range(n):
        y = res.results[i]["y"]
        yp[2 * i] = y[0:2048]
        yp[2 * i + 1] = y[2048:4096]
        ys[2 * i] = y[4096:8192]
        ys[2 * i + 1] = y[8192:12288]
    return (yp, ys)
```

```python
import contextlib
import math
import numpy as np
import concourse.bass as bass
import concourse.mybir as mybir
from concourse.bass_utils import run_bass_kernel_spmd

F32, BF16 = mybir.dt.float32, mybir.dt.bfloat16
AF, ALU = mybir.ActivationFunctionType, mybir.AluOpType

D = 1024
DC = 512
DIN = 2048
DFF = 2816
NM_UP = 44
EPS = 1e-6
MAGIC = 12582912.0
TWO_PI = 2.0 * math.pi


def _nm(ap):
    return ap.name


class Sched:
    def __init__(self, nc, es):
        self.nc, self.es = nc, es
        self.engs = {"pe": nc.tensor, "act": nc.scalar, "dve": nc.vector, "pool": nc.gpsimd, "sp": nc.sync}
        self.sems, self.cnt = {}, {}
        self.seen = {e: {} for e in self.engs}
        self.lastw, self.readers = {}, {}

    def _sem(self, key):
        if key not in self.sems:
            self.sems[key] = self.es.enter_context(self.nc.semaphore("s%d" % len(self.sems)))
            self.cnt[key] = 0
        return self.sems[key]

    def _wait(self, eng, deps):
        for (k, v) in deps:
            if eng == "pe" and k == "pe":
                continue
            if self.seen[eng].get(k, 0) < v:
                self.engs[eng].wait_ge(self._sem(k), v)
                self.seen[eng][k] = v

    def _deps(self, reads, writes):
        deps = []
        for b in reads:
            if b in self.lastw:
                deps.append(self.lastw[b])
        for b in writes:
            if b in self.lastw:
                deps.append(self.lastw[b])
            deps.extend(self.readers.get(b, {}).items())
        return deps

    def _record(self, ev, reads, writes):
        for b in writes:
            self.lastw[b] = ev
            self.readers[b] = {}
        for b in reads:
            r = self.readers.setdefault(b, {})
            if r.get(ev[0], 0) < ev[1]:
                r[ev[0]] = ev[1]

    def op(self, eng, fn, reads=(), writes=(), inc=True):
        self._wait(eng, self._deps(reads, writes))
        ins = fn(self.engs[eng])
        self._sem(eng)
        if inc:
            self.cnt[eng] += 1
            ins.then_inc(self.sems[eng], 1)
            ev = (eng, self.cnt[eng])
        else:
            ev = (eng, self.cnt[eng] + 1)
        self._record(ev, reads, writes)
        return ev

    def dma(self, q, semkey, out, in_, reads=None, writes=None, **kw):
        reads = [_nm(in_)] if reads is None else reads
        writes = [_nm(out)] if writes is None else writes
        self._wait(q, self._deps(reads, writes))
        s = self._sem(semkey)
        self.cnt[semkey] += 16
        self.engs[q].dma_start(out=out, in_=in_, **kw).then_inc(s, 16)
        ev = (semkey, self.cnt[semkey])
        self._record(ev, reads, writes)
        return ev

    def dma_split(self, q, semkey, out, in_, axis, nchunks, **kw):
        n = out.shape[axis]
        step = -(-n // nchunks)
        ev = None
        for lo in range(0, n, step):
            hi = min(n, lo + step)
            idx = [slice(None)] * len(out.shape)
            idx[axis] = slice(lo, hi)
            ev = self.dma(q, semkey, out[tuple(idx)], in_[tuple(idx)], **kw)
        return ev

    def wait_all(self, eng):
        mx = {}
        for (k, v) in self.lastw.values():
            mx[k] = max(mx.get(k, 0), v)
        for r in self.readers.values():
            for k, v in r.items():
                mx[k] = max(mx.get(k, 0), v)
        self._wait(eng, [(k, min(v, self.cnt[k])) for k, v in mx.items() if k != eng or eng != "pe"])

    def barrier(self):
        for e in self.engs:
            self.wait_all(e)
        self.lastw, self.readers = {}, {}

    def tt(self, eng, out, in0, in1, op, rk=None, wk=None):
        return self.op(eng, lambda e: e.tensor_tensor(out=out, in0=in0, in1=in1, op=op),
                       reads=rk if rk is not None else [_nm(in0), _nm(in1)], writes=wk if wk is not None else [_nm(out)])

    def ts(self, eng, out, in0, s1, s2, op0, op1=None, rk=None, wk=None):
        r = [_nm(in0)] + [_nm(s) for s in (s1, s2) if hasattr(s, "name")]
        if op1 is None:
            f = lambda e: e.tensor_scalar(out=out, in0=in0, scalar1=s1, scalar2=None, op0=op0)
        else:
            f = lambda e: e.tensor_scalar(out=out, in0=in0, scalar1=s1, scalar2=s2, op0=op0, op1=op1)
        return self.op(eng, f, reads=rk if rk is not None else r, writes=wk if wk is not None else [_nm(out)])

    def stt(self, out, in0, scalar, in1, op0, op1, rk=None, wk=None):
        r = [_nm(in0), _nm(in1)] + ([_nm(scalar)] if hasattr(scalar, "name") else [])
        return self.op("dve", lambda e: e.scalar_tensor_tensor(out=out, in0=in0, scalar=scalar, in1=in1, op0=op0, op1=op1),
                       reads=rk if rk is not None else r, writes=wk if wk is not None else [_nm(out)])

    def act(self, out, in_, func, bias=None, scale=None, accum_out=None, rk=None, wk=None):
        kw = {}
        r = [_nm(in_)]
        w = [_nm(out)]
        if bias is not None:
            kw["bias"] = bias
            if hasattr(bias, "name"):
                r.append(_nm(bias))
        if scale is not None:
            kw["scale"] = scale
            if hasattr(scale, "name"):
                r.append(_nm(scale))
        if accum_out is not None:
            kw["accum_out"] = accum_out
            w.append(_nm(accum_out))
        return self.op("act", lambda e: e.activation(out=out, in_=in_, func=func, **kw),
                       reads=rk if rk is not None else r, writes=wk if wk is not None else w)

    def copy(self, eng, out, in_, rk=None, wk=None):
        if eng == "act":
            return self.act(out, in_, AF.Copy, rk=rk, wk=wk)
        return self.op(eng, lambda e: e.tensor_copy(out=out, in_=in_),
                       reads=rk if rk is not None else [_nm(in_)], writes=wk if wk is not None else [_nm(out)])

    def memset(self, eng, out, val, wk=None):
        return self.op(eng, lambda e: e.memset(out, val), reads=[], writes=wk if wk is not None else [_nm(out)])

    def mm(self, out, lhsT, rhs, start, stop, inc=None, tp=None, rk=None, wk=None):
        inc = stop if inc is None else inc
        kw = {} if tp is None else {"tile_position": tp}
        return self.op("pe", lambda e: e.matmul(out, lhsT=lhsT, rhs=rhs, start=start, stop=stop, **kw),
                       reads=rk if rk is not None else [_nm(lhsT), _nm(rhs)],
                       writes=wk if wk is not None else [_nm(out)], inc=inc)

    def tr(self, out, in_, ident, inc=True, tp=None, rk=None, wk=None):
        kw = {} if tp is None else {"tile_position": tp}
        return self.op("pe", lambda e: e.transpose(out, in_, ident, **kw),
                       reads=rk if rk is not None else [_nm(in_), _nm(ident)],
                       writes=wk if wk is not None else [_nm(out)], inc=inc)

    def scan(self, out, d0, d1, init=0.0):
        return self.op("dve", lambda e: e.tensor_tensor_scan(out=out, data0=d0, data1=d1, initial=init,
                                                             op0=ALU.mult, op1=ALU.add),
                       reads=[_nm(d0), _nm(d1)], writes=[_nm(out)])


class Ctx:
    pass


def pipeline(n, stages):
    for t in range(n + len(stages) - 1):
        for k, st in enumerate(stages):
            i = t - k
            if 0 <= i < n:
                st(i)


def rsqrt_act(S, out, in_, scale, tmp, eps_ap, rk=None, wk=None, tk=None):
    S.act(tmp, in_, AF.Ln, scale=scale, bias=eps_ap, rk=rk, wk=tk)
    S.act(out, tmp, AF.Exp, scale=-0.5, rk=tk, wk=wk)


def build(seq_lens, debug=False):
    NT = sum(seq_lens)
    assert all(L % 1024 == 0 for L in seq_lens)
    nc = bass.Bass("TRN2", target_bir_lowering=False)
    c = Ctx()
    c.nc, c.NT, c.seq_lens, c.debug = nc, NT, seq_lens, debug
    c.seqs = []
    t = 0
    for L in seq_lens:
        c.seqs.append((t, L))
        t += L

    def din(name, shape):
        return nc.dram_tensor(name, list(shape), F32, kind="ExternalInput").ap()

    c.x = din("x", [NT, D])
    c.pre_mix_g = din("pre_mix_g", [1, D])
    c.w_in = din("w_in", [D, DIN])
    c.conv_w = din("conv_w", [3, DC])
    c.lam_re = din("lam_re", [2, 32, 64])
    c.lam_im = din("lam_im", [2, 32, 64])
    c.log_step = din("log_step", [2, 32])
    c.b_re = din("b_re", [2, 32, 64, 16])
    c.b_im = din("b_im", [2, 32, 64, 16])
    c.c_re = din("c_re", [2, 32, 16, 64])
    c.c_im = din("c_im", [2, 32, 16, 64])
    c.d_skip = din("d_skip", [1, DC])
    c.w_glu = din("w_glu", [DC, DC])
    c.b_glu = din("b_glu", [1, DC])
    c.gn_conv = din("gn_conv", [1, DC])
    c.gn_ssm = din("gn_ssm", [1, DC])
    c.w_out = din("w_out", [D, D])
    c.post_mix_g = din("post_mix_g", [1, D])
    c.pre_ffn_g = din("pre_ffn_g", [1, D])
    c.w_up = din("w_up", [D, 2 * DFF])
    c.ffn_conv_w = din("ffn_conv_w", [3, 2 * DFF])
    c.ffn_conv_b = din("ffn_conv_b", [1, 2 * DFF])
    c.w_down = din("w_down", [DFF, D])
    c.post_ffn_g = din("post_ffn_g", [1, D])
    c.ident = din("ident", [128, 128])
    c.mask32 = din("mask32", [128, 32])
    c.y = nc.dram_tensor("y", [NT, D], F32, kind="ExternalOutput").ap()

    sk = "ExternalOutput" if debug else "Internal"
    c.zuT_d = nc.dram_tensor("zuT_d", [4, 128, NT], BF16, kind=sk).ap()
    c.ycat_d = nc.dram_tensor("ycat_d", [8, 128, NT], BF16, kind=sk).ap()
    c.yg_d = nc.dram_tensor("yg_d", [4, 128, NT], F32, kind=sk).ap()
    c.x1_d = nc.dram_tensor("x1_d", [NT, D], F32, kind=sk).ap()
    c.h2T_d = nc.dram_tensor("h2T_d", [8, 128, NT], BF16, kind=sk).ap()
    c.wup_d = nc.dram_tensor("wup_d", [NM_UP, 128, 8, 128], BF16, kind="Internal").ap()
    c.WX_d = nc.dram_tensor("WX_d", [4, 128, 4096], BF16, kind="Internal").ap()
    c.WY_d = nc.dram_tensor("WY_d", [32, 128, 576], BF16, kind="Internal").ap()
    c.IN_d = nc.dram_tensor("IN_d", [4, 128, 2048], BF16, kind="Internal").ap()

    with contextlib.ExitStack() as es:
        S = Sched(nc, es)
        c.S = S
        c.es = es
        E = es.enter_context
        c.identb = E(nc.sbuf_tensor("identb", [128, 128], BF16))
        c.identf = E(nc.sbuf_tensor("identf", [128, 128], F32))
        c.onesb = E(nc.sbuf_tensor("onesb", [128, 128], BF16))
        S.dma("pool", "c0", c.identb[:], c.ident[:, :])
        S.dma("sp", "c1", c.identf[:], c.ident[:, :])
        S.memset("dve", c.onesb[:], 1.0)
        c.epsc = E(nc.sbuf_tensor("epsc", [128, 1], F32))
        S.memset("dve", c.epsc[:], EPS)
        S.barrier()
        gsb = lambda n, sh: E(nc.sbuf_tensor(n, sh, F32))
        c.lamre, c.lamim, c.lst = gsb("lamre", [128, 32]), gsb("lamim", [128, 32]), gsb("lst", [128, 32])
        c.dsk, c.m32 = gsb("dsk", [128, 4]), gsb("m32", [128, 32])
        S.dma_split("act", "p0", c.lamre[:].rearrange("p (d r) -> p d r", d=2), c.lam_re.rearrange("d (r q) p -> (q p) d r", q=2),
                    0, 4, allow_slow_non_contiguous=True)
        S.dma_split("act", "p1", c.lamim[:].rearrange("p (d r) -> p d r", d=2), c.lam_im.rearrange("d (r q) p -> (q p) d r", q=2),
                    0, 4, allow_slow_non_contiguous=True)
        for q in range(2):
            src = c.log_step.rearrange("d (r q) -> q d r", q=2)[q:q + 1]
            S.dma_split("act", "p2", c.lst[64 * q:64 * q + 64, :].rearrange("p (d r) -> p d r", d=2), src.to_broadcast([64, 2, 16]),
                        0, 2, allow_slow_non_contiguous=True)
        S.dma("act", "p7", c.dsk[:], c.d_skip[0, :].rearrange("(f p) -> p f", p=128), allow_slow_non_contiguous=True)
        S.dma("act", "p8", c.m32[:], c.mask32[:, :])
        stage = c.stage = ("ALL" if not debug else debug)
        phase_A(c)
        if stage in ("ALL", "B1", "B2", "C1", "C2"):
            phase_B1(c)
        if stage in ("ALL", "B2", "C1", "C2"):
            c.wo = E(nc.sbuf_tensor("wo", [128, 8, D], BF16))
            c.wdn = E(nc.sbuf_tensor("wdn", [128, 22, D], BF16))
            for kt in range(8):
                S.dma("pool", "wold", c.wo[:, kt, :], c.w_out[kt * 128:(kt + 1) * 128, :])
            for kt in range(22):
                S.dma("pool", "wdnld", c.wdn[:, kt, :], c.w_down[kt * 128:(kt + 1) * 128, :])
            phase_B2(c)
        if stage in ("ALL", "C1", "C2"):
            phase_C1(c)
        if stage in ("ALL", "C2"):
            phase_C2(c)
        for e in ("sp", "pool", "act", "dve", "pe"):
            S.wait_all(e)
    return nc


def phase_prep_wup(c):
    nc, S = c.nc, c.S
    with contextlib.ExitStack() as es:
        E = es.enter_context
        st = [E(nc.sbuf_tensor("wupst%d" % i, [128, 2 * DFF], BF16)) for i in range(2)]
        for kt in range(8):
            s = st[kt % 2]
            S.dma("pool", "wupst%d" % (kt % 2), s[:], c.w_up[kt * 128:(kt + 1) * 128, :])
            S.dma("sp", "wupwr%d" % (kt % 2), c.wup_d[:, :, kt, :].rearrange("m p c -> p m c"),
                  s[:].rearrange("p (m c) -> p m c", c=128), writes=["wup_d"])
        S.barrier()


def phase_A(c):
    nc, S = c.nc, c.S
    NT = c.NT
    with contextlib.ExitStack() as es:
        E = es.enter_context
        sb = lambda n, sh, dt: E(nc.sbuf_tensor(n, sh, dt))
        w_in_sb = sb("w_in_sb", [128, 8, DIN], BF16)
        gpre = sb("gpre", [128, D], F32)
        cw = sb("cw", [128, 4, 3], F32)
        gnc = sb("gnc", [128, 4], F32)
        xt = [sb("xt%d" % i, [128, 4, D], F32) for i in range(2)]
        junk = sb("junkA", [128, D], BF16)
        junk2 = sb("junkA2", [128, D], BF16)
        ss = [sb("ssA%d" % i, [128, 4], F32) for i in range(2)]
        var4 = [sb("var4A%d" % i, [128, 4], F32) for i in range(2)]
        rstd = [sb("rstdA%d" % i, [128, 4], F32) for i in range(2)]
        xn = [sb("xn%d" % i, [128, D], BF16) for i in range(4)]
        hT = [sb("hT%d" % i, [128, 8, 512], BF16) for i in range(2)]
        pbuf = [sb("pbuf%d" % i, [128, 4, 514], F32) for i in range(2)]
        zbb = [sb("zbb%d" % i, [128, 4, 512], F32) for i in range(2)]
        zcs = [sb("zcs%d" % i, [128, 512], F32) for i in range(4)]
        zus = [sb("zus%d" % i, [128, 4, 512], BF16) for i in range(2)]
        acc = [sb("accA%d" % i, [128, 512], F32) for i in range(4)]
        yc = sb("ycA", [128, 4, 512], F32)
        sq = [sb("sqA%d" % i, [128, 512], BF16) for i in range(4)]
        var = sb("varA", [128, 512], F32)
        rst = sb("rstA", [128, 512], F32)
        ycn = [sb("ycn%d" % i, [128, 4, 512], BF16) for i in range(2)]
        ps_t = [E(nc.psum_tensor("psA_t%d" % i, [128, D], BF16)) for i in range(2)]
        ps_z = [E(nc.psum_tensor("psA_z%d" % i, [128, 512], F32)) for i in range(5)]
        ps_s = E(nc.psum_tensor("psA_s", [128, 512], F32))

        for cb in range(4):
            S.dma("pool", "winld%d" % cb, w_in_sb[:, :, cb * 512:(cb + 1) * 512],
                  c.w_in[:, cb * 512:(cb + 1) * 512].rearrange("(k p) n -> p k n", p=128), writes=["w_in_sb_%d" % cb])
        S.dma("sp", "cA0", gpre[:], c.pre_mix_g.to_broadcast([128, D]))
        for j in range(3):
            S.dma("sp", "cA1", cw[:, :, j], c.conv_w[j, :].rearrange("(f p) -> p f", p=128), allow_slow_non_contiguous=True)
        S.dma("sp", "cA2", gnc[:], c.gn_conv[0, :].rearrange("(f p) -> p f", p=128), allow_slow_non_contiguous=True)

        tiles = []
        for (s0, L) in c.seqs:
            n = L // 512
            for i in range(n):
                tiles.append((s0 + i * 512, i == 0, i == n - 1))

        def load_x(i):
            t0 = tiles[i][0]
            S.dma("sp", "xld%d" % (i % 2), xt[i % 2][:], c.x[t0:t0 + 512, :].rearrange("(g p) d -> p g d", p=128))

        def fA1(i, g):
            s = i % 2
            if g % 2 == 0:
                S.act(junk[:], xt[s][:, g, :], AF.Square, accum_out=ss[s][:, g:g + 1], wk=["junkA", "ssA%d_%d" % (s, g)])
            else:
                S.op("dve", lambda e: e.scalar_tensor_tensor(out=junk2[:], in0=xt[s][:, g, :], scalar=1.0, in1=xt[s][:, g, :],
                                                               op0=ALU.mult, op1=ALU.mult, accum_out=ss[s][:, g:g + 1]),
                     reads=[_nm(xt[s][:])], writes=["junkA2", "ssA%d_%d" % (s, g)])
            rsqrt_act(S, rstd[s][:, g:g + 1], ss[s][:, g:g + 1], 1.0 / D, var4[s][:, g:g + 1], c.epsc[:],
                      rk=["ssA%d_%d" % (s, g)], tk=["var4A%d_%d" % (s, g)], wk=["rstdA%d_%d" % (s, g)])
            S.stt(xn[g][:], xt[s][:, g, :], rstd[s][:, g:g + 1], gpre[:], ALU.mult, ALU.mult,
                  rk=[_nm(xt[s][:]), "rstdA%d_%d" % (s, g), "gpre"])

        def fA2(i, g):
            pt = ps_t[g % 2]
            for kt in range(8):
                S.tr(pt[:, kt * 128:(kt + 1) * 128], xn[g][:, kt * 128:(kt + 1) * 128], c.identb[:], inc=(kt == 7))

        def fA3(i, g):
            s = i % 2
            S.copy("act", hT[s][:, :, g * 128:(g + 1) * 128], ps_t[g % 2][:].rearrange("p (k t) -> p k t", k=8),
                   wk=["hT%d_%d" % (s, g)])

        def front_items(i):
            it = [(lambda g=g: fA1(i, g)) for g in range(4)]
            for g in range(4):
                it.append(lambda g=g: fA2(i, g))
                it.append(lambda g=g: fA3(i, g))
            return it

        def c1(i, f):
            S.act(acc[f][:], pbuf[i % 2][:, f, 1:513], AF.Identity, scale=cw[:, f, 1:2])

        def c2(i, f):
            s = i % 2
            S.stt(acc[f][:], pbuf[s][:, f, 0:512], cw[:, f, 0:1], acc[f][:], ALU.mult, ALU.add)
            S.stt(acc[f][:], pbuf[s][:, f, 2:514], cw[:, f, 2:3], acc[f][:], ALU.mult, ALU.add)
            S.tt("dve", yc[:, f, :], acc[f][:], zbb[s][:, f, :], ALU.mult, wk=["ycA_%d" % f])

        def c3(i, f):
            S.act(sq[f][:], yc[:, f, :], AF.Square, rk=["ycA_%d" % f])
            S.mm(ps_s[:], c.onesb[:], sq[f][:], start=(f == 0), stop=(f == 3), inc=True)

        def c4(i):
            t0 = tiles[i][0]
            rsqrt_act(S, rst[:], ps_s[:], 1.0 / DC, var[:], c.epsc[:])
            yn = ycn[i % 2]
            for ff in range(4):
                S.stt(yn[:, ff, :], yc[:, ff, :], gnc[:, ff:ff + 1], rst[:], ALU.mult, ALU.mult,
                      rk=["ycA_%d" % ff, "gnc", "rstA"])
            S.dma("pool", "ycnst%d" % (i % 2), c.ycat_d[0:4, :, t0:t0 + 512].rearrange("f p t -> p f t"), yn[:],
                  writes=[("ycat_d", 0, t0)])

        def conv_items(i):
            order = [("c1", 0), ("c1", 1), ("c2", 0), ("c1", 2), ("c2", 1), ("c3", 0), ("c1", 3), ("c2", 2), ("c3", 1),
                     ("c2", 3), ("c3", 2), ("c3", 3)]
            fn = {"c1": c1, "c2": c2, "c3": c3}
            it = [(lambda k=k, f=f: fn[k](i, f)) for (k, f) in order]
            it.append(lambda: c4(i))
            return it

        load_x(0)
        for w_ in front_items(0):
            w_()
        bg = []
        for i, (t0, first, last) in enumerate(tiles):
            s = i % 2
            if i + 1 < len(tiles):
                load_x(i + 1)
                bg.extend(front_items(i + 1))
            def evac(m, s=s, first=first, last=last, i=i):
                pz = ps_z[m % 5]
                f = m % 4
                if m < 4:
                    S.copy("dve", zbb[s][:, f, :], pz[:])
                elif m < 8:
                    S.copy("dve", zcs[f][:], pz[:])
                elif m < 12:
                    S.tt("dve", pbuf[s][:, f, 1:513], pz[:], zcs[f][:], ALU.mult)
                else:
                    S.copy("act", zus[s][:, f, :].rearrange("p (j c) -> p c j", j=8), pz[:].rearrange("p (c j) -> p c j", j=8))
                if m == 11:
                    if first:
                        S.memset("pool", pbuf[s][:, :, 0:1], 0.0)
                    else:
                        S.copy("pool", pbuf[s][:, :, 0:1], pbuf[1 - s][:, :, 512:513])
                        S.copy("pool", pbuf[1 - s][:, :, 513:514], pbuf[s][:, :, 1:2])
                        bg.extend(conv_items(i - 1))
                    if last:
                        S.memset("pool", pbuf[s][:, :, 513:514], 0.0)

            for m in range(16):
                pz = ps_z[m % 5]
                for kt in range(8):
                    S.mm(pz[:], w_in_sb[:, kt, m * 128:(m + 1) * 128], hT[s][:, kt, :], start=(kt == 0), stop=(kt == 7),
                         rk=["w_in_sb_%d" % (m // 4)] + ["hT%d_%d" % (s, g) for g in range(4)])
                if m >= 1:
                    evac(m - 1)
                npop = -(-len(bg) // (16 - m)) if m >= 12 else min(len(bg), 2)
                for _ in range(npop):
                    bg.pop(0)()
            evac(15)
            S.dma("pool", "zust%d" % s, c.zuT_d[:, :, t0:t0 + 512].rearrange("q p t -> p q t"), zus[s][:],
                  writes=[("zuT_d", t0)])
            while bg:
                bg.pop(0)()
            if last:
                bg.extend(conv_items(i))
        while bg:
            bg.pop(0)()
        S.barrier()


def cmul(S, eng, o_re, o_im, x_re, x_im, y_re, y_im, t1, t2, conj_x=False):
    S.tt(eng, t1, x_re, y_re, ALU.mult)
    S.tt(eng, t2, x_im, y_im, ALU.mult)
    S.tt(eng, o_re, t1, t2, ALU.add if conj_x else ALU.subtract)
    S.tt(eng, t1, x_re, y_im, ALU.mult)
    S.tt(eng, t2, x_im, y_re, ALU.mult)
    S.tt(eng, o_im, t1, t2, ALU.subtract if conj_x else ALU.add)


def phase_B1(c):
    nc, S = c.nc, c.S
    with contextlib.ExitStack() as es:
        E = es.enter_context
        sb = lambda n, sh, dt: E(nc.sbuf_tensor(n, sh, dt))
        U8r = sb("U8r", [128, 32], F32)
        U8i = sb("U8i", [128, 32], F32)
        R8 = sb("R8", [128, 32], F32)
        with contextlib.ExitStack() as es1:
            WX = es1.enter_context(nc.sbuf_tensor("WX", [128, 4, 2, 2, 8, 128], BF16))
            WY = es1.enter_context(nc.sbuf_tensor("WY", [128, 32, 2, 9, 32], BF16))
            INTRA = es1.enter_context(nc.sbuf_tensor("INTRA", [128, 4, 64, 32], BF16))
            s5_prep(c, WX, WY, INTRA, U8r, U8i, R8)
            S.dma("sp", "spl0", c.WX_d.rearrange("q p x -> p q x"), WX[:].rearrange("p q a b e n -> p q (a b e n)"), writes=["WX_d"])
            S.dma_split("sp", "spl1", c.WY_d.rearrange("u p x -> p u x"), WY[:].rearrange("p u a k n -> p u (a k n)"), 1, 4,
                        writes=["WY_d"])
            S.dma("sp", "spl2", c.IN_d.rearrange("q p x -> p q x"), INTRA[:].rearrange("p q j n -> p q (j n)"), writes=["IN_d"])
            S.barrier()
        s5_main(c, U8r, U8i, R8)


def s5_prep(c, WX, WY, INTRA, U8r, U8i, R8):
    nc, S = c.nc, c.S
    with contextlib.ExitStack() as es:
        E = es.enter_context
        sb = lambda n, sh, dt=F32: E(nc.sbuf_tensor(n, sh, dt))
        lamre, lamim, lst, dsk, m32 = c.lamre, c.lamim, c.lst, c.dsk, c.m32
        Bre, Bim = sb("Bre", [128, 32, 16]), sb("Bim", [128, 32, 16])
        S.dma_split("sp", "p3", Bre[:].rearrange("p (d r) h -> p d r h", d=2), c.b_re.rearrange("d (r q) p h -> (q p) d r h", q=2), 0, 4)
        S.dma_split("sp", "p4", Bim[:].rearrange("p (d r) h -> p d r h", d=2), c.b_im.rearrange("d (r q) p h -> (q p) d r h", q=2), 0, 4)
        Cre, Cim = sb("Cre", [128, 32, 16]), sb("Cim", [128, 32, 16])
        with contextlib.ExitStack() as es2:
            Cnr = es2.enter_context(nc.sbuf_tensor("Cnr", [16, 64, 64], F32))
            Cni = es2.enter_context(nc.sbuf_tensor("Cni", [16, 64, 64], F32))
            psC = [es2.enter_context(nc.psum_tensor("psC%d" % i, [128, 32, 16], F32)) for i in range(2)]
            S.dma("sp", "p5", Cnr[:], c.c_re.rearrange("d g h p -> h (d g) p"))
            S.dma("sp", "p6", Cni[:], c.c_im.rearrange("d g h p -> h (d g) p"))
            for ri, (Cn, Cst) in enumerate(((Cnr, Cre), (Cni, Cim))):
                for u in range(32):
                    d, pr = divmod(u, 16)
                    a = d * 32 + 2 * pr
                    S.tr(psC[ri][:, u, :], Cn[:, a:a + 2, :].rearrange("h g p -> h (g p)"), c.identf[0:16, 0:16], inc=(u == 31))
                S.copy("dve", Cst[:], psC[ri][:])
            S.barrier()

        dt, lr, th = sb("dtp", [128, 32]), sb("lr", [128, 32]), sb("th", [128, 32])
        mag, kf, thr = sb("mag", [128, 32]), sb("kf", [128, 32]), sb("thr", [128, 32])
        sh, ch, t1, t2 = sb("sh", [128, 32]), sb("ch", [128, 32]), sb("t1p", [128, 32]), sb("t2p", [128, 32])
        cs, sn = sb("cs", [128, 32]), sb("sn", [128, 32])
        hpi = sb("hpi", [128, 1])
        S.memset("dve", hpi[:], math.pi / 2)
        S.act(dt[:], lst[:], AF.Exp)
        S.tt("dve", lr[:], lamre[:], dt[:], ALU.mult)
        S.tt("dve", th[:], lamim[:], dt[:], ALU.mult)
        S.act(mag[:], lr[:], AF.Exp)
        S.act(R8[:], lr[:], AF.Exp, scale=8.0)
        S.ts("dve", kf[:], th[:], 1.0 / TWO_PI, MAGIC, ALU.mult, ALU.add)
        S.ts("dve", kf[:], kf[:], -MAGIC, -TWO_PI, ALU.add, ALU.mult)
        S.tt("dve", thr[:], th[:], kf[:], ALU.add)
        S.act(sh[:], thr[:], AF.Sin, scale=0.5)
        S.act(ch[:], thr[:], AF.Sin, scale=0.5, bias=hpi[:])
        S.tt("dve", t1[:], ch[:], ch[:], ALU.mult)
        S.tt("dve", t2[:], sh[:], sh[:], ALU.mult)
        S.tt("dve", cs[:], t1[:], t2[:], ALU.subtract)
        S.tt("dve", t1[:], sh[:], ch[:], ALU.mult)
        S.ts("dve", sn[:], t1[:], 2.0, None, ALU.mult)
        UPr, UPi = sb("UPr", [128, 32, 9]), sb("UPi", [128, 32, 9])
        APr, APi = sb("APr", [128, 32, 9]), sb("APi", [128, 32, 9])
        T1, T2 = sb("T1p", [128, 32, 4]), sb("T2p", [128, 32, 4])
        S.memset("dve", UPr[:, :, 0:1], 1.0)
        S.memset("dve", UPi[:, :, 0:1], 0.0)
        S.copy("dve", UPr[:, :, 1], cs[:])
        S.copy("dve", UPi[:, :, 1], sn[:])
        cmul(S, "dve", UPr[:, :, 2], UPi[:, :, 2], UPr[:, :, 1], UPi[:, :, 1], UPr[:, :, 1], UPi[:, :, 1], T1[:, :, 0], T2[:, :, 0])
        bc = lambda ap, n: ap.to_broadcast([128, 32, n])
        cmul(S, "dve", UPr[:, :, 3:5], UPi[:, :, 3:5], UPr[:, :, 1:3], UPi[:, :, 1:3], bc(UPr[:, :, 2:3], 2), bc(UPi[:, :, 2:3], 2),
             T1[:, :, 0:2], T2[:, :, 0:2])
        cmul(S, "dve", UPr[:, :, 5:9], UPi[:, :, 5:9], UPr[:, :, 1:5], UPi[:, :, 1:5], bc(UPr[:, :, 4:5], 4), bc(UPi[:, :, 4:5], 4),
             T1[:], T2[:])
        S.copy("dve", U8r[:], UPr[:, :, 8])
        S.copy("dve", U8i[:], UPi[:, :, 8])
        MG = sb("MG", [128, 32, 9])
        S.memset("dve", MG[:, :, 0:1], 1.0)
        for k in range(1, 9):
            S.act(MG[:, :, k], lr[:], AF.Exp, scale=float(k))
        S.tt("dve", APr[:], UPr[:], MG[:], ALU.mult)
        S.tt("dve", APi[:], UPi[:], MG[:], ALU.mult)
        nr, den, inv = sb("nr", [128, 32]), sb("den", [128, 32]), sb("inv", [128, 32])
        qr, qi = sb("qr", [128, 32]), sb("qi", [128, 32])
        S.ts("dve", nr[:], APr[:, :, 1], -1.0, None, ALU.add)
        S.tt("dve", t1[:], lamre[:], lamre[:], ALU.mult)
        S.tt("dve", t2[:], lamim[:], lamim[:], ALU.mult)
        S.tt("dve", den[:], t1[:], t2[:], ALU.add)
        S.op("dve", lambda e: e.reciprocal(out=inv[:], in_=den[:]), reads=["den"], writes=["inv"])
        S.tt("dve", t1[:], nr[:], lamre[:], ALU.mult)
        S.tt("dve", t2[:], APi[:, :, 1], lamim[:], ALU.mult)
        S.tt("dve", t1[:], t1[:], t2[:], ALU.add)
        S.tt("dve", qr[:], t1[:], inv[:], ALU.mult)
        S.tt("dve", t1[:], APi[:, :, 1], lamre[:], ALU.mult)
        S.tt("dve", t2[:], nr[:], lamim[:], ALU.mult)
        S.tt("dve", t1[:], t1[:], t2[:], ALU.subtract)
        S.tt("dve", qi[:], t1[:], inv[:], ALU.mult)
        Bbr, Bbi = sb("Bbr", [128, 32, 16]), sb("Bbi", [128, 32, 16])
        V1, V2 = sb("V1", [128, 32, 16]), sb("V2", [128, 32, 16])
        b16 = lambda ap: ap.unsqueeze(2).to_broadcast([128, 32, 16])
        cmul(S, "dve", Bbr[:], Bbi[:], b16(qr[:]), b16(qi[:]), Bre[:], Bim[:], V1[:], V2[:])
        Gbd = sb("Gbd", [128, 32, 2, 8, 32], BF16)
        Bbd = sb("Bbd", [128, 32, 2, 32], BF16)
        S.memset("pool", Gbd[:], 0.0)
        S.memset("pool", Bbd[:], 0.0)
        S.memset("pool", WY[:], 0.0)
        W1, W2 = sb("W1", [128, 32, 9, 16]), sb("W2", [128, 32, 9, 16])

        def bk(ap, n):
            return ap.unsqueeze(3).to_broadcast([128, 32, n, 16])

        def bh(ap, n):
            return ap.unsqueeze(2).to_broadcast([128, 32, n, 16])

        def halves(dst_fn, src, n):
            for g2 in range(2):
                P = slice(64 * g2, 64 * g2 + 64)
                S.copy("act", dst_fn(P, slice(16 * g2, 16 * g2 + 16)), src[P])

        S.tt("dve", W2[:, :, 0:8, :], bk(APi[:, :, 0:8], 8), bh(Bbi[:], 8), ALU.mult)
        S.tt("dve", W1[:, :, 0:8, :], bk(APr[:, :, 0:8], 8), bh(Bbr[:], 8), ALU.mult)
        S.tt("dve", W1[:, :, 0:8, :], W1[:, :, 0:8, :], W2[:, :, 0:8, :], ALU.subtract)
        halves(lambda P, Cc: Gbd[P, :, 0, :, Cc], W1[:, :, 0:8, :], 8)
        S.tt("dve", W2[:, :, 0:8, :], bk(APi[:, :, 0:8], 8), bh(Bbr[:], 8), ALU.mult)
        S.tt("dve", W1[:, :, 0:8, :], bk(APr[:, :, 0:8], 8), bh(Bbi[:], 8), ALU.mult)
        S.tt("dve", W1[:, :, 0:8, :], W1[:, :, 0:8, :], W2[:, :, 0:8, :], ALU.add)
        halves(lambda P, Cc: Gbd[P, :, 1, :, Cc], W1[:, :, 0:8, :], 8)
        halves(lambda P, Cc: Bbd[P, :, 0, Cc], Bbr[:], 1)
        halves(lambda P, Cc: Bbd[P, :, 1, Cc], Bbi[:], 1)
        S.tt("dve", W2[:], bk(APi[:], 9), bh(Cim[:], 9), ALU.mult)
        S.tt("dve", W1[:], bk(APr[:], 9), bh(Cre[:], 9), ALU.mult)
        S.tt("dve", W1[:], W1[:], W2[:], ALU.subtract)
        halves(lambda P, Cc: WY[P, :, 0, :, Cc], W1[:], 9)
        S.tt("dve", W2[:], bk(APr[:], 9), bh(Cim[:], 9), ALU.mult)
        S.tt("dve", W1[:], bk(APi[:], 9), bh(Cre[:], 9), ALU.mult)
        S.tt("dve", W1[:], W1[:], W2[:], ALU.add)
        S.ts("dve", W1[:], W1[:], -1.0, None, ALU.mult)
        halves(lambda P, Cc: WY[P, :, 1, :, Cc], W1[:], 9)

        Kps = [E(nc.psum_tensor("Kps%d" % d, [128, 4, 8, 32], F32)) for d in range(2)]
        for d in range(2):
            for pr in range(16):
                u = d * 16 + pr
                q, pq = divmod(pr, 4)
                for ri in range(2):
                    S.mm(Kps[d][32 * pq:32 * pq + 32, q, :, :], Bbd[:, u, ri, :], WY[:, u, ri, 0:8, :],
                         start=(ri == 0), stop=(ri == 1), inc=(ri == 1 and pr == 15), tp=(0, 32 * pq))
        Kf, Kb = sb("Kf", [128, 4, 8, 32]), sb("Kb", [128, 4, 8, 32])
        S.copy("dve", Kf[:], Kps[0][:])
        S.copy("dve", Kb[:], Kps[1][:])
        Dd = sb("Dd", [128, 4, 32])
        S.tt("dve", Dd[:], m32[:].unsqueeze(1).to_broadcast([128, 4, 32]), dsk[:].unsqueeze(2).to_broadcast([128, 4, 32]), ALU.mult)
        S.tt("dve", Dd[:], Dd[:], Kf[:, :, 0, :], ALU.add)
        S.tt("dve", Dd[:], Dd[:], Kb[:, :, 0, :], ALU.add)
        S.copy("dve", INTRA[:, :, 0:64:9, :], Dd[:].unsqueeze(2).to_broadcast([128, 4, 8, 32]))
        for k in range(1, 8):
            n = 8 - k
            S.copy("dve", INTRA[:, :, k:k + 9 * (n - 1) + 1:9, :], Kf[:, :, k:k + 1, :].to_broadcast([128, 4, n, 32]))
            S.copy("dve", INTRA[:, :, 8 * k:8 * k + 9 * (n - 1) + 1:9, :], Kb[:, :, k:k + 1, :].to_broadcast([128, 4, n, 32]))

        psT = [E(nc.psum_tensor("psWX%d" % i, [128, 8, 128], BF16)) for i in range(2)]
        n = 0
        for q in range(4):
            for d in range(2):
                for ri in range(2):
                    pt = psT[n % 2]
                    n += 1
                    for e in range(8):
                        for pq in range(4):
                            u = d * 16 + 4 * q + pq
                            S.tr(pt[32 * pq:32 * pq + 32, e, :], Gbd[:, u, ri, e, :], c.identb[:],
                                 inc=(e == 7 and pq == 3), tp=(0, 32 * pq))
                    S.copy("act", WX[:, q, d, ri, :, :], pt[:])
        S.barrier()


def s5_main(c, U8r, U8i, R8):
    nc, S = c.nc, c.S
    Lmax = max(c.seq_lens)
    NCm = Lmax // 8
    with contextlib.ExitStack() as es:
        E = es.enter_context
        sb = lambda n, sh, dt=F32: E(nc.sbuf_tensor(n, sh, dt))
        WXq = sb("WXq", [128, 2, 2, 8, 128], BF16)
        WYq = sb("WYq", [128, 8, 2, 9, 32], BF16)
        INq = sb("INq", [128, 64, 32], BF16)
        Ec, Es = sb("Ec", [128, 8, NCm]), sb("Es", [128, 8, NCm])
        pwr, pwi = sb("pwr", [128, 8]), sb("pwi", [128, 8])
        pt1, pt2, pt3 = sb("pt1", [128, 8]), sb("pt2", [128, 8]), sb("pt3", [128, 8])
        zum = [[sb("zum%d_%d" % (r, i), [128, Lmax], BF16) for i in range(4)] for r in range(2)]
        Sbf = [sb("Sbf%d" % r, [128, 8, 2, NCm + 1], BF16) for r in range(2)]
        tmps = [[sb("tm%s%d" % (nm, i), [128, NCm]) for nm in "ABCDEF"] for i in range(2)]
        yg = sb("ygB", [128, Lmax])
        ET1 = yg[:, 0:Lmax // 2].rearrange("p (u k) -> p u k", u=8)
        ET2 = yg[:, Lmax // 2:Lmax].rearrange("p (u k) -> p u k", u=8)
        Xps = [[E(nc.psum_tensor("Xps%d_%d" % (i, ri), [128, 512], F32)) for ri in range(2)] for i in range(2)]
        Yps = [E(nc.psum_tensor("Yps%d" % i, [128, 512], F32)) for i in range(2)]
        for r in range(2):
            S.memset("pool" if r == 0 else "dve", Sbf[r][:], 0.0)
            for pq in range(4):
                if r == 0:
                    S.memset("pool", zum[r][pq][:], 0.0)
                else:
                    S.op("act", lambda e, t_=zum[r][pq]: e.memzero(t_[:]), reads=[], writes=[_nm(zum[r][pq][:])])
        nseq = len(c.seqs)

        def scan_prologue(q):
            S.dma("sp", "wq0", WXq[:].rearrange("p a b e n -> p (a b e n)"), c.WX_d[q], reads=["WX_d"])
            for d in range(2):
                usl = slice(d * 16 + 4 * q, d * 16 + 4 * q + 4)
                S.copy("dve", pwr[:, 4 * d:4 * d + 4], U8r[:, usl])
                S.copy("dve", pwi[:, 4 * d:4 * d + 4], U8i[:, usl])
            S.memset("dve", Ec[:, :, 0:1], 1.0)
            S.memset("dve", Es[:, :, 0:1], 0.0)
            seg = 1
            while seg < NCm:
                bcs = lambda ap: ap.unsqueeze(2).to_broadcast([128, 8, seg])
                cmul(S, "dve", Ec[:, :, seg:2 * seg], Es[:, :, seg:2 * seg], Ec[:, :, 0:seg], Es[:, :, 0:seg],
                     bcs(pwr[:]), bcs(pwi[:]), ET1[:, :, 0:seg], ET2[:, :, 0:seg])
                seg *= 2
                if seg < NCm:
                    S.tt("dve", pt1[:], pwr[:], pwr[:], ALU.mult)
                    S.tt("dve", pt2[:], pwi[:], pwi[:], ALU.mult)
                    S.tt("dve", pt3[:], pwr[:], pwi[:], ALU.mult)
                    S.tt("dve", pwr[:], pt1[:], pt2[:], ALU.subtract)
                    S.ts("dve", pwi[:], pt3[:], 2.0, None, ALU.mult)

        def out_prologue(q):
            for d in range(2):
                u0 = d * 16 + 4 * q
                S.dma("sp", "wq1", WYq[:, 4 * d:4 * d + 4].rearrange("p u a k n -> p u (a k n)"),
                      c.WY_d[u0:u0 + 4].rearrange("u p x -> p u x"), reads=["WY_d"])
            S.dma("sp", "wq2", INq[:].rearrange("p j n -> p (j n)"), c.IN_d[q], reads=["IN_d"])

        if True:
            NIT = 4 * nseq

            def step(t):
                g, h = t, t - 1
                gv, hv = g < NIT, h >= 0
                if gv:
                    q, k = divmod(g, nseq)
                    if k == 0:
                        scan_prologue(q)
                    s0, L = c.seqs[k]
                    n_c = L // 8
                    r = g % 2
                    for pq in range(4):
                        P = slice(32 * pq, 32 * pq + 32)
                        S.dma("sp" if pq % 2 == 0 else "act", "zuld%d_%d" % (r, pq), zum[r][pq][P, 0:L], c.zuT_d[q, P, s0:s0 + L],
                              reads=[("zuT_d", s0 + 512 * i) for i in range(L // 512)])
                    zdi = [zum[r][pq][:, 0:L].rearrange("p (t j c) -> p t j c", j=8, c=64) for pq in range(4)]
                if hv:
                    qh, kh = divmod(h, nseq)
                    if kh == 0:
                        out_prologue(qh)
                    s0h, Lh = c.seqs[kh]
                    n_ch = Lh // 8
                    rh = h % 2
                    zdh = [zum[rh][pq][:, 0:Lh].rearrange("p (t j c) -> p t j c", j=8, c=64) for pq in range(4)]
                for pair in range(4):
                    if gv:
                        ops = []
                        for uu in (2 * pair, 2 * pair + 1):
                            d, pq = divmod(uu, 4)
                            u = d * 16 + 4 * q + pq
                            xp = Xps[uu % 2]
                            for ri in range(2):
                                for j in range(8):
                                    e = 7 - j if d == 0 else j
                                    S.mm(xp[ri][:, 0:n_c], WXq[:, d, ri, e, :], zdi[pq][:, :, j, :], start=(j == 0), stop=(j == 7))
                            ec, esn = Ec[:, uu, 0:n_c], Es[:, uu, 0:n_c]
                            if d == 0:
                                xr, xi = xp[0][:, 0:n_c], xp[1][:, 0:n_c]
                                o_re, o_im = Sbf[r][:, uu, 0, 1:n_c + 1], Sbf[r][:, uu, 1, 1:n_c + 1]
                            else:
                                xr, xi = xp[0][:, 0:n_c][:, ::-1], xp[1][:, 0:n_c][:, ::-1]
                                o_re, o_im = Sbf[r][:, uu, 0, 0:n_c][:, ::-1], Sbf[r][:, uu, 1, 0:n_c][:, ::-1]
                            a, b, cc, dd, ee, ff = [tt_[:, 0:n_c] for tt_ in tmps[uu % 2]]
                            r8 = R8[:, u:u + 1].to_broadcast([128, n_c])
                            chain = [
                                lambda a=a, xr=xr, ec=ec: S.tt("dve", a, xr, ec, ALU.mult),
                                lambda b=b, xi=xi, esn=esn: S.tt("dve", b, xi, esn, ALU.mult),
                                lambda a=a, b=b: S.tt("dve", a, a, b, ALU.add),
                                lambda cc=cc, xi=xi, ec=ec: S.tt("dve", cc, xi, ec, ALU.mult),
                                lambda dd=dd, xr=xr, esn=esn: S.tt("dve", dd, xr, esn, ALU.mult),
                                lambda cc=cc, dd=dd: S.tt("dve", cc, cc, dd, ALU.subtract),
                                lambda a=a, b=b, r8=r8: S.scan(b, r8, a),
                                lambda cc=cc, dd=dd, r8=r8: S.scan(dd, r8, cc),
                                lambda a=a, b=b, ec=ec: S.tt("dve", a, b, ec, ALU.mult),
                                lambda ee=ee, dd=dd, esn=esn: S.tt("pool", ee, dd, esn, ALU.mult),
                                lambda cc=cc, dd=dd, ec=ec: S.tt("dve", cc, dd, ec, ALU.mult),
                                lambda ff=ff, b=b, esn=esn: S.tt("pool", ff, b, esn, ALU.mult),
                                lambda o_re=o_re, a=a, ee=ee: S.tt("pool", o_re, a, ee, ALU.subtract),
                                lambda o_im=o_im, cc=cc, ff=ff: S.tt("pool", o_im, cc, ff, ALU.add),
                            ]
                            if d == 1:
                                chain.append(lambda uu=uu: S.memset("pool", Sbf[r][:, uu, :, n_c:n_c + 1], 0.0))
                            ops.append(chain)
                        for i_ in range(max(len(ch) for ch in ops)):
                            for ch in ops:
                                if i_ < len(ch):
                                    ch[i_]()
                    if hv:
                        for j in (2 * pair, 2 * pair + 1):
                            yp = Yps[j % 2]
                            for pq in range(4):
                                P = slice(32 * pq, 32 * pq + 32)
                                first = True
                                for d in range(2):
                                    uu = d * 4 + pq
                                    kk = j + 1 if d == 0 else 8 - j
                                    for ri in range(2):
                                        rhs = Sbf[rh][:, uu, ri, 0:n_ch] if d == 0 else Sbf[rh][:, uu, ri, 1:n_ch + 1]
                                        S.mm(yp[P, 0:n_ch], WYq[:, uu, ri, kk, :], rhs, start=first, stop=False, inc=False, tp=(0, 32 * pq))
                                        first = False
                            for pq in range(4):
                                P = slice(32 * pq, 32 * pq + 32)
                                for jp in range(8):
                                    S.mm(yp[P, 0:n_ch], INq[:, jp * 8 + j, :], zdh[pq][:, :, jp, :], start=False, stop=(jp == 7),
                                         inc=(jp == 7 and pq == 3), tp=(0, 32 * pq))
                            S.act(yg[:, j:Lh:8], yp[:, 0:n_ch], AF.Gelu_apprx_tanh)
                if hv:
                    S.dma("pool", "ygst", c.yg_d[qh, :, s0h:s0h + Lh], yg[:, 0:Lh], writes=[("yg_d", qh, s0h)])

            for t in range(NIT + 1):
                step(t)
        S.barrier()


def phase_B2(c):
    nc, S = c.nc, c.S
    with contextlib.ExitStack() as es:
        E = es.enter_context
        sb = lambda n, sh, dt=F32: E(nc.sbuf_tensor(n, sh, dt))
        wg = sb("wg", [128, 4, DC], BF16)
        bg, gns = sb("bg", [128, 4]), sb("gns", [128, 4])
        ygf = [sb("ygf%d" % i, [128, 4, 512]) for i in range(3)]
        ygb = [sb("ygb%d" % i, [128, 4, 512], BF16) for i in range(2)]
        sg = [sb("sg%d" % i, [128, 512]) for i in range(4)]
        y2 = [sb("y2_%d" % i, [128, 4, 512]) for i in range(2)]
        sq = [sb("sqB%d" % i, [128, 512], BF16) for i in range(4)]
        var = [sb("varB%d" % i, [128, 512]) for i in range(2)]
        rst = [sb("rstB%d" % i, [128, 512]) for i in range(2)]
        y2n = [sb("y2n%d" % i, [128, 4, 512], BF16) for i in range(2)]
        ps_g = [E(nc.psum_tensor("psB_g%d" % i, [128, 512], F32)) for i in range(3)]
        ps_s = [E(nc.psum_tensor("psB_s%d" % i, [128, 512], F32)) for i in range(2)]
        for kt in range(4):
            S.dma("pool", "wgld", wg[:, kt, :], c.w_glu[kt * 128:(kt + 1) * 128, :])
        S.dma("sp", "cB0", bg[:], c.b_glu[0, :].rearrange("(f p) -> p f", p=128), allow_slow_non_contiguous=True)
        S.dma("sp", "cB1", gns[:], c.gn_ssm[0, :].rearrange("(f p) -> p f", p=128), allow_slow_non_contiguous=True)
        tiles = []
        for (s0_, L) in c.seqs:
            for i in range(L // 512):
                tiles.append((s0_ + i * 512, s0_))
        NTI = len(tiles)

        def load(i):
            if i >= NTI:
                return
            t0, s0_ = tiles[i]
            S.dma("sp", "ygld%d" % (i % 3), ygf[i % 3][:], c.yg_d[:, :, t0:t0 + 512].rearrange("q p t -> p q t"),
                  reads=[("yg_d", q, s0_) for q in range(4)])

        wst = [sb("wupst%d" % i, [128, 2 * DFF], BF16) for i in range(2)]

        def st0(i):
            if i < 8:
                S.dma("pool", "wupst%d" % (i % 2), wst[i % 2][:], c.w_up[i * 128:(i + 1) * 128, :])
            for kt in range(4):
                S.copy("act" if kt % 2 else "dve", ygb[i % 2][:, kt, :], ygf[i % 3][:, kt, :], wk=["ygb%d_%d" % (i % 2, kt)])

        def st1(i):
            if i < 8:
                S.dma_split("pool", "wupwr%d" % (i % 2), c.wup_d[:, :, i, :].rearrange("m p c -> p m c"),
                            wst[i % 2][:].rearrange("p (m c) -> p m c", c=128), 1, 4, writes=["wup_d"])
            for m in range(4):
                pg = ps_g[m % 3]
                for kt in range(4):
                    S.mm(pg[:], wg[:, kt, m * 128:(m + 1) * 128], ygb[i % 2][:, kt, :], start=(kt == 0), stop=(kt == 3),
                         rk=["wg", "ygb%d_%d" % (i % 2, kt)])
                S.act(sg[m][:], pg[:], AF.Sigmoid, bias=bg[:, m:m + 1])
            for m in range(4):
                S.tt("dve", y2[i % 2][:, m, :], ygf[i % 3][:, m, :], sg[m][:], ALU.mult, wk=["y2_%d_%d" % (i % 2, m)])
            for m in range(4):
                S.act(sq[m][:], y2[i % 2][:, m, :], AF.Square, rk=["y2_%d_%d" % (i % 2, m)])
                S.mm(ps_s[i % 2][:], c.onesb[:], sq[m][:], start=(m == 0), stop=(m == 3), inc=True)
            load(i + 3)

        def st2(i):
            t0 = tiles[i][0]
            rsqrt_act(S, rst[i % 2][:], ps_s[i % 2][:], 1.0 / DC, var[i % 2][:], c.epsc[:])
            for m in range(4):
                S.stt(y2n[i % 2][:, m, :], y2[i % 2][:, m, :], gns[:, m:m + 1], rst[i % 2][:], ALU.mult, ALU.mult,
                      rk=["y2_%d_%d" % (i % 2, m), "gns", _nm(rst[i % 2][:])])
            S.dma("pool", "y2nst%d" % (i % 2), c.ycat_d[4:8, :, t0:t0 + 512].rearrange("f p t -> p f t"), y2n[i % 2][:],
                  writes=[("ycat_d", 1, t0)])

        for i in range(3):
            load(i)
        pipeline(NTI, [st0, st1, st2])
        for kt in range(NTI, 8):
            S.dma("pool", "wupst%d" % (kt % 2), wst[kt % 2][:], c.w_up[kt * 128:(kt + 1) * 128, :])
            S.dma_split("pool", "wupwr%d" % (kt % 2), c.wup_d[:, :, kt, :].rearrange("m p c -> p m c"),
                        wst[kt % 2][:].rearrange("p (m c) -> p m c", c=128), 1, 4, writes=["wup_d"])
        S.barrier()


def phase_C1(c):
    nc, S = c.nc, c.S
    with contextlib.ExitStack() as es:
        E = es.enter_context
        sb = lambda n, sh, dt=F32: E(nc.sbuf_tensor(n, sh, dt))
        wo = c.wo
        gpost, gffn = sb("gpost", [128, D]), sb("gffn", [128, D])
        yc = [sb("ycC%d" % i, [128, 8, 512], BF16) for i in range(2)]
        xt = [sb("xtC%d" % i, [128, 4, D]) for i in range(2)]
        junk = sb("junkC", [128, D], BF16)
        st = [sb("stC%d" % i, [128, 8]) for i in range(3)]
        tmp = [sb("tmpC%d" % i, [128, D]) for i in range(2)]
        x1 = [sb("x1C%d" % i, [128, D]) for i in range(3)]
        h2 = [sb("h2C%d" % i, [128, D], BF16) for i in range(3)]
        h2T = [sb("h2TC%d" % i, [128, 8, 512], BF16) for i in range(2)]
        ps_o = [E(nc.psum_tensor("psC_o%d" % i, [128, D], F32)) for i in range(3)]
        ps_t = [E(nc.psum_tensor("psC_t%d" % i, [128, D], BF16)) for i in range(2)]
        S.dma("sp", "cC0", gpost[:], c.post_mix_g.to_broadcast([128, D]))
        S.dma("sp", "cC1", gffn[:], c.pre_ffn_g.to_broadcast([128, D]))
        tiles = []
        for (s0_, L) in c.seqs:
            for i in range(L // 512):
                tiles.append(s0_ + i * 512)
        NTI = len(tiles)

        def load(ti):
            if ti >= NTI:
                return
            t0 = tiles[ti]
            S.dma("sp", "ycld%d" % (ti % 2), yc[ti % 2][:], c.ycat_d[:, :, t0:t0 + 512].rearrange("f p t -> p f t"),
                  reads=[("ycat_d", 0, t0), ("ycat_d", 1, t0)])
            S.dma("sp", "xldC%d" % (ti % 2), xt[ti % 2][:], c.x[t0:t0 + 512, :].rearrange("(g p) d -> p g d", p=128))

        def s0(i):
            ti, g = divmod(i, 4)
            s = ti % 2
            if g == 1:
                load(ti + 1)
            po = ps_o[i % 3]
            for hh in range(2):
                for kt in range(8):
                    S.mm(po[:, hh * 512:(hh + 1) * 512], yc[s][:, kt, g * 128:(g + 1) * 128], wo[:, kt, hh * 512:(hh + 1) * 512],
                         start=(kt == 0), stop=(kt == 7), inc=(kt == 7 and hh == 1))

        def s1(i):
            ti, g = divmod(i, 4)
            s, b3, t0 = ti % 2, i % 3, tiles[ti]
            po, sv = ps_o[b3], st[b3]
            S.act(junk[:], po[:], AF.Square, accum_out=sv[:, 0:1], wk=["junkC", "stC%d_a" % b3])
            rsqrt_act(S, sv[:, 2:3], sv[:, 0:1], 1.0 / D, sv[:, 1:2], c.epsc[:], rk=["stC%d_a" % b3], tk=["stC%d_b" % b3], wk=["stC%d_c" % b3])
            S.stt(tmp[i % 2][:], po[:], sv[:, 2:3], gpost[:], ALU.mult, ALU.mult, rk=[_nm(po[:]), "stC%d_c" % b3, "gpost"])
            S.tt("dve", x1[b3][:], tmp[i % 2][:], xt[s][:, g, :], ALU.add)
            S.dma("pool", "x1st%d" % b3, c.x1_d[t0 + g * 128:t0 + (g + 1) * 128, :], x1[b3][:], writes=[("x1_d", t0 + g * 128)])

        def s2(i):
            b3 = i % 3
            sv = st[b3]
            S.act(junk[:], x1[b3][:], AF.Square, accum_out=sv[:, 3:4], wk=["junkC", "stC%d_d" % b3])
            rsqrt_act(S, sv[:, 5:6], sv[:, 3:4], 1.0 / D, sv[:, 4:5], c.epsc[:], rk=["stC%d_d" % b3], tk=["stC%d_e" % b3], wk=["stC%d_f" % b3])
            S.stt(h2[b3][:], x1[b3][:], sv[:, 5:6], gffn[:], ALU.mult, ALU.mult, rk=[_nm(x1[b3][:]), "stC%d_f" % b3, "gffn"])

        def s3(i):
            ti, g = divmod(i, 4)
            s, b3, t0 = ti % 2, i % 3, tiles[ti]
            pt = ps_t[i % 2]
            for kt in range(8):
                S.tr(pt[:, kt * 128:(kt + 1) * 128], h2[b3][:, kt * 128:(kt + 1) * 128], c.identb[:], inc=(kt == 7))
            S.copy("act", h2T[s][:, :, g * 128:(g + 1) * 128], pt[:].rearrange("p (k t) -> p k t", k=8))
            if g == 3:
                S.dma("pool", "h2Tst%d" % s, c.h2T_d[:, :, t0:t0 + 512].rearrange("f p t -> p f t"), h2T[s][:],
                      writes=[("h2T_d", t0)])

        load(0)
        pipeline(NTI * 4, [s0, s1, s2, s3])
        S.barrier()


def phase_C2(c):
    nc, S = c.nc, c.S
    BLK = 1024
    with contextlib.ExitStack() as es:
        E = es.enter_context
        sb = lambda n, sh, dt=F32: E(nc.sbuf_tensor(n, sh, dt))
        wdn = c.wdn
        fw, fb = sb("fw", [128, NM_UP, 3]), sb("fb", [128, NM_UP])
        gpo = sb("gpo", [128, D])
        wu = [sb("wu%d" % i, [128, 8, 128], BF16) for i in range(4)]
        hT = [sb("hTF%d" % i, [128, 8, BLK + 2], BF16) for i in range(2)]
        actT = sb("actT", [128, 22, BLK], BF16)
        ag = [sb("agF%d" % i, [128, BLK]) for i in range(2)]
        av = [sb("avF%d" % i, [128, BLK]) for i in range(2)]
        sgl = [sb("sgF%d" % i, [128, BLK]) for i in range(2)]
        hv = [sb("hvF%d" % i, [128, 2]) for i in range(2)]
        x1 = [sb("x1F%d" % i, [128, D]) for i in range(2)]
        junk = sb("junkF", [128, D], BF16)
        st = [sb("stF%d" % i, [128, 4]) for i in range(2)]
        tmp = [sb("tmpF%d" % i, [128, D]) for i in range(2)]
        yo = [sb("yoF%d" % i, [128, D]) for i in range(2)]
        PS = [E(nc.psum_tensor("psF%d" % i, [128, 1024], F32)) for i in range(3)]
        PHs = [E(nc.psum_tensor("psFh%d" % i, [128, 512], F32)) for i in range(2)]
        for j in range(3):
            S.dma_split("sp", "cF0", fw[:, :, j], c.ffn_conv_w[j, :].rearrange("(m p) -> p m", p=128), 1, 11,
                        allow_slow_non_contiguous=True)
        S.dma_split("sp", "cF1", fb[:], c.ffn_conv_b[0, :].rearrange("(m p) -> p m", p=128), 1, 11, allow_slow_non_contiguous=True)
        S.dma("sp", "cF2", gpo[:], c.post_ffn_g.to_broadcast([128, D]))
        blocks = []
        for (s0, L) in c.seqs:
            nb = L // BLK
            for i in range(nb):
                blocks.append((s0 + i * BLK, i == 0, i == nb - 1))
        nwu = [0]

        def load_wu(m):
            w = wu[nwu[0] % 4]
            S.dma("sp", "wuld%d" % (nwu[0] % 4), w[:], c.wup_d[m, :, :, :], reads=["wup_d"])
            nwu[0] += 1
            return w

        def load_h(bi):
            t0, first, last = blocks[bi]
            h = hT[bi % 2]
            lo = 1 if first else 0
            hi = BLK + 1 if last else BLK + 2
            S.dma("sp", "hld%d" % (bi % 2), h[:, :, lo:hi], c.h2T_d[:, :, t0 - 1 + lo:t0 - 1 + hi].rearrange("f p t -> p f t"),
                  reads=[("h2T_d", t0 + 512 * k) for k in range(-1 if not first else 0, 3 if not last else 2)])
            if first:
                S.memset("pool", h[:, :, 0:1], 0.0)
            if last:
                S.memset("pool", h[:, :, BLK + 1:BLK + 2], 0.0)

        load_h(0)
        nps = 0
        nh = 0
        nd = 0
        for bi, (t0, first, last) in enumerate(blocks):
            h = hT[bi % 2]
            if bi + 1 < len(blocks):
                load_h(bi + 1)
            interior = (not first) or (not last)
            for mp in range(22):
                res = []
                for which, m in enumerate((mp, 22 + mp)):
                    w = load_wu(m)
                    ps = PS[nps % 3]
                    nps += 1
                    for hh in range(2):
                        for kt in range(8):
                            S.mm(ps[:, hh * 512:(hh + 1) * 512], w[:, kt, :], h[:, kt, 1 + hh * 512:1 + (hh + 1) * 512],
                                 start=(kt == 0), stop=(kt == 7), inc=(kt == 7 and hh == 1))
                    PH = PHs[nh % 2]
                    nh += 1
                    for kt in range(8):
                        S.mm(PH[:, 0:2], w[:, kt, :], h[:, kt, 0:BLK + 2:BLK + 1], start=(kt == 0), stop=(kt == 7))
                    a = (ag if which == 0 else av)[mp % 2]
                    S.act(a[:], ps[:], AF.Identity, scale=fw[:, m, 1:2], bias=fb[:, m:m + 1])
                    S.stt(a[:, 1:BLK], ps[:, 0:BLK - 1], fw[:, m, 0:1], a[:, 1:BLK], ALU.mult, ALU.add)
                    S.stt(a[:, 0:BLK - 1], ps[:, 1:BLK], fw[:, m, 2:3], a[:, 0:BLK - 1], ALU.mult, ALU.add)
                    hvv = hv[which]
                    S.tt("dve", hvv[:], PH[:, 0:2], fw[:, m, 0:3:2], ALU.mult)
                    S.tt("dve", a[:, 0:BLK:BLK - 1], a[:, 0:BLK:BLK - 1], hvv[:], ALU.add)
                    res.append(a)
                g = sgl[mp % 2]
                S.act(g[:], res[0][:], AF.Silu)
                S.tt("pool", actT[:, mp, :], g[:], res[1][:], ALU.mult, wk=["actT_%d" % mp])
            for tg in range(BLK // 128):
                b = nd % 2
                nd += 1
                PD = PS[nps % 3]
                nps += 1
                tt0 = t0 + tg * 128
                S.dma("sp", "x1ld%d" % b, x1[b][:], c.x1_d[tt0:tt0 + 128, :], reads=[("x1_d", tt0)])
                for hh in range(2):
                    for kt in range(22):
                        S.mm(PD[:, hh * 512:(hh + 1) * 512], actT[:, kt, tg * 128:(tg + 1) * 128], wdn[:, kt, hh * 512:(hh + 1) * 512],
                             start=(kt == 0), stop=(kt == 21), inc=(kt == 21 and hh == 1), rk=["actT_%d" % kt, "wdn"])
                sv = st[b]
                S.act(junk[:], PD[:], AF.Square, accum_out=sv[:, 0:1], wk=["junkF", "stF%d_a" % b])
                rsqrt_act(S, sv[:, 2:3], sv[:, 0:1], 1.0 / D, sv[:, 1:2], c.epsc[:], rk=["stF%d_a" % b], tk=["stF%d_b" % b], wk=["stF%d_c" % b])
                S.stt(tmp[b][:], PD[:], sv[:, 2:3], gpo[:], ALU.mult, ALU.mult, rk=[_nm(PD[:]), "stF%d_c" % b, "gpo"])
                S.tt("pool", yo[b][:], tmp[b][:], x1[b][:], ALU.add)
                S.dma("pool", "yst%d" % b, c.y[tt0:tt0 + 128, :], yo[b][:], writes=[("y", tt0)])
        S.barrier()


SEQ_LENS = (2048, 2048, 4096, 4096)
_CONSTS = {
    "ident": np.eye(128, dtype=np.float32),
    "mask32": (np.arange(128)[:, None] % 32 == np.arange(32)[None, :]).astype(np.float32),
}
_WKEYS = ["pre_mix_g", "w_in", "conv_w", "lam_re", "lam_im", "log_step", "b_re", "b_im", "c_re", "c_im", "d_skip",
          "w_glu", "b_glu", "gn_conv", "gn_ssm", "w_out", "post_mix_g", "pre_ffn_g", "w_up", "ffn_conv_w",
          "ffn_conv_b", "w_down", "post_ffn_g"]


def _weights_map(inputs):
    m = {}
    for k in _WKEYS:
        a = np.asarray(inputs[k], dtype=np.float32)
        a = a[0]
        if a.ndim == 1:
            a = a[None, :]
        m[k] = np.ascontiguousarray(a)
    m.update(_CONSTS)
    return m


def kernel(**inputs):
    xp = np.asarray(inputs["x_prompt"], dtype=np.float32)
    xs = np.asarray(inputs["x_sample"], dtype=np.float32)
    n = 8
    nc = build(SEQ_LENS)
    wm = _weights_map(inputs)
    in_maps = []
    for i in range(n):
        xc = np.concatenate([xp[2 * i].reshape(-1, D), xp[2 * i + 1].reshape(-1, D),
                             xs[2 * i].reshape(-1, D), xs[2 * i + 1].reshape(-1, D)], axis=0)
        d = dict(wm)
        d["x"] = np.ascontiguousarray(xc)
        in_maps.append(d)
    res = run_bass_kernel_spmd(nc, in_maps, core_ids=list(range(n)))
    yp = np.empty_like(xp)
    ys = np.empty_like(xs)
    for i in range(n):
        y = res.results[i]["y"]
        yp[2 * i] = y[0:2048]
        yp[2 * i + 1] = y[2048:4096]
        ys[2 * i] = y[4096:8192]
        ys[2 * i + 1] = y[8192:12288]
    return (yp, ys)
```

```python
import contextlib
import math
import numpy as np
import concourse.bass as bass
import concourse.mybir as mybir
from concourse.bass_utils import run_bass_kernel_spmd

F32, BF16 = mybir.dt.float32, mybir.dt.bfloat16
AF, ALU = mybir.ActivationFunctionType, mybir.AluOpType

D = 1024
DC = 512
DIN = 2048
DFF = 2816
NM_UP = 44
EPS = 1e-6
MAGIC = 12582912.0
TWO_PI = 2.0 * math.pi


def _nm(ap):
    return ap.name


class Sched:
    def __init__(self, nc, es):
        self.nc, self.es = nc, es
        self.engs = {"pe": nc.tensor, "act": nc.scalar, "dve": nc.vector, "pool": nc.gpsimd, "sp": nc.sync}
        self.sems, self.cnt = {}, {}
        self.seen = {e: {} for e in self.engs}
        self.lastw, self.readers = {}, {}

    def _sem(self, key):
        if key not in self.sems:
            self.sems[key] = self.es.enter_context(self.nc.semaphore("s%d" % len(self.sems)))
            self.cnt[key] = 0
        return self.sems[key]

    def _wait(self, eng, deps):
        for (k, v) in deps:
            if eng == "pe" and k == "pe":
                continue
            if self.seen[eng].get(k, 0) < v:
                self.engs[eng].wait_ge(self._sem(k), v)
                self.seen[eng][k] = v

    def _deps(self, reads, writes):
        deps = []
        for b in reads:
            if b in self.lastw:
                deps.append(self.lastw[b])
        for b in writes:
            if b in self.lastw:
                deps.append(self.lastw[b])
            deps.extend(self.readers.get(b, {}).items())
        return deps

    def _record(self, ev, reads, writes):
        for b in writes:
            self.lastw[b] = ev
            self.readers[b] = {}
        for b in reads:
            r = self.readers.setdefault(b, {})
            if r.get(ev[0], 0) < ev[1]:
                r[ev[0]] = ev[1]

    def op(self, eng, fn, reads=(), writes=(), inc=True):
        self._wait(eng, self._deps(reads, writes))
        ins = fn(self.engs[eng])
        self._sem(eng)
        if inc:
            self.cnt[eng] += 1
            ins.then_inc(self.sems[eng], 1)
            ev = (eng, self.cnt[eng])
        else:
            ev = (eng, self.cnt[eng] + 1)
        self._record(ev, reads, writes)
        return ev

    def dma(self, q, semkey, out, in_, reads=None, writes=None, **kw):
        reads = [_nm(in_)] if reads is None else reads
        writes = [_nm(out)] if writes is None else writes
        self._wait(q, self._deps(reads, writes))
        s = self._sem(semkey)
        self.cnt[semkey] += 16
        self.engs[q].dma_start(out=out, in_=in_, **kw).then_inc(s, 16)
        ev = (semkey, self.cnt[semkey])
        self._record(ev, reads, writes)
        return ev

    def dma_split(self, q, semkey, out, in_, axis, nchunks, **kw):
        n = out.shape[axis]
        step = -(-n // nchunks)
        ev = None
        for lo in range(0, n, step):
            hi = min(n, lo + step)
            idx = [slice(None)] * len(out.shape)
            idx[axis] = slice(lo, hi)
            ev = self.dma(q, semkey, out[tuple(idx)], in_[tuple(idx)], **kw)
        return ev

    def wait_all(self, eng):
        mx = {}
        for (k, v) in self.lastw.values():
            mx[k] = max(mx.get(k, 0), v)
        for r in self.readers.values():
            for k, v in r.items():
                mx[k] = max(mx.get(k, 0), v)
        self._wait(eng, [(k, min(v, self.cnt[k])) for k, v in mx.items() if k != eng or eng != "pe"])

    def barrier(self):
        for e in self.engs:
            self.wait_all(e)
        self.lastw, self.readers = {}, {}

    def tt(self, eng, out, in0, in1, op, rk=None, wk=None):
        return self.op(eng, lambda e: e.tensor_tensor(out=out, in0=in0, in1=in1, op=op),
                       reads=rk if rk is not None else [_nm(in0), _nm(in1)], writes=wk if wk is not None else [_nm(out)])

    def ts(self, eng, out, in0, s1, s2, op0, op1=None, rk=None, wk=None):
        r = [_nm(in0)] + [_nm(s) for s in (s1, s2) if hasattr(s, "name")]
        if op1 is None:
            f = lambda e: e.tensor_scalar(out=out, in0=in0, scalar1=s1, scalar2=None, op0=op0)
        else:
            f = lambda e: e.tensor_scalar(out=out, in0=in0, scalar1=s1, scalar2=s2, op0=op0, op1=op1)
        return self.op(eng, f, reads=rk if rk is not None else r, writes=wk if wk is not None else [_nm(out)])

    def stt(self, out, in0, scalar, in1, op0, op1, rk=None, wk=None):
        r = [_nm(in0), _nm(in1)] + ([_nm(scalar)] if hasattr(scalar, "name") else [])
        return self.op("dve", lambda e: e.scalar_tensor_tensor(out=out, in0=in0, scalar=scalar, in1=in1, op0=op0, op1=op1),
                       reads=rk if rk is not None else r, writes=wk if wk is not None else [_nm(out)])

    def act(self, out, in_, func, bias=None, scale=None, accum_out=None, rk=None, wk=None):
        kw = {}
        r = [_nm(in_)]
        w = [_nm(out)]
        if bias is not None:
            kw["bias"] = bias
            if hasattr(bias, "name"):
                r.append(_nm(bias))
        if scale is not None:
            kw["scale"] = scale
            if hasattr(scale, "name"):
                r.append(_nm(scale))
        if accum_out is not None:
            kw["accum_out"] = accum_out
            w.append(_nm(accum_out))
        return self.op("act", lambda e: e.activation(out=out, in_=in_, func=func, **kw),
                       reads=rk if rk is not None else r, writes=wk if wk is not None else w)

    def copy(self, eng, out, in_, rk=None, wk=None):
        if eng == "act":
            return self.act(out, in_, AF.Copy, rk=rk, wk=wk)
        return self.op(eng, lambda e: e.tensor_copy(out=out, in_=in_),
                       reads=rk if rk is not None else [_nm(in_)], writes=wk if wk is not None else [_nm(out)])

    def memset(self, eng, out, val, wk=None):
        return self.op(eng, lambda e: e.memset(out, val), reads=[], writes=wk if wk is not None else [_nm(out)])

    def mm(self, out, lhsT, rhs, start, stop, inc=None, tp=None, rk=None, wk=None):
        inc = stop if inc is None else inc
        kw = {} if tp is None else {"tile_position": tp}
        return self.op("pe", lambda e: e.matmul(out, lhsT=lhsT, rhs=rhs, start=start, stop=stop, **kw),
                       reads=rk if rk is not None else [_nm(lhsT), _nm(rhs)],
                       writes=wk if wk is not None else [_nm(out)], inc=inc)

    def tr(self, out, in_, ident, inc=True, tp=None, rk=None, wk=None):
        kw = {} if tp is None else {"tile_position": tp}
        return self.op("pe", lambda e: e.transpose(out, in_, ident, **kw),
                       reads=rk if rk is not None else [_nm(in_), _nm(ident)],
                       writes=wk if wk is not None else [_nm(out)], inc=inc)

    def scan(self, out, d0, d1, init=0.0):
        return self.op("dve", lambda e: e.tensor_tensor_scan(out=out, data0=d0, data1=d1, initial=init,
                                                             op0=ALU.mult, op1=ALU.add),
                       reads=[_nm(d0), _nm(d1)], writes=[_nm(out)])


class Ctx:
    pass


def pipeline(n, stages):
    for t in range(n + len(stages) - 1):
        for k, st in enumerate(stages):
            i = t - k
            if 0 <= i < n:
                st(i)


def rsqrt_act(S, out, in_, scale, tmp, eps_ap, rk=None, wk=None, tk=None):
    S.act(tmp, in_, AF.Ln, scale=scale, bias=eps_ap, rk=rk, wk=tk)
    S.act(out, tmp, AF.Exp, scale=-0.5, rk=tk, wk=wk)


def build(seq_lens, debug=False):
    NT = sum(seq_lens)
    assert all(L % 1024 == 0 for L in seq_lens)
    nc = bass.Bass("TRN2", target_bir_lowering=False)
    c = Ctx()
    c.nc, c.NT, c.seq_lens, c.debug = nc, NT, seq_lens, debug
    c.seqs = []
    t = 0
    for L in seq_lens:
        c.seqs.append((t, L))
        t += L

    def din(name, shape):
        return nc.dram_tensor(name, list(shape), F32, kind="ExternalInput").ap()

    c.x = din("x", [NT, D])
    c.pre_mix_g = din("pre_mix_g", [1, D])
    c.w_in = din("w_in", [D, DIN])
    c.conv_w = din("conv_w", [3, DC])
    c.lam_re = din("lam_re", [2, 32, 64])
    c.lam_im = din("lam_im", [2, 32, 64])
    c.log_step = din("log_step", [2, 32])
    c.b_re = din("b_re", [2, 32, 64, 16])
    c.b_im = din("b_im", [2, 32, 64, 16])
    c.c_re = din("c_re", [2, 32, 16, 64])
    c.c_im = din("c_im", [2, 32, 16, 64])
    c.d_skip = din("d_skip", [1, DC])
    c.w_glu = din("w_glu", [DC, DC])
    c.b_glu = din("b_glu", [1, DC])
    c.gn_conv = din("gn_conv", [1, DC])
    c.gn_ssm = din("gn_ssm", [1, DC])
    c.w_out = din("w_out", [D, D])
    c.post_mix_g = din("post_mix_g", [1, D])
    c.pre_ffn_g = din("pre_ffn_g", [1, D])
    c.w_up = din("w_up", [D, 2 * DFF])
    c.ffn_conv_w = din("ffn_conv_w", [3, 2 * DFF])
    c.ffn_conv_b = din("ffn_conv_b", [1, 2 * DFF])
    c.w_down = din("w_down", [DFF, D])
    c.post_ffn_g = din("post_ffn_g", [1, D])
    c.ident = din("ident", [128, 128])
    c.mask32 = din("mask32", [128, 32])
    c.y = nc.dram_tensor("y", [NT, D], F32, kind="ExternalOutput").ap()

    sk = "ExternalOutput" if debug else "Internal"
    c.zuT_d = nc.dram_tensor("zuT_d", [4, 128, NT], BF16, kind=sk).ap()
    c.ycat_d = nc.dram_tensor("ycat_d", [8, 128, NT], BF16, kind=sk).ap()
    c.yg_d = nc.dram_tensor("yg_d", [4, 128, NT], F32, kind=sk).ap()
    c.x1_d = nc.dram_tensor("x1_d", [NT, D], F32, kind=sk).ap()
    c.h2T_d = nc.dram_tensor("h2T_d", [8, 128, NT], BF16, kind=sk).ap()
    c.wup_d = nc.dram_tensor("wup_d", [NM_UP, 128, 8, 128], BF16, kind="Internal").ap()
    c.WX_d = nc.dram_tensor("WX_d", [4, 128, 4096], BF16, kind="Internal").ap()
    c.WY_d = nc.dram_tensor("WY_d", [32, 128, 576], BF16, kind="Internal").ap()
    c.IN_d = nc.dram_tensor("IN_d", [4, 128, 2048], BF16, kind="Internal").ap()

    with contextlib.ExitStack() as es:
        S = Sched(nc, es)
        c.S = S
        c.es = es
        E = es.enter_context
        c.identb = E(nc.sbuf_tensor("identb", [128, 128], BF16))
        c.identf = E(nc.sbuf_tensor("identf", [128, 128], F32))
        c.onesb = E(nc.sbuf_tensor("onesb", [128, 128], BF16))
        S.dma("pool", "c0", c.identb[:], c.ident[:, :])
        S.dma("sp", "c1", c.identf[:], c.ident[:, :])
        S.memset("dve", c.onesb[:], 1.0)
        c.epsc = E(nc.sbuf_tensor("epsc", [128, 1], F32))
        S.memset("dve", c.epsc[:], EPS)
        S.barrier()
        gsb = lambda n, sh: E(nc.sbuf_tensor(n, sh, F32))
        c.lamre, c.lamim, c.lst = gsb("lamre", [128, 32]), gsb("lamim", [128, 32]), gsb("lst", [128, 32])
        c.dsk, c.m32 = gsb("dsk", [128, 4]), gsb("m32", [128, 32])
        S.dma_split("act", "p0", c.lamre[:].rearrange("p (d r) -> p d r", d=2), c.lam_re.rearrange("d (r q) p -> (q p) d r", q=2),
                    0, 4, allow_slow_non_contiguous=True)
        S.dma_split("act", "p1", c.lamim[:].rearrange("p (d r) -> p d r", d=2), c.lam_im.rearrange("d (r q) p -> (q p) d r", q=2),
                    0, 4, allow_slow_non_contiguous=True)
        for q in range(2):
            src = c.log_step.rearrange("d (r q) -> q d r", q=2)[q:q + 1]
            S.dma_split("act", "p2", c.lst[64 * q:64 * q + 64, :].rearrange("p (d r) -> p d r", d=2), src.to_broadcast([64, 2, 16]),
                        0, 2, allow_slow_non_contiguous=True)
        S.dma("act", "p7", c.dsk[:], c.d_skip[0, :].rearrange("(f p) -> p f", p=128), allow_slow_non_contiguous=True)
        S.dma("act", "p8", c.m32[:], c.mask32[:, :])
        stage = c.stage = ("ALL" if not debug else debug)
        phase_A(c)
        if stage in ("ALL", "B1", "B2", "C1", "C2"):
            phase_B1(c)
        if stage in ("ALL", "B2", "C1", "C2"):
            c.wo = E(nc.sbuf_tensor("wo", [128, 8, D], BF16))
            c.wdn = E(nc.sbuf_tensor("wdn", [128, 22, D], BF16))
            phase_B2(c)
        if stage in ("ALL", "C1", "C2"):
            phase_C1(c)
        if stage in ("ALL", "C2"):
            phase_C2(c)
        for e in ("sp", "pool", "act", "dve", "pe"):
            S.wait_all(e)
    return nc


def phase_prep_wup(c):
    nc, S = c.nc, c.S
    with contextlib.ExitStack() as es:
        E = es.enter_context
        st = [E(nc.sbuf_tensor("wupst%d" % i, [128, 2 * DFF], BF16)) for i in range(2)]
        for kt in range(8):
            s = st[kt % 2]
            S.dma("pool", "wupst%d" % (kt % 2), s[:], c.w_up[kt * 128:(kt + 1) * 128, :])
            S.dma("sp", "wupwr%d" % (kt % 2), c.wup_d[:, :, kt, :].rearrange("m p c -> p m c"),
                  s[:].rearrange("p (m c) -> p m c", c=128), writes=["wup_d"])
        S.barrier()


def phase_A(c):
    nc, S = c.nc, c.S
    NT = c.NT
    with contextlib.ExitStack() as es:
        E = es.enter_context
        sb = lambda n, sh, dt: E(nc.sbuf_tensor(n, sh, dt))
        w_in_sb = sb("w_in_sb", [128, 8, DIN], BF16)
        gpre = sb("gpre", [128, D], F32)
        cw = sb("cw", [128, 4, 3], F32)
        gnc = sb("gnc", [128, 4], F32)
        xt = [sb("xt%d" % i, [128, 4, D], F32) for i in range(2)]
        junk = sb("junkA", [128, D], BF16)
        junk2 = sb("junkA2", [128, D], BF16)
        ss = [sb("ssA%d" % i, [128, 4], F32) for i in range(2)]
        var4 = [sb("var4A%d" % i, [128, 4], F32) for i in range(2)]
        rstd = [sb("rstdA%d" % i, [128, 4], F32) for i in range(2)]
        xn = [sb("xn%d" % i, [128, D], BF16) for i in range(4)]
        hT = [sb("hT%d" % i, [128, 8, 512], BF16) for i in range(2)]
        pbuf = [sb("pbuf%d" % i, [128, 4, 514], F32) for i in range(2)]
        zbb = [sb("zbb%d" % i, [128, 4, 512], F32) for i in range(2)]
        zcs = [sb("zcs%d" % i, [128, 512], F32) for i in range(4)]
        zus = [sb("zus%d" % i, [128, 4, 512], BF16) for i in range(2)]
        acc = [sb("accA%d" % i, [128, 512], F32) for i in range(4)]
        yc = sb("ycA", [128, 4, 512], F32)
        sq = [sb("sqA%d" % i, [128, 512], BF16) for i in range(4)]
        var = sb("varA", [128, 512], F32)
        rst = sb("rstA", [128, 512], F32)
        ycn = [sb("ycn%d" % i, [128, 4, 512], BF16) for i in range(2)]
        ps_t = [E(nc.psum_tensor("psA_t%d" % i, [128, D], BF16)) for i in range(2)]
        ps_z = [E(nc.psum_tensor("psA_z%d" % i, [128, 512], F32)) for i in range(5)]
        ps_s = E(nc.psum_tensor("psA_s", [128, 512], F32))

        for cb in range(4):
            S.dma("pool", "winld%d" % cb, w_in_sb[:, :, cb * 512:(cb + 1) * 512],
                  c.w_in[:, cb * 512:(cb + 1) * 512].rearrange("(k p) n -> p k n", p=128), writes=["w_in_sb_%d" % cb])
        S.dma("sp", "cA0", gpre[:], c.pre_mix_g.to_broadcast([128, D]))
        for j in range(3):
            S.dma("sp", "cA1", cw[:, :, j], c.conv_w[j, :].rearrange("(f p) -> p f", p=128), allow_slow_non_contiguous=True)
        S.dma("sp", "cA2", gnc[:], c.gn_conv[0, :].rearrange("(f p) -> p f", p=128), allow_slow_non_contiguous=True)

        tiles = []
        for (s0, L) in c.seqs:
            n = L // 512
            for i in range(n):
                tiles.append((s0 + i * 512, i == 0, i == n - 1))

        def load_x(i):
            t0 = tiles[i][0]
            S.dma("sp", "xld%d" % (i % 2), xt[i % 2][:], c.x[t0:t0 + 512, :].rearrange("(g p) d -> p g d", p=128))

        def fA1(i, g):
            s = i % 2
            if g % 2 == 0:
                S.act(junk[:], xt[s][:, g, :], AF.Square, accum_out=ss[s][:, g:g + 1], wk=["junkA", "ssA%d_%d" % (s, g)])
            else:
                S.op("dve", lambda e: e.scalar_tensor_tensor(out=junk2[:], in0=xt[s][:, g, :], scalar=1.0, in1=xt[s][:, g, :],
                                                               op0=ALU.mult, op1=ALU.mult, accum_out=ss[s][:, g:g + 1]),
                     reads=[_nm(xt[s][:])], writes=["junkA2", "ssA%d_%d" % (s, g)])
            rsqrt_act(S, rstd[s][:, g:g + 1], ss[s][:, g:g + 1], 1.0 / D, var4[s][:, g:g + 1], c.epsc[:],
                      rk=["ssA%d_%d" % (s, g)], tk=["var4A%d_%d" % (s, g)], wk=["rstdA%d_%d" % (s, g)])
            S.stt(xn[g][:], xt[s][:, g, :], rstd[s][:, g:g + 1], gpre[:], ALU.mult, ALU.mult,
                  rk=[_nm(xt[s][:]), "rstdA%d_%d" % (s, g), "gpre"])

        def fA2(i, g):
            pt = ps_t[g % 2]
            for kt in range(8):
                S.tr(pt[:, kt * 128:(kt + 1) * 128], xn[g][:, kt * 128:(kt + 1) * 128], c.identb[:], inc=(kt == 7))

        def fA3(i, g):
            s = i % 2
            S.copy("act", hT[s][:, :, g * 128:(g + 1) * 128], ps_t[g % 2][:].rearrange("p (k t) -> p k t", k=8),
                   wk=["hT%d_%d" % (s, g)])

        def front_items(i):
            it = [(lambda g=g: fA1(i, g)) for g in range(4)]
            for g in range(4):
                it.append(lambda g=g: fA2(i, g))
                it.append(lambda g=g: fA3(i, g))
            return it

        def c1(i, f):
            S.act(acc[f][:], pbuf[i % 2][:, f, 1:513], AF.Identity, scale=cw[:, f, 1:2])

        def c2(i, f):
            s = i % 2
            S.stt(acc[f][:], pbuf[s][:, f, 0:512], cw[:, f, 0:1], acc[f][:], ALU.mult, ALU.add)
            S.stt(acc[f][:], pbuf[s][:, f, 2:514], cw[:, f, 2:3], acc[f][:], ALU.mult, ALU.add)
            S.tt("dve", yc[:, f, :], acc[f][:], zbb[s][:, f, :], ALU.mult, wk=["ycA_%d" % f])

        def c3(i, f):
            S.act(sq[f][:], yc[:, f, :], AF.Square, rk=["ycA_%d" % f])
            S.mm(ps_s[:], c.onesb[:], sq[f][:], start=(f == 0), stop=(f == 3), inc=True)

        def c4(i):
            t0 = tiles[i][0]
            rsqrt_act(S, rst[:], ps_s[:], 1.0 / DC, var[:], c.epsc[:])
            yn = ycn[i % 2]
            for ff in range(4):
                S.stt(yn[:, ff, :], yc[:, ff, :], gnc[:, ff:ff + 1], rst[:], ALU.mult, ALU.mult,
                      rk=["ycA_%d" % ff, "gnc", "rstA"])
            S.dma("pool", "ycnst%d" % (i % 2), c.ycat_d[0:4, :, t0:t0 + 512].rearrange("f p t -> p f t"), yn[:],
                  writes=[("ycat_d", 0, t0)])

        def conv_items(i):
            order = [("c1", 0), ("c1", 1), ("c2", 0), ("c1", 2), ("c2", 1), ("c3", 0), ("c1", 3), ("c2", 2), ("c3", 1),
                     ("c2", 3), ("c3", 2), ("c3", 3)]
            fn = {"c1": c1, "c2": c2, "c3": c3}
            it = [(lambda k=k, f=f: fn[k](i, f)) for (k, f) in order]
            it.append(lambda: c4(i))
            return it

        load_x(0)
        for w_ in front_items(0):
            w_()
        bg = []
        for i, (t0, first, last) in enumerate(tiles):
            s = i % 2
            if i + 1 < len(tiles):
                load_x(i + 1)
                bg.extend(front_items(i + 1))
            def evac(m, s=s, first=first, last=last, i=i):
                pz = ps_z[m % 5]
                f = m % 4
                if m < 4:
                    S.copy("dve", zbb[s][:, f, :], pz[:])
                elif m < 8:
                    S.copy("dve", zcs[f][:], pz[:])
                elif m < 12:
                    S.tt("dve", pbuf[s][:, f, 1:513], pz[:], zcs[f][:], ALU.mult)
                else:
                    S.copy("act", zus[s][:, f, :].rearrange("p (j c) -> p c j", j=8), pz[:].rearrange("p (c j) -> p c j", j=8))
                if m == 11:
                    if first:
                        S.memset("pool", pbuf[s][:, :, 0:1], 0.0)
                    else:
                        S.copy("pool", pbuf[s][:, :, 0:1], pbuf[1 - s][:, :, 512:513])
                        S.copy("pool", pbuf[1 - s][:, :, 513:514], pbuf[s][:, :, 1:2])
                        bg.extend(conv_items(i - 1))
                    if last:
                        S.memset("pool", pbuf[s][:, :, 513:514], 0.0)

            for m in range(16):
                pz = ps_z[m % 5]
                for kt in range(8):
                    S.mm(pz[:], w_in_sb[:, kt, m * 128:(m + 1) * 128], hT[s][:, kt, :], start=(kt == 0), stop=(kt == 7),
                         rk=["w_in_sb_%d" % (m // 4)] + ["hT%d_%d" % (s, g) for g in range(4)])
                if m >= 1:
                    evac(m - 1)
                npop = -(-len(bg) // (16 - m)) if m >= 12 else min(len(bg), 2)
                for _ in range(npop):
                    bg.pop(0)()
            evac(15)
            S.dma("pool", "zust%d" % s, c.zuT_d[:, :, t0:t0 + 512].rearrange("q p t -> p q t"), zus[s][:],
                  writes=[("zuT_d", t0)])
            while bg:
                bg.pop(0)()
            if last:
                bg.extend(conv_items(i))
        while bg:
            bg.pop(0)()
        S.barrier()


def cmul(S, eng, o_re, o_im, x_re, x_im, y_re, y_im, t1, t2, conj_x=False):
    S.tt(eng, t1, x_re, y_re, ALU.mult)
    S.tt(eng, t2, x_im, y_im, ALU.mult)
    S.tt(eng, o_re, t1, t2, ALU.add if conj_x else ALU.subtract)
    S.tt(eng, t1, x_re, y_im, ALU.mult)
    S.tt(eng, t2, x_im, y_re, ALU.mult)
    S.tt(eng, o_im, t1, t2, ALU.subtract if conj_x else ALU.add)


def phase_B1(c):
    nc, S = c.nc, c.S
    with contextlib.ExitStack() as es:
        E = es.enter_context
        sb = lambda n, sh, dt: E(nc.sbuf_tensor(n, sh, dt))
        U8r = sb("U8r", [128, 32], F32)
        U8i = sb("U8i", [128, 32], F32)
        R8 = sb("R8", [128, 32], F32)
        with contextlib.ExitStack() as es1:
            WX = es1.enter_context(nc.sbuf_tensor("WX", [128, 4, 2, 2, 8, 128], BF16))
            WY = es1.enter_context(nc.sbuf_tensor("WY", [128, 32, 2, 9, 32], BF16))
            INTRA = es1.enter_context(nc.sbuf_tensor("INTRA", [128, 4, 64, 32], BF16))
            s5_prep(c, WX, WY, INTRA, U8r, U8i, R8)
            S.dma("sp", "spl0", c.WX_d.rearrange("q p x -> p q x"), WX[:].rearrange("p q a b e n -> p q (a b e n)"), writes=["WX_d"])
            S.dma_split("sp", "spl1", c.WY_d.rearrange("u p x -> p u x"), WY[:].rearrange("p u a k n -> p u (a k n)"), 1, 4,
                        writes=["WY_d"])
            S.dma("sp", "spl2", c.IN_d.rearrange("q p x -> p q x"), INTRA[:].rearrange("p q j n -> p q (j n)"), writes=["IN_d"])
            S.barrier()
        s5_main(c, U8r, U8i, R8)


def s5_prep(c, WX, WY, INTRA, U8r, U8i, R8):
    nc, S = c.nc, c.S
    with contextlib.ExitStack() as es:
        E = es.enter_context
        sb = lambda n, sh, dt=F32: E(nc.sbuf_tensor(n, sh, dt))
        lamre, lamim, lst, dsk, m32 = c.lamre, c.lamim, c.lst, c.dsk, c.m32
        Bre, Bim = sb("Bre", [128, 32, 16]), sb("Bim", [128, 32, 16])
        S.dma_split("sp", "p3", Bre[:].rearrange("p (d r) h -> p d r h", d=2), c.b_re.rearrange("d (r q) p h -> (q p) d r h", q=2), 0, 4)
        S.dma_split("sp", "p4", Bim[:].rearrange("p (d r) h -> p d r h", d=2), c.b_im.rearrange("d (r q) p h -> (q p) d r h", q=2), 0, 4)
        Cre, Cim = sb("Cre", [128, 32, 16]), sb("Cim", [128, 32, 16])
        with contextlib.ExitStack() as es2:
            Cnr = es2.enter_context(nc.sbuf_tensor("Cnr", [16, 64, 64], F32))
            Cni = es2.enter_context(nc.sbuf_tensor("Cni", [16, 64, 64], F32))
            psC = [es2.enter_context(nc.psum_tensor("psC%d" % i, [128, 32, 16], F32)) for i in range(2)]
            S.dma("sp", "p5", Cnr[:], c.c_re.rearrange("d g h p -> h (d g) p"))
            S.dma("sp", "p6", Cni[:], c.c_im.rearrange("d g h p -> h (d g) p"))
            for ri, (Cn, Cst) in enumerate(((Cnr, Cre), (Cni, Cim))):
                for u in range(32):
                    d, pr = divmod(u, 16)
                    a = d * 32 + 2 * pr
                    S.tr(psC[ri][:, u, :], Cn[:, a:a + 2, :].rearrange("h g p -> h (g p)"), c.identf[0:16, 0:16], inc=(u == 31))
                S.copy("dve", Cst[:], psC[ri][:])
            S.barrier()

        dt, lr, th = sb("dtp", [128, 32]), sb("lr", [128, 32]), sb("th", [128, 32])
        mag, kf, thr = sb("mag", [128, 32]), sb("kf", [128, 32]), sb("thr", [128, 32])
        sh, ch, t1, t2 = sb("sh", [128, 32]), sb("ch", [128, 32]), sb("t1p", [128, 32]), sb("t2p", [128, 32])
        cs, sn = sb("cs", [128, 32]), sb("sn", [128, 32])
        hpi = sb("hpi", [128, 1])
        S.memset("dve", hpi[:], math.pi / 2)
        S.act(dt[:], lst[:], AF.Exp)
        S.tt("dve", lr[:], lamre[:], dt[:], ALU.mult)
        S.tt("dve", th[:], lamim[:], dt[:], ALU.mult)
        S.act(mag[:], lr[:], AF.Exp)
        S.act(R8[:], lr[:], AF.Exp, scale=8.0)
        S.ts("dve", kf[:], th[:], 1.0 / TWO_PI, MAGIC, ALU.mult, ALU.add)
        S.ts("dve", kf[:], kf[:], -MAGIC, -TWO_PI, ALU.add, ALU.mult)
        S.tt("dve", thr[:], th[:], kf[:], ALU.add)
        S.act(sh[:], thr[:], AF.Sin, scale=0.5)
        S.act(ch[:], thr[:], AF.Sin, scale=0.5, bias=hpi[:])
        S.tt("dve", t1[:], ch[:], ch[:], ALU.mult)
        S.tt("dve", t2[:], sh[:], sh[:], ALU.mult)
        S.tt("dve", cs[:], t1[:], t2[:], ALU.subtract)
        S.tt("dve", t1[:], sh[:], ch[:], ALU.mult)
        S.ts("dve", sn[:], t1[:], 2.0, None, ALU.mult)
        UPr, UPi = sb("UPr", [128, 32, 9]), sb("UPi", [128, 32, 9])
        APr, APi = sb("APr", [128, 32, 9]), sb("APi", [128, 32, 9])
        T1, T2 = sb("T1p", [128, 32, 4]), sb("T2p", [128, 32, 4])
        S.memset("dve", UPr[:, :, 0:1], 1.0)
        S.memset("dve", UPi[:, :, 0:1], 0.0)
        S.copy("dve", UPr[:, :, 1], cs[:])
        S.copy("dve", UPi[:, :, 1], sn[:])
        cmul(S, "dve", UPr[:, :, 2], UPi[:, :, 2], UPr[:, :, 1], UPi[:, :, 1], UPr[:, :, 1], UPi[:, :, 1], T1[:, :, 0], T2[:, :, 0])
        bc = lambda ap, n: ap.to_broadcast([128, 32, n])
        cmul(S, "dve", UPr[:, :, 3:5], UPi[:, :, 3:5], UPr[:, :, 1:3], UPi[:, :, 1:3], bc(UPr[:, :, 2:3], 2), bc(UPi[:, :, 2:3], 2),
             T1[:, :, 0:2], T2[:, :, 0:2])
        cmul(S, "dve", UPr[:, :, 5:9], UPi[:, :, 5:9], UPr[:, :, 1:5], UPi[:, :, 1:5], bc(UPr[:, :, 4:5], 4), bc(UPi[:, :, 4:5], 4),
             T1[:], T2[:])
        S.copy("dve", U8r[:], UPr[:, :, 8])
        S.copy("dve", U8i[:], UPi[:, :, 8])
        MG = sb("MG", [128, 32, 9])
        S.memset("dve", MG[:, :, 0:1], 1.0)
        for k in range(1, 9):
            S.act(MG[:, :, k], lr[:], AF.Exp, scale=float(k))
        S.tt("dve", APr[:], UPr[:], MG[:], ALU.mult)
        S.tt("dve", APi[:], UPi[:], MG[:], ALU.mult)
        nr, den, inv = sb("nr", [128, 32]), sb("den", [128, 32]), sb("inv", [128, 32])
        qr, qi = sb("qr", [128, 32]), sb("qi", [128, 32])
        S.ts("dve", nr[:], APr[:, :, 1], -1.0, None, ALU.add)
        S.tt("dve", t1[:], lamre[:], lamre[:], ALU.mult)
        S.tt("dve", t2[:], lamim[:], lamim[:], ALU.mult)
        S.tt("dve", den[:], t1[:], t2[:], ALU.add)
        S.op("dve", lambda e: e.reciprocal(out=inv[:], in_=den[:]), reads=["den"], writes=["inv"])
        S.tt("dve", t1[:], nr[:], lamre[:], ALU.mult)
        S.tt("dve", t2[:], APi[:, :, 1], lamim[:], ALU.mult)
        S.tt("dve", t1[:], t1[:], t2[:], ALU.add)
        S.tt("dve", qr[:], t1[:], inv[:], ALU.mult)
        S.tt("dve", t1[:], APi[:, :, 1], lamre[:], ALU.mult)
        S.tt("dve", t2[:], nr[:], lamim[:], ALU.mult)
        S.tt("dve", t1[:], t1[:], t2[:], ALU.subtract)
        S.tt("dve", qi[:], t1[:], inv[:], ALU.mult)
        Bbr, Bbi = sb("Bbr", [128, 32, 16]), sb("Bbi", [128, 32, 16])
        V1, V2 = sb("V1", [128, 32, 16]), sb("V2", [128, 32, 16])
        b16 = lambda ap: ap.unsqueeze(2).to_broadcast([128, 32, 16])
        cmul(S, "dve", Bbr[:], Bbi[:], b16(qr[:]), b16(qi[:]), Bre[:], Bim[:], V1[:], V2[:])
        Gbd = sb("Gbd", [128, 32, 2, 8, 32], BF16)
        Bbd = sb("Bbd", [128, 32, 2, 32], BF16)
        S.memset("pool", Gbd[:], 0.0)
        S.memset("pool", Bbd[:], 0.0)
        S.memset("pool", WY[:], 0.0)
        W1, W2 = sb("W1", [128, 32, 9, 16]), sb("W2", [128, 32, 9, 16])

        def bk(ap, n):
            return ap.unsqueeze(3).to_broadcast([128, 32, n, 16])

        def bh(ap, n):
            return ap.unsqueeze(2).to_broadcast([128, 32, n, 16])

        def halves(dst_fn, src, n):
            for g2 in range(2):
                P = slice(64 * g2, 64 * g2 + 64)
                S.copy("act", dst_fn(P, slice(16 * g2, 16 * g2 + 16)), src[P])

        S.tt("dve", W2[:, :, 0:8, :], bk(APi[:, :, 0:8], 8), bh(Bbi[:], 8), ALU.mult)
        S.tt("dve", W1[:, :, 0:8, :], bk(APr[:, :, 0:8], 8), bh(Bbr[:], 8), ALU.mult)
        S.tt("dve", W1[:, :, 0:8, :], W1[:, :, 0:8, :], W2[:, :, 0:8, :], ALU.subtract)
        halves(lambda P, Cc: Gbd[P, :, 0, :, Cc], W1[:, :, 0:8, :], 8)
        S.tt("dve", W2[:, :, 0:8, :], bk(APi[:, :, 0:8], 8), bh(Bbr[:], 8), ALU.mult)
        S.tt("dve", W1[:, :, 0:8, :], bk(APr[:, :, 0:8], 8), bh(Bbi[:], 8), ALU.mult)
        S.tt("dve", W1[:, :, 0:8, :], W1[:, :, 0:8, :], W2[:, :, 0:8, :], ALU.add)
        halves(lambda P, Cc: Gbd[P, :, 1, :, Cc], W1[:, :, 0:8, :], 8)
        halves(lambda P, Cc: Bbd[P, :, 0, Cc], Bbr[:], 1)
        halves(lambda P, Cc: Bbd[P, :, 1, Cc], Bbi[:], 1)
        S.tt("dve", W2[:], bk(APi[:], 9), bh(Cim[:], 9), ALU.mult)
        S.tt("dve", W1[:], bk(APr[:], 9), bh(Cre[:], 9), ALU.mult)
        S.tt("dve", W1[:], W1[:], W2[:], ALU.subtract)
        halves(lambda P, Cc: WY[P, :, 0, :, Cc], W1[:], 9)
        S.tt("dve", W2[:], bk(APr[:], 9), bh(Cim[:], 9), ALU.mult)
        S.tt("dve", W1[:], bk(APi[:], 9), bh(Cre[:], 9), ALU.mult)
        S.tt("dve", W1[:], W1[:], W2[:], ALU.add)
        S.ts("dve", W1[:], W1[:], -1.0, None, ALU.mult)
        halves(lambda P, Cc: WY[P, :, 1, :, Cc], W1[:], 9)

        Kps = [E(nc.psum_tensor("Kps%d" % d, [128, 4, 8, 32], F32)) for d in range(2)]
        for d in range(2):
            for pr in range(16):
                u = d * 16 + pr
                q, pq = divmod(pr, 4)
                for ri in range(2):
                    S.mm(Kps[d][32 * pq:32 * pq + 32, q, :, :], Bbd[:, u, ri, :], WY[:, u, ri, 0:8, :],
                         start=(ri == 0), stop=(ri == 1), inc=(ri == 1 and pr == 15), tp=(0, 32 * pq))
        Kf, Kb = sb("Kf", [128, 4, 8, 32]), sb("Kb", [128, 4, 8, 32])
        S.copy("dve", Kf[:], Kps[0][:])
        S.copy("dve", Kb[:], Kps[1][:])
        Dd = sb("Dd", [128, 4, 32])
        S.tt("dve", Dd[:], m32[:].unsqueeze(1).to_broadcast([128, 4, 32]), dsk[:].unsqueeze(2).to_broadcast([128, 4, 32]), ALU.mult)
        S.tt("dve", Dd[:], Dd[:], Kf[:, :, 0, :], ALU.add)
        S.tt("dve", Dd[:], Dd[:], Kb[:, :, 0, :], ALU.add)
        S.copy("dve", INTRA[:, :, 0:64:9, :], Dd[:].unsqueeze(2).to_broadcast([128, 4, 8, 32]))
        for k in range(1, 8):
            n = 8 - k
            S.copy("dve", INTRA[:, :, k:k + 9 * (n - 1) + 1:9, :], Kf[:, :, k:k + 1, :].to_broadcast([128, 4, n, 32]))
            S.copy("dve", INTRA[:, :, 8 * k:8 * k + 9 * (n - 1) + 1:9, :], Kb[:, :, k:k + 1, :].to_broadcast([128, 4, n, 32]))

        psT = [E(nc.psum_tensor("psWX%d" % i, [128, 8, 128], BF16)) for i in range(2)]
        n = 0
        for q in range(4):
            for d in range(2):
                for ri in range(2):
                    pt = psT[n % 2]
                    n += 1
                    for e in range(8):
                        for pq in range(4):
                            u = d * 16 + 4 * q + pq
                            S.tr(pt[32 * pq:32 * pq + 32, e, :], Gbd[:, u, ri, e, :], c.identb[:],
                                 inc=(e == 7 and pq == 3), tp=(0, 32 * pq))
                    S.copy("act", WX[:, q, d, ri, :, :], pt[:])
        S.barrier()


def s5_main(c, U8r, U8i, R8):
    nc, S = c.nc, c.S
    Lmax = max(c.seq_lens)
    NCm = Lmax // 8
    with contextlib.ExitStack() as es:
        E = es.enter_context
        sb = lambda n, sh, dt=F32: E(nc.sbuf_tensor(n, sh, dt))
        WXq = sb("WXq", [128, 2, 2, 8, 128], BF16)
        WYq = sb("WYq", [128, 8, 2, 9, 32], BF16)
        INq = sb("INq", [128, 64, 32], BF16)
        Ec, Es = sb("Ec", [128, 8, NCm]), sb("Es", [128, 8, NCm])
        pwr, pwi = sb("pwr", [128, 8]), sb("pwi", [128, 8])
        pt1, pt2, pt3 = sb("pt1", [128, 8]), sb("pt2", [128, 8]), sb("pt3", [128, 8])
        zum = [[sb("zum%d_%d" % (r, i), [128, Lmax], BF16) for i in range(4)] for r in range(2)]
        Sbf = [sb("Sbf%d" % r, [128, 8, 2, NCm + 1], BF16) for r in range(2)]
        tmps = [[sb("tm%s%d" % (nm, i), [128, NCm]) for nm in "ABCDEF"] for i in range(2)]
        yg = sb("ygB", [128, Lmax])
        ET1 = yg[:, 0:Lmax // 2].rearrange("p (u k) -> p u k", u=8)
        ET2 = yg[:, Lmax // 2:Lmax].rearrange("p (u k) -> p u k", u=8)
        Xps = [[E(nc.psum_tensor("Xps%d_%d" % (i, ri), [128, 512], F32)) for ri in range(2)] for i in range(2)]
        Yps = [E(nc.psum_tensor("Yps%d" % i, [128, 512], F32)) for i in range(2)]
        for r in range(2):
            S.memset("pool" if r == 0 else "dve", Sbf[r][:], 0.0)
            for pq in range(4):
                if r == 0:
                    S.memset("pool", zum[r][pq][:], 0.0)
                else:
                    S.op("act", lambda e, t_=zum[r][pq]: e.memzero(t_[:]), reads=[], writes=[_nm(zum[r][pq][:])])
        nseq = len(c.seqs)

        def scan_prologue(q):
            S.dma("sp", "wq0", WXq[:].rearrange("p a b e n -> p (a b e n)"), c.WX_d[q], reads=["WX_d"])
            for d in range(2):
                usl = slice(d * 16 + 4 * q, d * 16 + 4 * q + 4)
                S.copy("dve", pwr[:, 4 * d:4 * d + 4], U8r[:, usl])
                S.copy("dve", pwi[:, 4 * d:4 * d + 4], U8i[:, usl])
            S.memset("dve", Ec[:, :, 0:1], 1.0)
            S.memset("dve", Es[:, :, 0:1], 0.0)
            seg = 1
            while seg < NCm:
                bcs = lambda ap: ap.unsqueeze(2).to_broadcast([128, 8, seg])
                cmul(S, "dve", Ec[:, :, seg:2 * seg], Es[:, :, seg:2 * seg], Ec[:, :, 0:seg], Es[:, :, 0:seg],
                     bcs(pwr[:]), bcs(pwi[:]), ET1[:, :, 0:seg], ET2[:, :, 0:seg])
                seg *= 2
                if seg < NCm:
                    S.tt("dve", pt1[:], pwr[:], pwr[:], ALU.mult)
                    S.tt("dve", pt2[:], pwi[:], pwi[:], ALU.mult)
                    S.tt("dve", pt3[:], pwr[:], pwi[:], ALU.mult)
                    S.tt("dve", pwr[:], pt1[:], pt2[:], ALU.subtract)
                    S.ts("dve", pwi[:], pt3[:], 2.0, None, ALU.mult)

        def out_prologue(q):
            for d in range(2):
                u0 = d * 16 + 4 * q
                S.dma("sp", "wq1", WYq[:, 4 * d:4 * d + 4].rearrange("p u a k n -> p u (a k n)"),
                      c.WY_d[u0:u0 + 4].rearrange("u p x -> p u x"), reads=["WY_d"])
            S.dma("sp", "wq2", INq[:].rearrange("p j n -> p (j n)"), c.IN_d[q], reads=["IN_d"])

        if True:
            NIT = 4 * nseq

            def step(t):
                g, h = t, t - 1
                gv, hv = g < NIT, h >= 0
                if gv:
                    q, k = divmod(g, nseq)
                    if k == 0:
                        scan_prologue(q)
                    s0, L = c.seqs[k]
                    n_c = L // 8
                    r = g % 2
                    for pq in range(4):
                        P = slice(32 * pq, 32 * pq + 32)
                        S.dma("sp" if pq % 2 == 0 else "act", "zuld%d_%d" % (r, pq), zum[r][pq][P, 0:L], c.zuT_d[q, P, s0:s0 + L],
                              reads=[("zuT_d", s0 + 512 * i) for i in range(L // 512)])
                    zdi = [zum[r][pq][:, 0:L].rearrange("p (t j c) -> p t j c", j=8, c=64) for pq in range(4)]
                if hv:
                    qh, kh = divmod(h, nseq)
                    if kh == 0:
                        out_prologue(qh)
                    s0h, Lh = c.seqs[kh]
                    n_ch = Lh // 8
                    rh = h % 2
                    zdh = [zum[rh][pq][:, 0:Lh].rearrange("p (t j c) -> p t j c", j=8, c=64) for pq in range(4)]
                for pair in range(4):
                    if gv:
                        ops = []
                        for uu in (2 * pair, 2 * pair + 1):
                            d, pq = divmod(uu, 4)
                            u = d * 16 + 4 * q + pq
                            xp = Xps[uu % 2]
                            for ri in range(2):
                                for j in range(8):
                                    e = 7 - j if d == 0 else j
                                    S.mm(xp[ri][:, 0:n_c], WXq[:, d, ri, e, :], zdi[pq][:, :, j, :], start=(j == 0), stop=(j == 7))
                            ec, esn = Ec[:, uu, 0:n_c], Es[:, uu, 0:n_c]
                            if d == 0:
                                xr, xi = xp[0][:, 0:n_c], xp[1][:, 0:n_c]
                                o_re, o_im = Sbf[r][:, uu, 0, 1:n_c + 1], Sbf[r][:, uu, 1, 1:n_c + 1]
                            else:
                                xr, xi = xp[0][:, 0:n_c][:, ::-1], xp[1][:, 0:n_c][:, ::-1]
                                o_re, o_im = Sbf[r][:, uu, 0, 0:n_c][:, ::-1], Sbf[r][:, uu, 1, 0:n_c][:, ::-1]
                            a, b, cc, dd, ee, ff = [tt_[:, 0:n_c] for tt_ in tmps[uu % 2]]
                            r8 = R8[:, u:u + 1].to_broadcast([128, n_c])
                            chain = [
                                lambda a=a, xr=xr, ec=ec: S.tt("dve", a, xr, ec, ALU.mult),
                                lambda b=b, xi=xi, esn=esn: S.tt("dve", b, xi, esn, ALU.mult),
                                lambda a=a, b=b: S.tt("dve", a, a, b, ALU.add),
                                lambda cc=cc, xi=xi, ec=ec: S.tt("dve", cc, xi, ec, ALU.mult),
                                lambda dd=dd, xr=xr, esn=esn: S.tt("dve", dd, xr, esn, ALU.mult),
                                lambda cc=cc, dd=dd: S.tt("dve", cc, cc, dd, ALU.subtract),
                                lambda a=a, b=b, r8=r8: S.scan(b, r8, a),
                                lambda cc=cc, dd=dd, r8=r8: S.scan(dd, r8, cc),
                                lambda a=a, b=b, ec=ec: S.tt("dve", a, b, ec, ALU.mult),
                                lambda ee=ee, dd=dd, esn=esn: S.tt("pool", ee, dd, esn, ALU.mult),
                                lambda cc=cc, dd=dd, ec=ec: S.tt("dve", cc, dd, ec, ALU.mult),
                                lambda ff=ff, b=b, esn=esn: S.tt("pool", ff, b, esn, ALU.mult),
                                lambda o_re=o_re, a=a, ee=ee: S.tt("pool", o_re, a, ee, ALU.subtract),
                                lambda o_im=o_im, cc=cc, ff=ff: S.tt("pool", o_im, cc, ff, ALU.add),
                            ]
                            if d == 1:
                                chain.append(lambda uu=uu: S.memset("pool", Sbf[r][:, uu, :, n_c:n_c + 1], 0.0))
                            ops.append(chain)
                        for i_ in range(max(len(ch) for ch in ops)):
                            for ch in ops:
                                if i_ < len(ch):
                                    ch[i_]()
                    if hv:
                        for j in (2 * pair, 2 * pair + 1):
                            yp = Yps[j % 2]
                            for pq in range(4):
                                P = slice(32 * pq, 32 * pq + 32)
                                first = True
                                for d in range(2):
                                    uu = d * 4 + pq
                                    kk = j + 1 if d == 0 else 8 - j
                                    for ri in range(2):
                                        rhs = Sbf[rh][:, uu, ri, 0:n_ch] if d == 0 else Sbf[rh][:, uu, ri, 1:n_ch + 1]
                                        S.mm(yp[P, 0:n_ch], WYq[:, uu, ri, kk, :], rhs, start=first, stop=False, inc=False, tp=(0, 32 * pq))
                                        first = False
                            for pq in range(4):
                                P = slice(32 * pq, 32 * pq + 32)
                                for jp in range(8):
                                    S.mm(yp[P, 0:n_ch], INq[:, jp * 8 + j, :], zdh[pq][:, :, jp, :], start=False, stop=(jp == 7),
                                         inc=(jp == 7 and pq == 3), tp=(0, 32 * pq))
                            S.act(yg[:, j:Lh:8], yp[:, 0:n_ch], AF.Gelu_apprx_tanh)
                if hv:
                    S.dma("pool", "ygst", c.yg_d[qh, :, s0h:s0h + Lh], yg[:, 0:Lh], writes=[("yg_d", qh, s0h)])

            for t in range(NIT + 1):
                step(t)
        S.barrier()


def phase_B2(c):
    nc, S = c.nc, c.S
    with contextlib.ExitStack() as es:
        E = es.enter_context
        sb = lambda n, sh, dt=F32: E(nc.sbuf_tensor(n, sh, dt))
        wg = sb("wg", [128, 4, DC], BF16)
        bg, gns = sb("bg", [128, 4]), sb("gns", [128, 4])
        ygf = [sb("ygf%d" % i, [128, 4, 512]) for i in range(3)]
        ygb = [sb("ygb%d" % i, [128, 4, 512], BF16) for i in range(2)]
        sg = [sb("sg%d" % i, [128, 512]) for i in range(4)]
        y2 = [sb("y2_%d" % i, [128, 4, 512]) for i in range(2)]
        sq = [sb("sqB%d" % i, [128, 512], BF16) for i in range(4)]
        var = [sb("varB%d" % i, [128, 512]) for i in range(2)]
        rst = [sb("rstB%d" % i, [128, 512]) for i in range(2)]
        y2n = [sb("y2n%d" % i, [128, 4, 512], BF16) for i in range(2)]
        ps_g = [E(nc.psum_tensor("psB_g%d" % i, [128, 512], F32)) for i in range(3)]
        ps_s = [E(nc.psum_tensor("psB_s%d" % i, [128, 512], F32)) for i in range(2)]
        for kt in range(4):
            S.dma("pool", "wgld", wg[:, kt, :], c.w_glu[kt * 128:(kt + 1) * 128, :])
        for kt in range(8):
            S.dma("pool", "wold", c.wo[:, kt, :], c.w_out[kt * 128:(kt + 1) * 128, :])
        for kt in range(22):
            S.dma("pool", "wdnld", c.wdn[:, kt, :], c.w_down[kt * 128:(kt + 1) * 128, :])
        S.dma("sp", "cB0", bg[:], c.b_glu[0, :].rearrange("(f p) -> p f", p=128), allow_slow_non_contiguous=True)
        S.dma("sp", "cB1", gns[:], c.gn_ssm[0, :].rearrange("(f p) -> p f", p=128), allow_slow_non_contiguous=True)
        tiles = []
        for (s0_, L) in c.seqs:
            for i in range(L // 512):
                tiles.append((s0_ + i * 512, s0_))
        NTI = len(tiles)

        def load(i):
            if i >= NTI:
                return
            t0, s0_ = tiles[i]
            S.dma("sp", "ygld%d" % (i % 3), ygf[i % 3][:], c.yg_d[:, :, t0:t0 + 512].rearrange("q p t -> p q t"),
                  reads=[("yg_d", q, s0_) for q in range(4)])

        wst = [sb("wupst%d" % i, [128, 2 * DFF], BF16) for i in range(2)]

        def st0(i):
            if i < 8:
                S.dma("pool", "wupst%d" % (i % 2), wst[i % 2][:], c.w_up[i * 128:(i + 1) * 128, :])
            for kt in range(4):
                S.copy("act" if kt % 2 else "dve", ygb[i % 2][:, kt, :], ygf[i % 3][:, kt, :], wk=["ygb%d_%d" % (i % 2, kt)])

        def st1(i):
            if i < 8:
                S.dma_split("pool", "wupwr%d" % (i % 2), c.wup_d[:, :, i, :].rearrange("m p c -> p m c"),
                            wst[i % 2][:].rearrange("p (m c) -> p m c", c=128), 1, 4, writes=["wup_d"])
            for m in range(4):
                pg = ps_g[m % 3]
                for kt in range(4):
                    S.mm(pg[:], wg[:, kt, m * 128:(m + 1) * 128], ygb[i % 2][:, kt, :], start=(kt == 0), stop=(kt == 3),
                         rk=["wg", "ygb%d_%d" % (i % 2, kt)])
                S.act(sg[m][:], pg[:], AF.Sigmoid, bias=bg[:, m:m + 1])
            for m in range(4):
                S.tt("dve", y2[i % 2][:, m, :], ygf[i % 3][:, m, :], sg[m][:], ALU.mult, wk=["y2_%d_%d" % (i % 2, m)])
            for m in range(4):
                S.act(sq[m][:], y2[i % 2][:, m, :], AF.Square, rk=["y2_%d_%d" % (i % 2, m)])
                S.mm(ps_s[i % 2][:], c.onesb[:], sq[m][:], start=(m == 0), stop=(m == 3), inc=True)
            load(i + 3)

        def st2(i):
            t0 = tiles[i][0]
            rsqrt_act(S, rst[i % 2][:], ps_s[i % 2][:], 1.0 / DC, var[i % 2][:], c.epsc[:])
            for m in range(4):
                S.stt(y2n[i % 2][:, m, :], y2[i % 2][:, m, :], gns[:, m:m + 1], rst[i % 2][:], ALU.mult, ALU.mult,
                      rk=["y2_%d_%d" % (i % 2, m), "gns", _nm(rst[i % 2][:])])
            S.dma("sp", "y2nst%d" % (i % 2), c.ycat_d[4:8, :, t0:t0 + 512].rearrange("f p t -> p f t"), y2n[i % 2][:],
                  writes=[("ycat_d", 1, t0)])

        for i in range(3):
            load(i)
        pipeline(NTI, [st0, st1, st2])
        for kt in range(NTI, 8):
            S.dma("pool", "wupst%d" % (kt % 2), wst[kt % 2][:], c.w_up[kt * 128:(kt + 1) * 128, :])
            S.dma_split("pool", "wupwr%d" % (kt % 2), c.wup_d[:, :, kt, :].rearrange("m p c -> p m c"),
                        wst[kt % 2][:].rearrange("p (m c) -> p m c", c=128), 1, 4, writes=["wup_d"])
        S.barrier()


def phase_C1(c):
    nc, S = c.nc, c.S
    with contextlib.ExitStack() as es:
        E = es.enter_context
        sb = lambda n, sh, dt=F32: E(nc.sbuf_tensor(n, sh, dt))
        wo = c.wo
        gpost, gffn = sb("gpost", [128, D]), sb("gffn", [128, D])
        yc = [sb("ycC%d" % i, [128, 8, 512], BF16) for i in range(2)]
        xt = [sb("xtC%d" % i, [128, 4, D]) for i in range(2)]
        junk = sb("junkC", [128, D], BF16)
        st = [sb("stC%d" % i, [128, 8]) for i in range(3)]
        tmp = [sb("tmpC%d" % i, [128, D]) for i in range(2)]
        x1 = [sb("x1C%d" % i, [128, D]) for i in range(3)]
        h2 = [sb("h2C%d" % i, [128, D], BF16) for i in range(3)]
        h2T = [sb("h2TC%d" % i, [128, 8, 512], BF16) for i in range(2)]
        ps_o = [E(nc.psum_tensor("psC_o%d" % i, [128, D], F32)) for i in range(3)]
        ps_t = [E(nc.psum_tensor("psC_t%d" % i, [128, D], BF16)) for i in range(2)]
        S.dma("sp", "cC0", gpost[:], c.post_mix_g.to_broadcast([128, D]))
        S.dma("sp", "cC1", gffn[:], c.pre_ffn_g.to_broadcast([128, D]))
        tiles = []
        for (s0_, L) in c.seqs:
            for i in range(L // 512):
                tiles.append(s0_ + i * 512)
        NTI = len(tiles)

        def load(ti):
            if ti >= NTI:
                return
            t0 = tiles[ti]
            S.dma("sp", "ycld%d" % (ti % 2), yc[ti % 2][:], c.ycat_d[:, :, t0:t0 + 512].rearrange("f p t -> p f t"),
                  reads=[("ycat_d", 0, t0), ("ycat_d", 1, t0)])
            S.dma("sp", "xldC%d" % (ti % 2), xt[ti % 2][:], c.x[t0:t0 + 512, :].rearrange("(g p) d -> p g d", p=128))

        def s0(i):
            ti, g = divmod(i, 4)
            s = ti % 2
            if g == 1:
                load(ti + 1)
            po = ps_o[i % 3]
            for hh in range(2):
                for kt in range(8):
                    S.mm(po[:, hh * 512:(hh + 1) * 512], yc[s][:, kt, g * 128:(g + 1) * 128], wo[:, kt, hh * 512:(hh + 1) * 512],
                         start=(kt == 0), stop=(kt == 7), inc=(kt == 7 and hh == 1))

        def s1(i):
            ti, g = divmod(i, 4)
            s, b3, t0 = ti % 2, i % 3, tiles[ti]
            po, sv = ps_o[b3], st[b3]
            S.act(junk[:], po[:], AF.Square, accum_out=sv[:, 0:1], wk=["junkC", "stC%d_a" % b3])
            rsqrt_act(S, sv[:, 2:3], sv[:, 0:1], 1.0 / D, sv[:, 1:2], c.epsc[:], rk=["stC%d_a" % b3], tk=["stC%d_b" % b3], wk=["stC%d_c" % b3])
            S.stt(tmp[i % 2][:], po[:], sv[:, 2:3], gpost[:], ALU.mult, ALU.mult, rk=[_nm(po[:]), "stC%d_c" % b3, "gpost"])
            S.tt("dve", x1[b3][:], tmp[i % 2][:], xt[s][:, g, :], ALU.add)
            S.dma("pool", "x1st%d" % b3, c.x1_d[t0 + g * 128:t0 + (g + 1) * 128, :], x1[b3][:], writes=[("x1_d", t0 + g * 128)])

        def s2(i):
            b3 = i % 3
            sv = st[b3]
            S.act(junk[:], x1[b3][:], AF.Square, accum_out=sv[:, 3:4], wk=["junkC", "stC%d_d" % b3])
            rsqrt_act(S, sv[:, 5:6], sv[:, 3:4], 1.0 / D, sv[:, 4:5], c.epsc[:], rk=["stC%d_d" % b3], tk=["stC%d_e" % b3], wk=["stC%d_f" % b3])
            S.stt(h2[b3][:], x1[b3][:], sv[:, 5:6], gffn[:], ALU.mult, ALU.mult, rk=[_nm(x1[b3][:]), "stC%d_f" % b3, "gffn"])

        def s3(i):
            ti, g = divmod(i, 4)
            s, b3, t0 = ti % 2, i % 3, tiles[ti]
            pt = ps_t[i % 2]
            for kt in range(8):
                S.tr(pt[:, kt * 128:(kt + 1) * 128], h2[b3][:, kt * 128:(kt + 1) * 128], c.identb[:], inc=(kt == 7))
            S.copy("act", h2T[s][:, :, g * 128:(g + 1) * 128], pt[:].rearrange("p (k t) -> p k t", k=8))
            if g == 3:
                S.dma("pool", "h2Tst%d" % s, c.h2T_d[:, :, t0:t0 + 512].rearrange("f p t -> p f t"), h2T[s][:],
                      writes=[("h2T_d", t0)])

        load(0)
        pipeline(NTI * 4, [s0, s1, s2, s3])
        S.barrier()


def phase_C2(c):
    nc, S = c.nc, c.S
    BLK = 1024
    with contextlib.ExitStack() as es:
        E = es.enter_context
        sb = lambda n, sh, dt=F32: E(nc.sbuf_tensor(n, sh, dt))
        wdn = c.wdn
        fw, fb = sb("fw", [128, NM_UP, 3]), sb("fb", [128, NM_UP])
        gpo = sb("gpo", [128, D])
        wu = [sb("wu%d" % i, [128, 8, 128], BF16) for i in range(4)]
        hT = [sb("hTF%d" % i, [128, 8, BLK + 2], BF16) for i in range(2)]
        actT = sb("actT", [128, 22, BLK], BF16)
        ag = [sb("agF%d" % i, [128, BLK]) for i in range(2)]
        av = [sb("avF%d" % i, [128, BLK]) for i in range(2)]
        sgl = [sb("sgF%d" % i, [128, BLK]) for i in range(2)]
        hv = [sb("hvF%d" % i, [128, 2]) for i in range(2)]
        x1 = [sb("x1F%d" % i, [128, D]) for i in range(2)]
        junk = sb("junkF", [128, D], BF16)
        st = [sb("stF%d" % i, [128, 4]) for i in range(2)]
        tmp = [sb("tmpF%d" % i, [128, D]) for i in range(2)]
        yo = [sb("yoF%d" % i, [128, D]) for i in range(2)]
        PS = [E(nc.psum_tensor("psF%d" % i, [128, 1024], F32)) for i in range(3)]
        PHs = [E(nc.psum_tensor("psFh%d" % i, [128, 512], F32)) for i in range(2)]
        for j in range(3):
            S.dma_split("sp", "cF0", fw[:, :, j], c.ffn_conv_w[j, :].rearrange("(m p) -> p m", p=128), 1, 11,
                        allow_slow_non_contiguous=True)
        S.dma_split("sp", "cF1", fb[:], c.ffn_conv_b[0, :].rearrange("(m p) -> p m", p=128), 1, 11, allow_slow_non_contiguous=True)
        S.dma("sp", "cF2", gpo[:], c.post_ffn_g.to_broadcast([128, D]))
        blocks = []
        for (s0, L) in c.seqs:
            nb = L // BLK
            for i in range(nb):
                blocks.append((s0 + i * BLK, i == 0, i == nb - 1))
        nwu = [0]

        def load_wu(m):
            w = wu[nwu[0] % 4]
            S.dma("sp", "wuld%d" % (nwu[0] % 4), w[:], c.wup_d[m, :, :, :], reads=["wup_d"])
            nwu[0] += 1
            return w

        def load_h(bi):
            t0, first, last = blocks[bi]
            h = hT[bi % 2]
            lo = 1 if first else 0
            hi = BLK + 1 if last else BLK + 2
            S.dma("sp", "hld%d" % (bi % 2), h[:, :, lo:hi], c.h2T_d[:, :, t0 - 1 + lo:t0 - 1 + hi].rearrange("f p t -> p f t"),
                  reads=[("h2T_d", t0 + 512 * k) for k in range(-1 if not first else 0, 3 if not last else 2)])
            if first:
                S.memset("pool", h[:, :, 0:1], 0.0)
            if last:
                S.memset("pool", h[:, :, BLK + 1:BLK + 2], 0.0)

        load_h(0)
        nps = 0
        nh = 0
        nd = 0
        for bi, (t0, first, last) in enumerate(blocks):
            h = hT[bi % 2]
            if bi + 1 < len(blocks):
                load_h(bi + 1)
            interior = (not first) or (not last)
            for mp in range(22):
                res = []
                for which, m in enumerate((mp, 22 + mp)):
                    w = load_wu(m)
                    ps = PS[nps % 3]
                    nps += 1
                    for hh in range(2):
                        for kt in range(8):
                            S.mm(ps[:, hh * 512:(hh + 1) * 512], w[:, kt, :], h[:, kt, 1 + hh * 512:1 + (hh + 1) * 512],
                                 start=(kt == 0), stop=(kt == 7), inc=(kt == 7 and hh == 1))
                    PH = PHs[nh % 2]
                    nh += 1
                    for kt in range(8):
                        S.mm(PH[:, 0:2], w[:, kt, :], h[:, kt, 0:BLK + 2:BLK + 1], start=(kt == 0), stop=(kt == 7))
                    a = (ag if which == 0 else av)[mp % 2]
                    S.act(a[:], ps[:], AF.Identity, scale=fw[:, m, 1:2], bias=fb[:, m:m + 1])
                    S.stt(a[:, 1:BLK], ps[:, 0:BLK - 1], fw[:, m, 0:1], a[:, 1:BLK], ALU.mult, ALU.add)
                    S.stt(a[:, 0:BLK - 1], ps[:, 1:BLK], fw[:, m, 2:3], a[:, 0:BLK - 1], ALU.mult, ALU.add)
                    hvv = hv[which]
                    S.tt("dve", hvv[:], PH[:, 0:2], fw[:, m, 0:3:2], ALU.mult)
                    S.tt("dve", a[:, 0:BLK:BLK - 1], a[:, 0:BLK:BLK - 1], hvv[:], ALU.add)
                    res.append(a)
                g = sgl[mp % 2]
                S.act(g[:], res[0][:], AF.Silu)
                S.tt("pool", actT[:, mp, :], g[:], res[1][:], ALU.mult, wk=["actT_%d" % mp])
            for tg in range(BLK // 128):
                b = nd % 2
                nd += 1
                PD = PS[nps % 3]
                nps += 1
                tt0 = t0 + tg * 128
                S.dma("sp", "x1ld%d" % b, x1[b][:], c.x1_d[tt0:tt0 + 128, :], reads=[("x1_d", tt0)])
                for hh in range(2):
                    for kt in range(22):
                        S.mm(PD[:, hh * 512:(hh + 1) * 512], actT[:, kt, tg * 128:(tg + 1) * 128], wdn[:, kt, hh * 512:(hh + 1) * 512],
                             start=(kt == 0), stop=(kt == 21), inc=(kt == 21 and hh == 1), rk=["actT_%d" % kt, "wdn"])
                sv = st[b]
                S.act(junk[:], PD[:], AF.Square, accum_out=sv[:, 0:1], wk=["junkF", "stF%d_a" % b])
                rsqrt_act(S, sv[:, 2:3], sv[:, 0:1], 1.0 / D, sv[:, 1:2], c.epsc[:], rk=["stF%d_a" % b], tk=["stF%d_b" % b], wk=["stF%d_c" % b])
                S.stt(tmp[b][:], PD[:], sv[:, 2:3], gpo[:], ALU.mult, ALU.mult, rk=[_nm(PD[:]), "stF%d_c" % b, "gpo"])
                S.tt("pool", yo[b][:], tmp[b][:], x1[b][:], ALU.add)
                S.dma("pool", "yst%d" % b, c.y[tt0:tt0 + 128, :], yo[b][:], writes=[("y", tt0)])
        S.barrier()


SEQ_LENS = (2048, 2048, 4096, 4096)
_CONSTS = {
    "ident": np.eye(128, dtype=np.float32),
    "mask32": (np.arange(128)[:, None] % 32 == np.arange(32)[None, :]).astype(np.float32),
}
_WKEYS = ["pre_mix_g", "w_in", "conv_w", "lam_re", "lam_im", "log_step", "b_re", "b_im", "c_re", "c_im", "d_skip",
          "w_glu", "b_glu", "gn_conv", "gn_ssm", "w_out", "post_mix_g", "pre_ffn_g", "w_up", "ffn_conv_w",
          "ffn_conv_b", "w_down", "post_ffn_g"]


def _weights_map(inputs):
    m = {}
    for k in _WKEYS:
        a = np.asarray(inputs[k], dtype=np.float32)
        a = a[0]
        if a.ndim == 1:
            a = a[None, :]
        m[k] = np.ascontiguousarray(a)
    m.update(_CONSTS)
    return m


def kernel(**inputs):
    xp = np.asarray(inputs["x_prompt"], dtype=np.float32)
    xs = np.asarray(inputs["x_sample"], dtype=np.float32)
    n = 8
    nc = build(SEQ_LENS)
    wm = _weights_map(inputs)
    in_maps = []
    for i in range(n):
        xc = np.concatenate([xp[2 * i].reshape(-1, D), xp[2 * i + 1].reshape(-1, D),
                             xs[2 * i].reshape(-1, D), xs[2 * i + 1].reshape(-1, D)], axis=0)
        d = dict(wm)
        d["x"] = np.ascontiguousarray(xc)
        in_maps.append(d)
    res = run_bass_kernel_spmd(nc, in_maps, core_ids=list(range(n)))
    yp = np.empty_like(xp)
    ys = np.empty_like(xs)
    for i in range(n):
        y = res.results[i]["y"]
        yp[2 * i] = y[0:2048]
        yp[2 * i + 1] = y[2048:4096]
        ys[2 * i] = y[4096:8192]
        ys[2 * i + 1] = y[8192:12288]
    return (yp, ys)
```

```python
import contextlib
import math
import numpy as np
import concourse.bass as bass
import concourse.mybir as mybir
from concourse.bass_utils import run_bass_kernel_spmd

F32, BF16 = mybir.dt.float32, mybir.dt.bfloat16
AF, ALU = mybir.ActivationFunctionType, mybir.AluOpType

D = 1024
DC = 512
DIN = 2048
DFF = 2816
NM_UP = 44
EPS = 1e-6
MAGIC = 12582912.0
TWO_PI = 2.0 * math.pi


def _nm(ap):
    return ap.name


class Sched:
    def __init__(self, nc, es):
        self.nc, self.es = nc, es
        self.engs = {"pe": nc.tensor, "act": nc.scalar, "dve": nc.vector, "pool": nc.gpsimd, "sp": nc.sync}
        self.sems, self.cnt = {}, {}
        self.seen = {e: {} for e in self.engs}
        self.lastw, self.readers = {}, {}

    def _sem(self, key):
        if key not in self.sems:
            self.sems[key] = self.es.enter_context(self.nc.semaphore("s%d" % len(self.sems)))
            self.cnt[key] = 0
        return self.sems[key]

    def _wait(self, eng, deps):
        for (k, v) in deps:
            if eng == "pe" and k == "pe":
                continue
            if self.seen[eng].get(k, 0) < v:
                self.engs[eng].wait_ge(self._sem(k), v)
                self.seen[eng][k] = v

    def _deps(self, reads, writes):
        deps = []
        for b in reads:
            if b in self.lastw:
                deps.append(self.lastw[b])
        for b in writes:
            if b in self.lastw:
                deps.append(self.lastw[b])
            deps.extend(self.readers.get(b, {}).items())
        return deps

    def _record(self, ev, reads, writes):
        for b in writes:
            self.lastw[b] = ev
            self.readers[b] = {}
        for b in reads:
            r = self.readers.setdefault(b, {})
            if r.get(ev[0], 0) < ev[1]:
                r[ev[0]] = ev[1]

    def op(self, eng, fn, reads=(), writes=(), inc=True):
        self._wait(eng, self._deps(reads, writes))
        ins = fn(self.engs[eng])
        self._sem(eng)
        if inc:
            self.cnt[eng] += 1
            ins.then_inc(self.sems[eng], 1)
            ev = (eng, self.cnt[eng])
        else:
            ev = (eng, self.cnt[eng] + 1)
        self._record(ev, reads, writes)
        return ev

    def dma(self, q, semkey, out, in_, reads=None, writes=None, **kw):
        reads = [_nm(in_)] if reads is None else reads
        writes = [_nm(out)] if writes is None else writes
        self._wait(q, self._deps(reads, writes))
        s = self._sem(semkey)
        self.cnt[semkey] += 16
        self.engs[q].dma_start(out=out, in_=in_, **kw).then_inc(s, 16)
        ev = (semkey, self.cnt[semkey])
        self._record(ev, reads, writes)
        return ev

    def dma_split(self, q, semkey, out, in_, axis, nchunks, **kw):
        n = out.shape[axis]
        step = -(-n // nchunks)
        ev = None
        for lo in range(0, n, step):
            hi = min(n, lo + step)
            idx = [slice(None)] * len(out.shape)
            idx[axis] = slice(lo, hi)
            ev = self.dma(q, semkey, out[tuple(idx)], in_[tuple(idx)], **kw)
        return ev

    def wait_all(self, eng):
        mx = {}
        for (k, v) in self.lastw.values():
            mx[k] = max(mx.get(k, 0), v)
        for r in self.readers.values():
            for k, v in r.items():
                mx[k] = max(mx.get(k, 0), v)
        self._wait(eng, [(k, min(v, self.cnt[k])) for k, v in mx.items() if k != eng or eng != "pe"])

    def barrier(self):
        for e in self.engs:
            self.wait_all(e)
        self.lastw, self.readers = {}, {}

    def tt(self, eng, out, in0, in1, op, rk=None, wk=None):
        return self.op(eng, lambda e: e.tensor_tensor(out=out, in0=in0, in1=in1, op=op),
                       reads=rk if rk is not None else [_nm(in0), _nm(in1)], writes=wk if wk is not None else [_nm(out)])

    def ts(self, eng, out, in0, s1, s2, op0, op1=None, rk=None, wk=None):
        r = [_nm(in0)] + [_nm(s) for s in (s1, s2) if hasattr(s, "name")]
        if op1 is None:
            f = lambda e: e.tensor_scalar(out=out, in0=in0, scalar1=s1, scalar2=None, op0=op0)
        else:
            f = lambda e: e.tensor_scalar(out=out, in0=in0, scalar1=s1, scalar2=s2, op0=op0, op1=op1)
        return self.op(eng, f, reads=rk if rk is not None else r, writes=wk if wk is not None else [_nm(out)])

    def stt(self, out, in0, scalar, in1, op0, op1, rk=None, wk=None):
        r = [_nm(in0), _nm(in1)] + ([_nm(scalar)] if hasattr(scalar, "name") else [])
        return self.op("dve", lambda e: e.scalar_tensor_tensor(out=out, in0=in0, scalar=scalar, in1=in1, op0=op0, op1=op1),
                       reads=rk if rk is not None else r, writes=wk if wk is not None else [_nm(out)])

    def act(self, out, in_, func, bias=None, scale=None, accum_out=None, rk=None, wk=None):
        kw = {}
        r = [_nm(in_)]
        w = [_nm(out)]
        if bias is not None:
            kw["bias"] = bias
            if hasattr(bias, "name"):
                r.append(_nm(bias))
        if scale is not None:
            kw["scale"] = scale
            if hasattr(scale, "name"):
                r.append(_nm(scale))
        if accum_out is not None:
            kw["accum_out"] = accum_out
            w.append(_nm(accum_out))
        return self.op("act", lambda e: e.activation(out=out, in_=in_, func=func, **kw),
                       reads=rk if rk is not None else r, writes=wk if wk is not None else w)

    def copy(self, eng, out, in_, rk=None, wk=None):
        if eng == "act":
            return self.act(out, in_, AF.Copy, rk=rk, wk=wk)
        return self.op(eng, lambda e: e.tensor_copy(out=out, in_=in_),
                       reads=rk if rk is not None else [_nm(in_)], writes=wk if wk is not None else [_nm(out)])

    def memset(self, eng, out, val, wk=None):
        return self.op(eng, lambda e: e.memset(out, val), reads=[], writes=wk if wk is not None else [_nm(out)])

    def mm(self, out, lhsT, rhs, start, stop, inc=None, tp=None, rk=None, wk=None):
        inc = stop if inc is None else inc
        kw = {} if tp is None else {"tile_position": tp}
        return self.op("pe", lambda e: e.matmul(out, lhsT=lhsT, rhs=rhs, start=start, stop=stop, **kw),
                       reads=rk if rk is not None else [_nm(lhsT), _nm(rhs)],
                       writes=wk if wk is not None else [_nm(out)], inc=inc)

    def tr(self, out, in_, ident, inc=True, tp=None, rk=None, wk=None):
        kw = {} if tp is None else {"tile_position": tp}
        return self.op("pe", lambda e: e.transpose(out, in_, ident, **kw),
                       reads=rk if rk is not None else [_nm(in_), _nm(ident)],
                       writes=wk if wk is not None else [_nm(out)], inc=inc)

    def scan(self, out, d0, d1, init=0.0):
        return self.op("dve", lambda e: e.tensor_tensor_scan(out=out, data0=d0, data1=d1, initial=init,
                                                             op0=ALU.mult, op1=ALU.add),
                       reads=[_nm(d0), _nm(d1)], writes=[_nm(out)])


class Ctx:
    pass


def pipeline(n, stages):
    for t in range(n + len(stages) - 1):
        for k, st in enumerate(stages):
            i = t - k
            if 0 <= i < n:
                st(i)


def rsqrt_act(S, out, in_, scale, tmp, eps_ap, rk=None, wk=None, tk=None):
    S.act(tmp, in_, AF.Ln, scale=scale, bias=eps_ap, rk=rk, wk=tk)
    S.act(out, tmp, AF.Exp, scale=-0.5, rk=tk, wk=wk)


def build(seq_lens, debug=False):
    NT = sum(seq_lens)
    assert all(L % 1024 == 0 for L in seq_lens)
    nc = bass.Bass("TRN2", target_bir_lowering=False)
    c = Ctx()
    c.nc, c.NT, c.seq_lens, c.debug = nc, NT, seq_lens, debug
    c.seqs = []
    t = 0
    for L in seq_lens:
        c.seqs.append((t, L))
        t += L

    def din(name, shape):
        return nc.dram_tensor(name, list(shape), F32, kind="ExternalInput").ap()

    c.x = din("x", [NT, D])
    c.pre_mix_g = din("pre_mix_g", [1, D])
    c.w_in = din("w_in", [D, DIN])
    c.conv_w = din("conv_w", [3, DC])
    c.lam_re = din("lam_re", [2, 32, 64])
    c.lam_im = din("lam_im", [2, 32, 64])
    c.log_step = din("log_step", [2, 32])
    c.b_re = din("b_re", [2, 32, 64, 16])
    c.b_im = din("b_im", [2, 32, 64, 16])
    c.c_re = din("c_re", [2, 32, 16, 64])
    c.c_im = din("c_im", [2, 32, 16, 64])
    c.d_skip = din("d_skip", [1, DC])
    c.w_glu = din("w_glu", [DC, DC])
    c.b_glu = din("b_glu", [1, DC])
    c.gn_conv = din("gn_conv", [1, DC])
    c.gn_ssm = din("gn_ssm", [1, DC])
    c.w_out = din("w_out", [D, D])
    c.post_mix_g = din("post_mix_g", [1, D])
    c.pre_ffn_g = din("pre_ffn_g", [1, D])
    c.w_up = din("w_up", [D, 2 * DFF])
    c.ffn_conv_w = din("ffn_conv_w", [3, 2 * DFF])
    c.ffn_conv_b = din("ffn_conv_b", [1, 2 * DFF])
    c.w_down = din("w_down", [DFF, D])
    c.post_ffn_g = din("post_ffn_g", [1, D])
    c.ident = din("ident", [128, 128])
    c.mask32 = din("mask32", [128, 32])
    c.y = nc.dram_tensor("y", [NT, D], F32, kind="ExternalOutput").ap()

    sk = "ExternalOutput" if debug else "Internal"
    c.zuT_d = nc.dram_tensor("zuT_d", [4, 128, NT], BF16, kind=sk).ap()
    c.ycat_d = nc.dram_tensor("ycat_d", [8, 128, NT], BF16, kind=sk).ap()
    c.yg_d = nc.dram_tensor("yg_d", [4, 128, NT], F32, kind=sk).ap()
    c.x1_d = nc.dram_tensor("x1_d", [NT, D], F32, kind=sk).ap()
    c.h2T_d = nc.dram_tensor("h2T_d", [8, 128, NT], BF16, kind=sk).ap()
    c.wup_d = nc.dram_tensor("wup_d", [NM_UP, 128, 8, 128], BF16, kind="Internal").ap()
    c.WX_d = nc.dram_tensor("WX_d", [4, 128, 4096], BF16, kind="Internal").ap()
    c.WY_d = nc.dram_tensor("WY_d", [32, 128, 576], BF16, kind="Internal").ap()
    c.IN_d = nc.dram_tensor("IN_d", [4, 128, 2048], BF16, kind="Internal").ap()

    with contextlib.ExitStack() as es:
        S = Sched(nc, es)
        c.S = S
        c.es = es
        E = es.enter_context
        c.identb = E(nc.sbuf_tensor("identb", [128, 128], BF16))
        c.identf = E(nc.sbuf_tensor("identf", [128, 128], F32))
        c.onesb = E(nc.sbuf_tensor("onesb", [128, 128], BF16))
        S.dma("pool", "c0", c.identb[:], c.ident[:, :])
        S.dma("sp", "c1", c.identf[:], c.ident[:, :])
        S.memset("dve", c.onesb[:], 1.0)
        c.epsc = E(nc.sbuf_tensor("epsc", [128, 1], F32))
        S.memset("dve", c.epsc[:], EPS)
        S.barrier()
        gsb = lambda n, sh: E(nc.sbuf_tensor(n, sh, F32))
        c.lamre, c.lamim, c.lst = gsb("lamre", [128, 32]), gsb("lamim", [128, 32]), gsb("lst", [128, 32])
        c.dsk, c.m32 = gsb("dsk", [128, 4]), gsb("m32", [128, 32])
        S.dma_split("act", "p0", c.lamre[:].rearrange("p (d r) -> p d r", d=2), c.lam_re.rearrange("d (r q) p -> (q p) d r", q=2),
                    0, 4, allow_slow_non_contiguous=True)
        S.dma_split("act", "p1", c.lamim[:].rearrange("p (d r) -> p d r", d=2), c.lam_im.rearrange("d (r q) p -> (q p) d r", q=2),
                    0, 4, allow_slow_non_contiguous=True)
        for q in range(2):
            src = c.log_step.rearrange("d (r q) -> q d r", q=2)[q:q + 1]
            S.dma_split("act", "p2", c.lst[64 * q:64 * q + 64, :].rearrange("p (d r) -> p d r", d=2), src.to_broadcast([64, 2, 16]),
                        0, 2, allow_slow_non_contiguous=True)
        S.dma("act", "p7", c.dsk[:], c.d_skip[0, :].rearrange("(f p) -> p f", p=128), allow_slow_non_contiguous=True)
        S.dma("act", "p8", c.m32[:], c.mask32[:, :])
        stage = c.stage = ("ALL" if not debug else debug)
        phase_A(c)
        if stage in ("ALL", "B1", "B2", "C1", "C2"):
            phase_B1(c)
        if stage in ("ALL", "B2", "C1", "C2"):
            c.wo = E(nc.sbuf_tensor("wo", [128, 8, D], BF16))
            c.wdn = E(nc.sbuf_tensor("wdn", [128, 22, D], BF16))
            phase_B2(c)
        if stage in ("ALL", "C1", "C2"):
            phase_C1(c)
        if stage in ("ALL", "C2"):
            phase_C2(c)
        for e in ("sp", "pool", "act", "dve", "pe"):
            S.wait_all(e)
    return nc


def phase_prep_wup(c):
    nc, S = c.nc, c.S
    with contextlib.ExitStack() as es:
        E = es.enter_context
        st = [E(nc.sbuf_tensor("wupst%d" % i, [128, 2 * DFF], BF16)) for i in range(2)]
        for kt in range(8):
            s = st[kt % 2]
            S.dma("pool", "wupst%d" % (kt % 2), s[:], c.w_up[kt * 128:(kt + 1) * 128, :])
            S.dma("sp", "wupwr%d" % (kt % 2), c.wup_d[:, :, kt, :].rearrange("m p c -> p m c"),
                  s[:].rearrange("p (m c) -> p m c", c=128), writes=["wup_d"])
        S.barrier()


def phase_A(c):
    nc, S = c.nc, c.S
    NT = c.NT
    with contextlib.ExitStack() as es:
        E = es.enter_context
        sb = lambda n, sh, dt: E(nc.sbuf_tensor(n, sh, dt))
        w_in_sb = sb("w_in_sb", [128, 8, DIN], BF16)
        gpre = sb("gpre", [128, D], F32)
        cw = sb("cw", [128, 4, 3], F32)
        gnc = sb("gnc", [128, 4], F32)
        xt = [sb("xt%d" % i, [128, 4, D], F32) for i in range(2)]
        junk = sb("junkA", [128, D], BF16)
        junk2 = sb("junkA2", [128, D], BF16)
        ss = [sb("ssA%d" % i, [128, 4], F32) for i in range(2)]
        var4 = [sb("var4A%d" % i, [128, 4], F32) for i in range(2)]
        rstd = [sb("rstdA%d" % i, [128, 4], F32) for i in range(2)]
        xn = [sb("xn%d" % i, [128, D], BF16) for i in range(4)]
        hT = [sb("hT%d" % i, [128, 8, 512], BF16) for i in range(2)]
        pbuf = [sb("pbuf%d" % i, [128, 4, 514], F32) for i in range(2)]
        zbb = [sb("zbb%d" % i, [128, 4, 512], F32) for i in range(2)]
        zcs = [sb("zcs%d" % i, [128, 512], F32) for i in range(4)]
        zus = [sb("zus%d" % i, [128, 4, 512], BF16) for i in range(2)]
        acc = [sb("accA%d" % i, [128, 512], F32) for i in range(4)]
        ycs = [sb("ycA%d" % i, [128, 4, 512], F32) for i in range(2)]
        sq = [sb("sqA%d" % i, [128, 512], BF16) for i in range(8)]
        var = sb("varA", [128, 512], F32)
        rst = sb("rstA", [128, 512], F32)
        ycn = [sb("ycn%d" % i, [128, 4, 512], BF16) for i in range(2)]
        ps_t = [E(nc.psum_tensor("psA_t%d" % i, [128, D], BF16)) for i in range(2)]
        ps_z = [E(nc.psum_tensor("psA_z%d" % i, [128, 512], F32)) for i in range(5)]
        ps_s = E(nc.psum_tensor("psA_s", [128, 512], F32))

        for cb in range(4):
            S.dma("pool", "winld%d" % cb, w_in_sb[:, :, cb * 512:(cb + 1) * 512],
                  c.w_in[:, cb * 512:(cb + 1) * 512].rearrange("(k p) n -> p k n", p=128), writes=["w_in_sb_%d" % cb])
        S.dma("sp", "cA0", gpre[:], c.pre_mix_g.to_broadcast([128, D]))
        for j in range(3):
            S.dma("sp", "cA1", cw[:, :, j], c.conv_w[j, :].rearrange("(f p) -> p f", p=128), allow_slow_non_contiguous=True)
        S.dma("sp", "cA2", gnc[:], c.gn_conv[0, :].rearrange("(f p) -> p f", p=128), allow_slow_non_contiguous=True)

        tiles = []
        for (s0, L) in c.seqs:
            n = L // 512
            for i in range(n):
                tiles.append((s0 + i * 512, i == 0, i == n - 1))

        def load_x(i):
            t0 = tiles[i][0]
            S.dma("sp", "xld%d" % (i % 2), xt[i % 2][:], c.x[t0:t0 + 512, :].rearrange("(g p) d -> p g d", p=128))

        def fA1(i, g):
            s = i % 2
            if g % 2 == 0:
                S.act(junk[:], xt[s][:, g, :], AF.Square, accum_out=ss[s][:, g:g + 1], wk=["junkA", "ssA%d_%d" % (s, g)])
            else:
                S.op("dve", lambda e: e.scalar_tensor_tensor(out=junk2[:], in0=xt[s][:, g, :], scalar=1.0, in1=xt[s][:, g, :],
                                                               op0=ALU.mult, op1=ALU.mult, accum_out=ss[s][:, g:g + 1]),
                     reads=[_nm(xt[s][:])], writes=["junkA2", "ssA%d_%d" % (s, g)])
            rsqrt_act(S, rstd[s][:, g:g + 1], ss[s][:, g:g + 1], 1.0 / D, var4[s][:, g:g + 1], c.epsc[:],
                      rk=["ssA%d_%d" % (s, g)], tk=["var4A%d_%d" % (s, g)], wk=["rstdA%d_%d" % (s, g)])
            S.stt(xn[g][:], xt[s][:, g, :], rstd[s][:, g:g + 1], gpre[:], ALU.mult, ALU.mult,
                  rk=[_nm(xt[s][:]), "rstdA%d_%d" % (s, g), "gpre"])

        def fA2(i, g):
            pt = ps_t[g % 2]
            for kt in range(8):
                S.tr(pt[:, kt * 128:(kt + 1) * 128], xn[g][:, kt * 128:(kt + 1) * 128], c.identb[:], inc=(kt == 7))

        def fA3(i, g):
            s = i % 2
            S.copy("act", hT[s][:, :, g * 128:(g + 1) * 128], ps_t[g % 2][:].rearrange("p (k t) -> p k t", k=8),
                   wk=["hT%d_%d" % (s, g)])

        def front_items(i):
            it = [(lambda g=g: fA1(i, g)) for g in range(4)]
            for g in range(4):
                it.append(lambda g=g: fA2(i, g))
                it.append(lambda g=g: fA3(i, g))
            return it

        def c1(i, f):
            S.act(acc[f][:], pbuf[i % 2][:, f, 1:513], AF.Identity, scale=cw[:, f, 1:2])

        def c2(i, f):
            s = i % 2
            yc = ycs[i % 2]
            S.stt(acc[f][:], pbuf[s][:, f, 0:512], cw[:, f, 0:1], acc[f][:], ALU.mult, ALU.add)
            S.stt(acc[f][:], pbuf[s][:, f, 2:514], cw[:, f, 2:3], acc[f][:], ALU.mult, ALU.add)
            S.tt("dve", yc[:, f, :], acc[f][:], zbb[s][:, f, :], ALU.mult, wk=["ycA%d_%d" % (i % 2, f)])

        def c3(i, f):
            yc = ycs[i % 2]
            q_ = sq[(i % 2) * 4 + f]
            S.act(q_[:], yc[:, f, :], AF.Square, rk=["ycA%d_%d" % (i % 2, f)])
            pe_late.append(lambda: S.mm(ps_s[:], c.onesb[:], q_[:], start=(f == 0), stop=(f == 3), inc=True))
            if f == 3:
                pe_late.append(lambda: c4(i))

        def c4(i):
            t0 = tiles[i][0]
            yc = ycs[i % 2]
            rsqrt_act(S, rst[:], ps_s[:], 1.0 / DC, var[:], c.epsc[:])
            yn = ycn[i % 2]
            for ff in range(4):
                S.stt(yn[:, ff, :], yc[:, ff, :], gnc[:, ff:ff + 1], rst[:], ALU.mult, ALU.mult,
                      rk=["ycA%d_%d" % (i % 2, ff), "gnc", "rstA"])
            S.dma("pool", "ycnst%d" % (i % 2), c.ycat_d[0:4, :, t0:t0 + 512].rearrange("f p t -> p f t"), yn[:],
                  writes=[("ycat_d", 0, t0)])

        def conv_items(i):
            order = [("c1", 0), ("c1", 1), ("c2", 0), ("c1", 2), ("c2", 1), ("c3", 0), ("c1", 3), ("c2", 2), ("c3", 1),
                     ("c2", 3), ("c3", 2), ("c3", 3)]
            fn = {"c1": c1, "c2": c2, "c3": c3}
            return [(lambda k=k, f=f: fn[k](i, f)) for (k, f) in order]

        pe_late = []
        load_x(0)
        for w_ in front_items(0):
            w_()
        bg = []
        for i, (t0, first, last) in enumerate(tiles):
            s = i % 2
            if i + 1 < len(tiles):
                load_x(i + 1)
                bg.extend(front_items(i + 1))
            def evac(m, s=s, first=first, last=last, i=i):
                pz = ps_z[m % 5]
                f = m % 4
                if m < 4:
                    S.copy("dve", zbb[s][:, f, :], pz[:])
                elif m < 8:
                    S.copy("dve", zcs[f][:], pz[:])
                elif m < 12:
                    S.tt("dve", pbuf[s][:, f, 1:513], pz[:], zcs[f][:], ALU.mult)
                else:
                    S.copy("act", zus[s][:, f, :].rearrange("p (j c) -> p c j", j=8), pz[:].rearrange("p (c j) -> p c j", j=8))
                if m == 11:
                    if first:
                        S.memset("pool", pbuf[s][:, :, 0:1], 0.0)
                    else:
                        S.copy("pool", pbuf[s][:, :, 0:1], pbuf[1 - s][:, :, 512:513])
                        S.copy("pool", pbuf[1 - s][:, :, 513:514], pbuf[s][:, :, 1:2])
                        bg.extend(conv_items(i - 1))
                    if last:
                        S.memset("pool", pbuf[s][:, :, 513:514], 0.0)

            for m in range(16):
                pz = ps_z[m % 5]
                for kt in range(8):
                    S.mm(pz[:], w_in_sb[:, kt, m * 128:(m + 1) * 128], hT[s][:, kt, :], start=(kt == 0), stop=(kt == 7),
                         rk=["w_in_sb_%d" % (m // 4)] + ["hT%d_%d" % (s, g) for g in range(4)])
                if m >= 1:
                    evac(m - 1)
                if m == 3:
                    while pe_late:
                        pe_late.pop(0)()
                npop = -(-len(bg) // (16 - m)) if m >= 12 else min(len(bg), 2)
                for _ in range(npop):
                    bg.pop(0)()
            evac(15)
            S.dma("pool", "zust%d" % s, c.zuT_d[:, :, t0:t0 + 512].rearrange("q p t -> p q t"), zus[s][:],
                  writes=[("zuT_d", t0)])
            while bg:
                bg.pop(0)()
            if last:
                bg.extend(conv_items(i))
        while bg:
            bg.pop(0)()
        while pe_late:
            pe_late.pop(0)()
        S.barrier()


def cmul(S, eng, o_re, o_im, x_re, x_im, y_re, y_im, t1, t2, conj_x=False):
    S.tt(eng, t1, x_re, y_re, ALU.mult)
    S.tt(eng, t2, x_im, y_im, ALU.mult)
    S.tt(eng, o_re, t1, t2, ALU.add if conj_x else ALU.subtract)
    S.tt(eng, t1, x_re, y_im, ALU.mult)
    S.tt(eng, t2, x_im, y_re, ALU.mult)
    S.tt(eng, o_im, t1, t2, ALU.subtract if conj_x else ALU.add)


def phase_B1(c):
    nc, S = c.nc, c.S
    with contextlib.ExitStack() as es:
        E = es.enter_context
        sb = lambda n, sh, dt: E(nc.sbuf_tensor(n, sh, dt))
        U8r = sb("U8r", [128, 32], F32)
        U8i = sb("U8i", [128, 32], F32)
        R8 = sb("R8", [128, 32], F32)
        with contextlib.ExitStack() as es1:
            WX = es1.enter_context(nc.sbuf_tensor("WX", [128, 4, 2, 2, 8, 128], BF16))
            WY = es1.enter_context(nc.sbuf_tensor("WY", [128, 32, 2, 9, 32], BF16))
            INTRA = es1.enter_context(nc.sbuf_tensor("INTRA", [128, 4, 64, 32], BF16))
            s5_prep(c, WX, WY, INTRA, U8r, U8i, R8)
            S.dma("sp", "spl0", c.WX_d.rearrange("q p x -> p q x"), WX[:].rearrange("p q a b e n -> p q (a b e n)"), writes=["WX_d"])
            S.dma_split("sp", "spl1", c.WY_d.rearrange("u p x -> p u x"), WY[:].rearrange("p u a k n -> p u (a k n)"), 1, 4,
                        writes=["WY_d"])
            S.dma("sp", "spl2", c.IN_d.rearrange("q p x -> p q x"), INTRA[:].rearrange("p q j n -> p q (j n)"), writes=["IN_d"])
            S.barrier()
        s5_main(c, U8r, U8i, R8)


def s5_prep(c, WX, WY, INTRA, U8r, U8i, R8):
    nc, S = c.nc, c.S
    with contextlib.ExitStack() as es:
        E = es.enter_context
        sb = lambda n, sh, dt=F32: E(nc.sbuf_tensor(n, sh, dt))
        lamre, lamim, lst, dsk, m32 = c.lamre, c.lamim, c.lst, c.dsk, c.m32
        Bre, Bim = sb("Bre", [128, 32, 16]), sb("Bim", [128, 32, 16])
        S.dma_split("sp", "p3", Bre[:].rearrange("p (d r) h -> p d r h", d=2), c.b_re.rearrange("d (r q) p h -> (q p) d r h", q=2), 0, 4)
        S.dma_split("sp", "p4", Bim[:].rearrange("p (d r) h -> p d r h", d=2), c.b_im.rearrange("d (r q) p h -> (q p) d r h", q=2), 0, 4)
        Cre, Cim = sb("Cre", [128, 32, 16]), sb("Cim", [128, 32, 16])
        with contextlib.ExitStack() as es2:
            Cnr = es2.enter_context(nc.sbuf_tensor("Cnr", [16, 64, 64], F32))
            Cni = es2.enter_context(nc.sbuf_tensor("Cni", [16, 64, 64], F32))
            psC = [es2.enter_context(nc.psum_tensor("psC%d" % i, [128, 32, 16], F32)) for i in range(2)]
            S.dma("sp", "p5", Cnr[:], c.c_re.rearrange("d g h p -> h (d g) p"))
            S.dma("sp", "p6", Cni[:], c.c_im.rearrange("d g h p -> h (d g) p"))
            for ri, (Cn, Cst) in enumerate(((Cnr, Cre), (Cni, Cim))):
                for u in range(32):
                    d, pr = divmod(u, 16)
                    a = d * 32 + 2 * pr
                    S.tr(psC[ri][:, u, :], Cn[:, a:a + 2, :].rearrange("h g p -> h (g p)"), c.identf[0:16, 0:16], inc=(u == 31))
                S.copy("dve", Cst[:], psC[ri][:])
            S.barrier()

        dt, lr, th = sb("dtp", [128, 32]), sb("lr", [128, 32]), sb("th", [128, 32])
        mag, kf, thr = sb("mag", [128, 32]), sb("kf", [128, 32]), sb("thr", [128, 32])
        sh, ch, t1, t2 = sb("sh", [128, 32]), sb("ch", [128, 32]), sb("t1p", [128, 32]), sb("t2p", [128, 32])
        cs, sn = sb("cs", [128, 32]), sb("sn", [128, 32])
        hpi = sb("hpi", [128, 1])
        S.memset("dve", hpi[:], math.pi / 2)
        S.act(dt[:], lst[:], AF.Exp)
        S.tt("dve", lr[:], lamre[:], dt[:], ALU.mult)
        S.tt("dve", th[:], lamim[:], dt[:], ALU.mult)
        S.act(mag[:], lr[:], AF.Exp)
        S.act(R8[:], lr[:], AF.Exp, scale=8.0)
        S.ts("dve", kf[:], th[:], 1.0 / TWO_PI, MAGIC, ALU.mult, ALU.add)
        S.ts("dve", kf[:], kf[:], -MAGIC, -TWO_PI, ALU.add, ALU.mult)
        S.tt("dve", thr[:], th[:], kf[:], ALU.add)
        S.act(sh[:], thr[:], AF.Sin, scale=0.5)
        S.act(ch[:], thr[:], AF.Sin, scale=0.5, bias=hpi[:])
        S.tt("dve", t1[:], ch[:], ch[:], ALU.mult)
        S.tt("dve", t2[:], sh[:], sh[:], ALU.mult)
        S.tt("dve", cs[:], t1[:], t2[:], ALU.subtract)
        S.tt("dve", t1[:], sh[:], ch[:], ALU.mult)
        S.ts("dve", sn[:], t1[:], 2.0, None, ALU.mult)
        UPr, UPi = sb("UPr", [128, 32, 9]), sb("UPi", [128, 32, 9])
        APr, APi = sb("APr", [128, 32, 9]), sb("APi", [128, 32, 9])
        T1, T2 = sb("T1p", [128, 32, 4]), sb("T2p", [128, 32, 4])
        S.memset("dve", UPr[:, :, 0:1], 1.0)
        S.memset("dve", UPi[:, :, 0:1], 0.0)
        S.copy("dve", UPr[:, :, 1], cs[:])
        S.copy("dve", UPi[:, :, 1], sn[:])
        cmul(S, "dve", UPr[:, :, 2], UPi[:, :, 2], UPr[:, :, 1], UPi[:, :, 1], UPr[:, :, 1], UPi[:, :, 1], T1[:, :, 0], T2[:, :, 0])
        bc = lambda ap, n: ap.to_broadcast([128, 32, n])
        cmul(S, "dve", UPr[:, :, 3:5], UPi[:, :, 3:5], UPr[:, :, 1:3], UPi[:, :, 1:3], bc(UPr[:, :, 2:3], 2), bc(UPi[:, :, 2:3], 2),
             T1[:, :, 0:2], T2[:, :, 0:2])
        cmul(S, "dve", UPr[:, :, 5:9], UPi[:, :, 5:9], UPr[:, :, 1:5], UPi[:, :, 1:5], bc(UPr[:, :, 4:5], 4), bc(UPi[:, :, 4:5], 4),
             T1[:], T2[:])
        S.copy("dve", U8r[:], UPr[:, :, 8])
        S.copy("dve", U8i[:], UPi[:, :, 8])
        MG = sb("MG", [128, 32, 9])
        S.memset("dve", MG[:, :, 0:1], 1.0)
        for k in range(1, 9):
            S.act(MG[:, :, k], lr[:], AF.Exp, scale=float(k))
        S.tt("dve", APr[:], UPr[:], MG[:], ALU.mult)
        S.tt("dve", APi[:], UPi[:], MG[:], ALU.mult)
        nr, den, inv = sb("nr", [128, 32]), sb("den", [128, 32]), sb("inv", [128, 32])
        qr, qi = sb("qr", [128, 32]), sb("qi", [128, 32])
        S.ts("dve", nr[:], APr[:, :, 1], -1.0, None, ALU.add)
        S.tt("dve", t1[:], lamre[:], lamre[:], ALU.mult)
        S.tt("dve", t2[:], lamim[:], lamim[:], ALU.mult)
        S.tt("dve", den[:], t1[:], t2[:], ALU.add)
        S.op("dve", lambda e: e.reciprocal(out=inv[:], in_=den[:]), reads=["den"], writes=["inv"])
        S.tt("dve", t1[:], nr[:], lamre[:], ALU.mult)
        S.tt("dve", t2[:], APi[:, :, 1], lamim[:], ALU.mult)
        S.tt("dve", t1[:], t1[:], t2[:], ALU.add)
        S.tt("dve", qr[:], t1[:], inv[:], ALU.mult)
        S.tt("dve", t1[:], APi[:, :, 1], lamre[:], ALU.mult)
        S.tt("dve", t2[:], nr[:], lamim[:], ALU.mult)
        S.tt("dve", t1[:], t1[:], t2[:], ALU.subtract)
        S.tt("dve", qi[:], t1[:], inv[:], ALU.mult)
        Bbr, Bbi = sb("Bbr", [128, 32, 16]), sb("Bbi", [128, 32, 16])
        V1, V2 = sb("V1", [128, 32, 16]), sb("V2", [128, 32, 16])
        b16 = lambda ap: ap.unsqueeze(2).to_broadcast([128, 32, 16])
        cmul(S, "dve", Bbr[:], Bbi[:], b16(qr[:]), b16(qi[:]), Bre[:], Bim[:], V1[:], V2[:])
        Gbd = sb("Gbd", [128, 32, 2, 8, 32], BF16)
        Bbd = sb("Bbd", [128, 32, 2, 32], BF16)
        S.memset("pool", Gbd[:], 0.0)
        S.memset("pool", Bbd[:], 0.0)
        S.memset("pool", WY[:], 0.0)
        W1, W2 = sb("W1", [128, 32, 9, 16]), sb("W2", [128, 32, 9, 16])

        def bk(ap, n):
            return ap.unsqueeze(3).to_broadcast([128, 32, n, 16])

        def bh(ap, n):
            return ap.unsqueeze(2).to_broadcast([128, 32, n, 16])

        def halves(dst_fn, src, n):
            for g2 in range(2):
                P = slice(64 * g2, 64 * g2 + 64)
                S.copy("act", dst_fn(P, slice(16 * g2, 16 * g2 + 16)), src[P])

        S.tt("dve", W2[:, :, 0:8, :], bk(APi[:, :, 0:8], 8), bh(Bbi[:], 8), ALU.mult)
        S.tt("dve", W1[:, :, 0:8, :], bk(APr[:, :, 0:8], 8), bh(Bbr[:], 8), ALU.mult)
        S.tt("dve", W1[:, :, 0:8, :], W1[:, :, 0:8, :], W2[:, :, 0:8, :], ALU.subtract)
        halves(lambda P, Cc: Gbd[P, :, 0, :, Cc], W1[:, :, 0:8, :], 8)
        S.tt("dve", W2[:, :, 0:8, :], bk(APi[:, :, 0:8], 8), bh(Bbr[:], 8), ALU.mult)
        S.tt("dve", W1[:, :, 0:8, :], bk(APr[:, :, 0:8], 8), bh(Bbi[:], 8), ALU.mult)
        S.tt("dve", W1[:, :, 0:8, :], W1[:, :, 0:8, :], W2[:, :, 0:8, :], ALU.add)
        halves(lambda P, Cc: Gbd[P, :, 1, :, Cc], W1[:, :, 0:8, :], 8)
        halves(lambda P, Cc: Bbd[P, :, 0, Cc], Bbr[:], 1)
        halves(lambda P, Cc: Bbd[P, :, 1, Cc], Bbi[:], 1)
        S.tt("dve", W2[:], bk(APi[:], 9), bh(Cim[:], 9), ALU.mult)
        S.tt("dve", W1[:], bk(APr[:], 9), bh(Cre[:], 9), ALU.mult)
        S.tt("dve", W1[:], W1[:], W2[:], ALU.subtract)
        halves(lambda P, Cc: WY[P, :, 0, :, Cc], W1[:], 9)
        S.tt("dve", W2[:], bk(APr[:], 9), bh(Cim[:], 9), ALU.mult)
        S.tt("dve", W1[:], bk(APi[:], 9), bh(Cre[:], 9), ALU.mult)
        S.tt("dve", W1[:], W1[:], W2[:], ALU.add)
        S.ts("dve", W1[:], W1[:], -1.0, None, ALU.mult)
        halves(lambda P, Cc: WY[P, :, 1, :, Cc], W1[:], 9)

        Kps = [E(nc.psum_tensor("Kps%d" % d, [128, 4, 8, 32], F32)) for d in range(2)]
        for d in range(2):
            for pr in range(16):
                u = d * 16 + pr
                q, pq = divmod(pr, 4)
                for ri in range(2):
                    S.mm(Kps[d][32 * pq:32 * pq + 32, q, :, :], Bbd[:, u, ri, :], WY[:, u, ri, 0:8, :],
                         start=(ri == 0), stop=(ri == 1), inc=(ri == 1 and pr == 15), tp=(0, 32 * pq))
        Kf, Kb = sb("Kf", [128, 4, 8, 32]), sb("Kb", [128, 4, 8, 32])
        S.copy("dve", Kf[:], Kps[0][:])
        S.copy("dve", Kb[:], Kps[1][:])
        Dd = sb("Dd", [128, 4, 32])
        S.tt("dve", Dd[:], m32[:].unsqueeze(1).to_broadcast([128, 4, 32]), dsk[:].unsqueeze(2).to_broadcast([128, 4, 32]), ALU.mult)
        S.tt("dve", Dd[:], Dd[:], Kf[:, :, 0, :], ALU.add)
        S.tt("dve", Dd[:], Dd[:], Kb[:, :, 0, :], ALU.add)
        S.copy("dve", INTRA[:, :, 0:64:9, :], Dd[:].unsqueeze(2).to_broadcast([128, 4, 8, 32]))
        for k in range(1, 8):
            n = 8 - k
            S.copy("dve", INTRA[:, :, k:k + 9 * (n - 1) + 1:9, :], Kf[:, :, k:k + 1, :].to_broadcast([128, 4, n, 32]))
            S.copy("dve", INTRA[:, :, 8 * k:8 * k + 9 * (n - 1) + 1:9, :], Kb[:, :, k:k + 1, :].to_broadcast([128, 4, n, 32]))

        psT = [E(nc.psum_tensor("psWX%d" % i, [128, 8, 128], BF16)) for i in range(2)]
        n = 0
        for q in range(4):
            for d in range(2):
                for ri in range(2):
                    pt = psT[n % 2]
                    n += 1
                    for e in range(8):
                        for pq in range(4):
                            u = d * 16 + 4 * q + pq
                            S.tr(pt[32 * pq:32 * pq + 32, e, :], Gbd[:, u, ri, e, :], c.identb[:],
                                 inc=(e == 7 and pq == 3), tp=(0, 32 * pq))
                    S.copy("act", WX[:, q, d, ri, :, :], pt[:])
        S.barrier()


def s5_main(c, U8r, U8i, R8):
    nc, S = c.nc, c.S
    Lmax = max(c.seq_lens)
    NCm = Lmax // 8
    with contextlib.ExitStack() as es:
        E = es.enter_context
        sb = lambda n, sh, dt=F32: E(nc.sbuf_tensor(n, sh, dt))
        WXq = sb("WXq", [128, 2, 2, 8, 128], BF16)
        WYq = sb("WYq", [128, 8, 2, 9, 32], BF16)
        INq = sb("INq", [128, 64, 32], BF16)
        Ec, Es = sb("Ec", [128, 8, NCm]), sb("Es", [128, 8, NCm])
        pwr, pwi = sb("pwr", [128, 8]), sb("pwi", [128, 8])
        pt1, pt2, pt3 = sb("pt1", [128, 8]), sb("pt2", [128, 8]), sb("pt3", [128, 8])
        zum = [[sb("zum%d_%d" % (r, i), [128, Lmax], BF16) for i in range(4)] for r in range(2)]
        Sbf = [sb("Sbf%d" % r, [128, 8, 2, NCm + 1], BF16) for r in range(2)]
        tmps = [[sb("tm%s%d" % (nm, i), [128, NCm]) for nm in "ABCDEF"] for i in range(2)]
        yg = sb("ygB", [128, Lmax])
        wst = sb("wupstB", [128, 2 * DFF], BF16)
        ET1 = yg[:, 0:Lmax // 2].rearrange("p (u k) -> p u k", u=8)
        ET2 = yg[:, Lmax // 2:Lmax].rearrange("p (u k) -> p u k", u=8)
        Xps = [[E(nc.psum_tensor("Xps%d_%d" % (i, ri), [128, 512], F32)) for ri in range(2)] for i in range(2)]
        Yps = [E(nc.psum_tensor("Yps%d" % i, [128, 512], F32)) for i in range(2)]
        for r in range(2):
            S.memset("pool" if r == 0 else "dve", Sbf[r][:], 0.0)
            for pq in range(4):
                if r == 0:
                    S.memset("pool", zum[r][pq][:], 0.0)
                else:
                    S.op("act", lambda e, t_=zum[r][pq]: e.memzero(t_[:]), reads=[], writes=[_nm(zum[r][pq][:])])
        nseq = len(c.seqs)

        def scan_prologue(q):
            S.dma("sp", "wq0", WXq[:].rearrange("p a b e n -> p (a b e n)"), c.WX_d[q], reads=["WX_d"])
            for d in range(2):
                usl = slice(d * 16 + 4 * q, d * 16 + 4 * q + 4)
                S.copy("dve", pwr[:, 4 * d:4 * d + 4], U8r[:, usl])
                S.copy("dve", pwi[:, 4 * d:4 * d + 4], U8i[:, usl])
            S.memset("dve", Ec[:, :, 0:1], 1.0)
            S.memset("dve", Es[:, :, 0:1], 0.0)
            seg = 1
            while seg < NCm:
                bcs = lambda ap: ap.unsqueeze(2).to_broadcast([128, 8, seg])
                cmul(S, "dve", Ec[:, :, seg:2 * seg], Es[:, :, seg:2 * seg], Ec[:, :, 0:seg], Es[:, :, 0:seg],
                     bcs(pwr[:]), bcs(pwi[:]), ET1[:, :, 0:seg], ET2[:, :, 0:seg])
                seg *= 2
                if seg < NCm:
                    S.tt("dve", pt1[:], pwr[:], pwr[:], ALU.mult)
                    S.tt("dve", pt2[:], pwi[:], pwi[:], ALU.mult)
                    S.tt("dve", pt3[:], pwr[:], pwi[:], ALU.mult)
                    S.tt("dve", pwr[:], pt1[:], pt2[:], ALU.subtract)
                    S.ts("dve", pwi[:], pt3[:], 2.0, None, ALU.mult)

        def out_prologue(q):
            for d in range(2):
                u0 = d * 16 + 4 * q
                S.dma("sp", "wq1", WYq[:, 4 * d:4 * d + 4].rearrange("p u a k n -> p u (a k n)"),
                      c.WY_d[u0:u0 + 4].rearrange("u p x -> p u x"), reads=["WY_d"])
            S.dma("sp", "wq2", INq[:].rearrange("p j n -> p (j n)"), c.IN_d[q], reads=["IN_d"])

        if True:
            NIT = 4 * nseq

            def step(t):
                g, h = t, t - 1
                gv, hv = g < NIT, h >= 0
                if t < 8:
                    S.dma("pool", "wupstB", wst[:], c.w_up[t * 128:(t + 1) * 128, :])
                if gv:
                    q, k = divmod(g, nseq)
                    if k == 0:
                        scan_prologue(q)
                    s0, L = c.seqs[k]
                    n_c = L // 8
                    r = g % 2
                    for pq in range(4):
                        P = slice(32 * pq, 32 * pq + 32)
                        S.dma("sp" if pq % 2 == 0 else "act", "zuld%d_%d" % (r, pq), zum[r][pq][P, 0:L], c.zuT_d[q, P, s0:s0 + L],
                              reads=[("zuT_d", s0 + 512 * i) for i in range(L // 512)])
                    zdi = [zum[r][pq][:, 0:L].rearrange("p (t j c) -> p t j c", j=8, c=64) for pq in range(4)]
                if hv:
                    qh, kh = divmod(h, nseq)
                    if kh == 0:
                        out_prologue(qh)
                    s0h, Lh = c.seqs[kh]
                    n_ch = Lh // 8
                    rh = h % 2
                    zdh = [zum[rh][pq][:, 0:Lh].rearrange("p (t j c) -> p t j c", j=8, c=64) for pq in range(4)]
                for pair in range(4):
                    if gv:
                        ops = []
                        for uu in (2 * pair, 2 * pair + 1):
                            d, pq = divmod(uu, 4)
                            u = d * 16 + 4 * q + pq
                            xp = Xps[uu % 2]
                            for ri in range(2):
                                for j in range(8):
                                    e = 7 - j if d == 0 else j
                                    S.mm(xp[ri][:, 0:n_c], WXq[:, d, ri, e, :], zdi[pq][:, :, j, :], start=(j == 0), stop=(j == 7))
                            ec, esn = Ec[:, uu, 0:n_c], Es[:, uu, 0:n_c]
                            if d == 0:
                                xr, xi = xp[0][:, 0:n_c], xp[1][:, 0:n_c]
                                o_re, o_im = Sbf[r][:, uu, 0, 1:n_c + 1], Sbf[r][:, uu, 1, 1:n_c + 1]
                            else:
                                xr, xi = xp[0][:, 0:n_c][:, ::-1], xp[1][:, 0:n_c][:, ::-1]
                                o_re, o_im = Sbf[r][:, uu, 0, 0:n_c][:, ::-1], Sbf[r][:, uu, 1, 0:n_c][:, ::-1]
                            a, b, cc, dd, ee, ff = [tt_[:, 0:n_c] for tt_ in tmps[uu % 2]]
                            r8 = R8[:, u:u + 1].to_broadcast([128, n_c])
                            chain = [
                                lambda a=a, xr=xr, ec=ec: S.tt("dve", a, xr, ec, ALU.mult),
                                lambda b=b, xi=xi, esn=esn: S.tt("dve", b, xi, esn, ALU.mult),
                                lambda a=a, b=b: S.tt("dve", a, a, b, ALU.add),
                                lambda cc=cc, xi=xi, ec=ec: S.tt("dve", cc, xi, ec, ALU.mult),
                                lambda dd=dd, xr=xr, esn=esn: S.tt("dve", dd, xr, esn, ALU.mult),
                                lambda cc=cc, dd=dd: S.tt("dve", cc, cc, dd, ALU.subtract),
                                lambda a=a, b=b, r8=r8: S.scan(b, r8, a),
                                lambda cc=cc, dd=dd, r8=r8: S.scan(dd, r8, cc),
                                lambda a=a, b=b, ec=ec: S.tt("dve", a, b, ec, ALU.mult),
                                lambda ee=ee, dd=dd, esn=esn: S.tt("pool", ee, dd, esn, ALU.mult),
                                lambda cc=cc, dd=dd, ec=ec: S.tt("dve", cc, dd, ec, ALU.mult),
                                lambda ff=ff, b=b, esn=esn: S.tt("pool", ff, b, esn, ALU.mult),
                                lambda o_re=o_re, a=a, ee=ee: S.tt("pool", o_re, a, ee, ALU.subtract),
                                lambda o_im=o_im, cc=cc, ff=ff: S.tt("pool", o_im, cc, ff, ALU.add),
                            ]
                            if d == 1:
                                chain.append(lambda uu=uu: S.memset("pool", Sbf[r][:, uu, :, n_c:n_c + 1], 0.0))
                            ops.append(chain)
                        for i_ in range(max(len(ch) for ch in ops)):
                            for ch in ops:
                                if i_ < len(ch):
                                    ch[i_]()
                    if hv:
                        for j in (2 * pair, 2 * pair + 1):
                            yp = Yps[j % 2]
                            for pq in range(4):
                                P = slice(32 * pq, 32 * pq + 32)
                                first = True
                                for d in range(2):
                                    uu = d * 4 + pq
                                    kk = j + 1 if d == 0 else 8 - j
                                    for ri in range(2):
                                        rhs = Sbf[rh][:, uu, ri, 0:n_ch] if d == 0 else Sbf[rh][:, uu, ri, 1:n_ch + 1]
                                        S.mm(yp[P, 0:n_ch], WYq[:, uu, ri, kk, :], rhs, start=first, stop=False, inc=False, tp=(0, 32 * pq))
                                        first = False
                            for pq in range(4):
                                P = slice(32 * pq, 32 * pq + 32)
                                for jp in range(8):
                                    S.mm(yp[P, 0:n_ch], INq[:, jp * 8 + j, :], zdh[pq][:, :, jp, :], start=False, stop=(jp == 7),
                                         inc=(jp == 7 and pq == 3), tp=(0, 32 * pq))
                            S.act(yg[:, j:Lh:8], yp[:, 0:n_ch], AF.Gelu_apprx_tanh)
                if hv:
                    S.dma("pool", "ygst", c.yg_d[qh, :, s0h:s0h + Lh], yg[:, 0:Lh], writes=[("yg_d", qh, s0h)])
                if t < 8:
                    S.dma_split("act", "wupwrB", c.wup_d[:, :, t, :].rearrange("m p c -> p m c"),
                                wst[:].rearrange("p (m c) -> p m c", c=128), 1, 4, writes=["wup_d"])

            assert NIT + 1 >= 8
            for t in range(NIT + 1):
                step(t)
        S.barrier()


def phase_B2(c):
    nc, S = c.nc, c.S
    with contextlib.ExitStack() as es:
        E = es.enter_context
        sb = lambda n, sh, dt=F32: E(nc.sbuf_tensor(n, sh, dt))
        wg = sb("wg", [128, 4, DC], BF16)
        bg, gns = sb("bg", [128, 4]), sb("gns", [128, 4])
        ygf = [sb("ygf%d" % i, [128, 4, 512]) for i in range(3)]
        ygb = [sb("ygb%d" % i, [128, 4, 512], BF16) for i in range(2)]
        sg = [sb("sg%d" % i, [128, 512]) for i in range(4)]
        y2 = [sb("y2_%d" % i, [128, 4, 512]) for i in range(3)]
        sq = [sb("sqB%d" % i, [128, 512], BF16) for i in range(8)]
        var = [sb("varB%d" % i, [128, 512]) for i in range(2)]
        rst = [sb("rstB%d" % i, [128, 512]) for i in range(2)]
        y2n = [sb("y2n%d" % i, [128, 4, 512], BF16) for i in range(2)]
        ps_g = [E(nc.psum_tensor("psB_g%d" % i, [128, 512], F32)) for i in range(3)]
        ps_s = [E(nc.psum_tensor("psB_s%d" % i, [128, 512], F32)) for i in range(2)]
        for kt in range(4):
            S.dma("pool", "wgld", wg[:, kt, :], c.w_glu[kt * 128:(kt + 1) * 128, :])
        for kt in range(8):
            S.dma("pool", "wold", c.wo[:, kt, :], c.w_out[kt * 128:(kt + 1) * 128, :])
        for kt in range(22):
            S.dma("pool", "wdnld", c.wdn[:, kt, :], c.w_down[kt * 128:(kt + 1) * 128, :])
        S.dma("sp", "cB0", bg[:], c.b_glu[0, :].rearrange("(f p) -> p f", p=128), allow_slow_non_contiguous=True)
        S.dma("sp", "cB1", gns[:], c.gn_ssm[0, :].rearrange("(f p) -> p f", p=128), allow_slow_non_contiguous=True)
        tiles = []
        for (s0_, L) in c.seqs:
            for i in range(L // 512):
                tiles.append((s0_ + i * 512, s0_))
        NTI = len(tiles)

        def load(i):
            if i >= NTI:
                return
            t0, s0_ = tiles[i]
            S.dma("sp", "ygld%d" % (i % 3), ygf[i % 3][:], c.yg_d[:, :, t0:t0 + 512].rearrange("q p t -> p q t"),
                  reads=[("yg_d", q, s0_) for q in range(4)])

        def st0(i):
            for kt in range(4):
                S.copy("act" if kt % 2 else "dve", ygb[i % 2][:, kt, :], ygf[i % 3][:, kt, :], wk=["ygb%d_%d" % (i % 2, kt)])

        def st1(i):
            for m in range(4):
                pg = ps_g[m % 3]
                for kt in range(4):
                    S.mm(pg[:], wg[:, kt, m * 128:(m + 1) * 128], ygb[i % 2][:, kt, :], start=(kt == 0), stop=(kt == 3),
                         rk=["wg", "ygb%d_%d" % (i % 2, kt)])
                S.act(sg[m][:], pg[:], AF.Sigmoid, bias=bg[:, m:m + 1])
            for m in range(4):
                S.tt("dve", y2[i % 3][:, m, :], ygf[i % 3][:, m, :], sg[m][:], ALU.mult, wk=["y2_%d_%d" % (i % 3, m)])
            for m in range(4):
                S.act(sq[(i % 2) * 4 + m][:], y2[i % 3][:, m, :], AF.Square, rk=["y2_%d_%d" % (i % 3, m)])
            load(i + 3)

        def st1b(i):
            for m in range(4):
                S.mm(ps_s[i % 2][:], c.onesb[:], sq[(i % 2) * 4 + m][:], start=(m == 0), stop=(m == 3), inc=True)

        def st2(i):
            t0 = tiles[i][0]
            rsqrt_act(S, rst[i % 2][:], ps_s[i % 2][:], 1.0 / DC, var[i % 2][:], c.epsc[:])
            for m in range(4):
                S.stt(y2n[i % 2][:, m, :], y2[i % 3][:, m, :], gns[:, m:m + 1], rst[i % 2][:], ALU.mult, ALU.mult,
                      rk=["y2_%d_%d" % (i % 3, m), "gns", _nm(rst[i % 2][:])])
            S.dma("sp", "y2nst%d" % (i % 2), c.ycat_d[4:8, :, t0:t0 + 512].rearrange("f p t -> p f t"), y2n[i % 2][:],
                  writes=[("ycat_d", 1, t0)])

        for i in range(3):
            load(i)
        pipeline(NTI, [st0, st1, st1b, st2])
        S.barrier()


def phase_C1(c):
    nc, S = c.nc, c.S
    with contextlib.ExitStack() as es:
        E = es.enter_context
        sb = lambda n, sh, dt=F32: E(nc.sbuf_tensor(n, sh, dt))
        wo = c.wo
        gpost, gffn = sb("gpost", [128, D]), sb("gffn", [128, D])
        yc = [sb("ycC%d" % i, [128, 8, 512], BF16) for i in range(2)]
        xt = [sb("xtC%d" % i, [128, 4, D]) for i in range(2)]
        junk = sb("junkC", [128, D], BF16)
        st = [sb("stC%d" % i, [128, 8]) for i in range(3)]
        tmp = [sb("tmpC%d" % i, [128, D]) for i in range(2)]
        x1 = [sb("x1C%d" % i, [128, D]) for i in range(3)]
        h2 = [sb("h2C%d" % i, [128, D], BF16) for i in range(3)]
        h2T = [sb("h2TC%d" % i, [128, 8, 512], BF16) for i in range(2)]
        ps_o = [E(nc.psum_tensor("psC_o%d" % i, [128, D], F32)) for i in range(3)]
        ps_t = [E(nc.psum_tensor("psC_t%d" % i, [128, D], BF16)) for i in range(2)]
        S.dma("sp", "cC0", gpost[:], c.post_mix_g.to_broadcast([128, D]))
        S.dma("sp", "cC1", gffn[:], c.pre_ffn_g.to_broadcast([128, D]))
        tiles = []
        for (s0_, L) in c.seqs:
            for i in range(L // 512):
                tiles.append(s0_ + i * 512)
        NTI = len(tiles)

        def load(ti):
            if ti >= NTI:
                return
            t0 = tiles[ti]
            S.dma("sp", "ycld%d" % (ti % 2), yc[ti % 2][:], c.ycat_d[:, :, t0:t0 + 512].rearrange("f p t -> p f t"),
                  reads=[("ycat_d", 0, t0), ("ycat_d", 1, t0)])
            S.dma("sp", "xldC%d" % (ti % 2), xt[ti % 2][:], c.x[t0:t0 + 512, :].rearrange("(g p) d -> p g d", p=128))

        def s0(i):
            ti, g = divmod(i, 4)
            s = ti % 2
            if g == 1:
                load(ti + 1)
            po = ps_o[i % 3]
            for hh in range(2):
                for kt in range(8):
                    S.mm(po[:, hh * 512:(hh + 1) * 512], yc[s][:, kt, g * 128:(g + 1) * 128], wo[:, kt, hh * 512:(hh + 1) * 512],
                         start=(kt == 0), stop=(kt == 7), inc=(kt == 7 and hh == 1))

        def s1(i):
            ti, g = divmod(i, 4)
            s, b3, t0 = ti % 2, i % 3, tiles[ti]
            po, sv = ps_o[b3], st[b3]
            S.act(junk[:], po[:], AF.Square, accum_out=sv[:, 0:1], wk=["junkC", "stC%d_a" % b3])
            rsqrt_act(S, sv[:, 2:3], sv[:, 0:1], 1.0 / D, sv[:, 1:2], c.epsc[:], rk=["stC%d_a" % b3], tk=["stC%d_b" % b3], wk=["stC%d_c" % b3])
            S.stt(tmp[i % 2][:], po[:], sv[:, 2:3], gpost[:], ALU.mult, ALU.mult, rk=[_nm(po[:]), "stC%d_c" % b3, "gpost"])
            S.tt("dve", x1[b3][:], tmp[i % 2][:], xt[s][:, g, :], ALU.add)
            S.dma("pool", "x1st%d" % b3, c.x1_d[t0 + g * 128:t0 + (g + 1) * 128, :], x1[b3][:], writes=[("x1_d", t0 + g * 128)])

        def s2(i):
            b3 = i % 3
            sv = st[b3]
            S.act(junk[:], x1[b3][:], AF.Square, accum_out=sv[:, 3:4], wk=["junkC", "stC%d_d" % b3])
            rsqrt_act(S, sv[:, 5:6], sv[:, 3:4], 1.0 / D, sv[:, 4:5], c.epsc[:], rk=["stC%d_d" % b3], tk=["stC%d_e" % b3], wk=["stC%d_f" % b3])
            S.stt(h2[b3][:], x1[b3][:], sv[:, 5:6], gffn[:], ALU.mult, ALU.mult, rk=[_nm(x1[b3][:]), "stC%d_f" % b3, "gffn"])

        def s3(i):
            ti, g = divmod(i, 4)
            s, b3, t0 = ti % 2, i % 3, tiles[ti]
            pt = ps_t[i % 2]
            for kt in range(8):
                S.tr(pt[:, kt * 128:(kt + 1) * 128], h2[b3][:, kt * 128:(kt + 1) * 128], c.identb[:], inc=(kt == 7))
            S.copy("act", h2T[s][:, :, g * 128:(g + 1) * 128], pt[:].rearrange("p (k t) -> p k t", k=8))
            if g == 3:
                S.dma("pool", "h2Tst%d" % s, c.h2T_d[:, :, t0:t0 + 512].rearrange("f p t -> p f t"), h2T[s][:],
                      writes=[("h2T_d", t0)])

        load(0)
        pipeline(NTI * 4, [s0, s1, s2, s3])
        S.barrier()


def phase_C2(c):
    nc, S = c.nc, c.S
    BLK = 1024
    with contextlib.ExitStack() as es:
        E = es.enter_context
        sb = lambda n, sh, dt=F32: E(nc.sbuf_tensor(n, sh, dt))
        wdn = c.wdn
        fw, fb = sb("fw", [128, NM_UP, 3]), sb("fb", [128, NM_UP])
        gpo = sb("gpo", [128, D])
        wu = [sb("wu%d" % i, [128, 8, 128], BF16) for i in range(4)]
        hT = [sb("hTF%d" % i, [128, 8, BLK + 2], BF16) for i in range(2)]
        actT = sb("actT", [128, 22, BLK], BF16)
        ag = [sb("agF%d" % i, [128, BLK]) for i in range(2)]
        av = [sb("avF%d" % i, [128, BLK]) for i in range(2)]
        sgl = [sb("sgF%d" % i, [128, BLK]) for i in range(2)]
        hv = [sb("hvF%d" % i, [128, 2]) for i in range(2)]
        x1 = [sb("x1F%d" % i, [128, D]) for i in range(2)]
        junk = sb("junkF", [128, D], BF16)
        st = [sb("stF%d" % i, [128, 4]) for i in range(2)]
        tmp = [sb("tmpF%d" % i, [128, D]) for i in range(2)]
        yo = [sb("yoF%d" % i, [128, D]) for i in range(2)]
        PS = [E(nc.psum_tensor("psF%d" % i, [128, 1024], F32)) for i in range(3)]
        PHs = [E(nc.psum_tensor("psFh%d" % i, [128, 512], F32)) for i in range(2)]
        for j in range(3):
            S.dma_split("sp", "cF0", fw[:, :, j], c.ffn_conv_w[j, :].rearrange("(m p) -> p m", p=128), 1, 11,
                        allow_slow_non_contiguous=True)
        S.dma_split("sp", "cF1", fb[:], c.ffn_conv_b[0, :].rearrange("(m p) -> p m", p=128), 1, 11, allow_slow_non_contiguous=True)
        S.dma("sp", "cF2", gpo[:], c.post_ffn_g.to_broadcast([128, D]))
        blocks = []
        for (s0, L) in c.seqs:
            nb = L // BLK
            for i in range(nb):
                blocks.append((s0 + i * BLK, i == 0, i == nb - 1))
        nwu = [0]

        def load_wu(m):
            w = wu[nwu[0] % 4]
            S.dma("sp", "wuld%d" % (nwu[0] % 4), w[:], c.wup_d[m, :, :, :], reads=["wup_d"])
            nwu[0] += 1
            return w

        def load_h(bi):
            t0, first, last = blocks[bi]
            h = hT[bi % 2]
            lo = 1 if first else 0
            hi = BLK + 1 if last else BLK + 2
            S.dma("sp", "hld%d" % (bi % 2), h[:, :, lo:hi], c.h2T_d[:, :, t0 - 1 + lo:t0 - 1 + hi].rearrange("f p t -> p f t"),
                  reads=[("h2T_d", t0 + 512 * k) for k in range(-1 if not first else 0, 3 if not last else 2)])
            if first:
                S.memset("pool", h[:, :, 0:1], 0.0)
            if last:
                S.memset("pool", h[:, :, BLK + 1:BLK + 2], 0.0)

        load_h(0)
        nps = 0
        nh = 0
        nd = 0
        for bi, (t0, first, last) in enumerate(blocks):
            h = hT[bi % 2]
            if bi + 1 < len(blocks):
                load_h(bi + 1)
            interior = (not first) or (not last)
            for mp in range(22):
                res = []
                for which, m in enumerate((mp, 22 + mp)):
                    w = load_wu(m)
                    ps = PS[nps % 3]
                    nps += 1
                    for hh in range(2):
                        for kt in range(8):
                            S.mm(ps[:, hh * 512:(hh + 1) * 512], w[:, kt, :], h[:, kt, 1 + hh * 512:1 + (hh + 1) * 512],
                                 start=(kt == 0), stop=(kt == 7), inc=(kt == 7 and hh == 1))
                    PH = PHs[nh % 2]
                    nh += 1
                    for kt in range(8):
                        S.mm(PH[:, 0:2], w[:, kt, :], h[:, kt, 0:BLK + 2:BLK + 1], start=(kt == 0), stop=(kt == 7))
                    a = (ag if which == 0 else av)[mp % 2]
                    S.act(a[:], ps[:], AF.Identity, scale=fw[:, m, 1:2], bias=fb[:, m:m + 1])
                    S.stt(a[:, 1:BLK], ps[:, 0:BLK - 1], fw[:, m, 0:1], a[:, 1:BLK], ALU.mult, ALU.add)
                    S.stt(a[:, 0:BLK - 1], ps[:, 1:BLK], fw[:, m, 2:3], a[:, 0:BLK - 1], ALU.mult, ALU.add)
                    hvv = hv[which]
                    S.tt("dve", hvv[:], PH[:, 0:2], fw[:, m, 0:3:2], ALU.mult)
                    S.tt("dve", a[:, 0:BLK:BLK - 1], a[:, 0:BLK:BLK - 1], hvv[:], ALU.add)
                    res.append(a)
                g = sgl[mp % 2]
                S.act(g[:], res[0][:], AF.Silu)
                S.tt("pool", actT[:, mp, :], g[:], res[1][:], ALU.mult, wk=["actT_%d" % mp])
            for tg in range(BLK // 128):
                b = nd % 2
                nd += 1
                PD = PS[nps % 3]
                nps += 1
                tt0 = t0 + tg * 128
                S.dma("sp", "x1ld%d" % b, x1[b][:], c.x1_d[tt0:tt0 + 128, :], reads=[("x1_d", tt0)])
                for hh in range(2):
                    for kt in range(22):
                        S.mm(PD[:, hh * 512:(hh + 1) * 512], actT[:, kt, tg * 128:(tg + 1) * 128], wdn[:, kt, hh * 512:(hh + 1) * 512],
                             start=(kt == 0), stop=(kt == 21), inc=(kt == 21 and hh == 1), rk=["actT_%d" % kt, "wdn"])
                sv = st[b]
                S.act(junk[:], PD[:], AF.Square, accum_out=sv[:, 0:1], wk=["junkF", "stF%d_a" % b])
                rsqrt_act(S, sv[:, 2:3], sv[:, 0:1], 1.0 / D, sv[:, 1:2], c.epsc[:], rk=["stF%d_a" % b], tk=["stF%d_b" % b], wk=["stF%d_c" % b])
                S.stt(tmp[b][:], PD[:], sv[:, 2:3], gpo[:], ALU.mult, ALU.mult, rk=[_nm(PD[:]), "stF%d_c" % b, "gpo"])
                S.tt("pool", yo[b][:], tmp[b][:], x1[b][:], ALU.add)
                S.dma("pool", "yst%d" % b, c.y[tt0:tt0 + 128, :], yo[b][:], writes=[("y", tt0)])
        S.barrier()


SEQ_LENS = (2048, 2048, 4096, 4096)
_CONSTS = {
    "ident": np.eye(128, dtype=np.float32),
    "mask32": (np.arange(128)[:, None] % 32 == np.arange(32)[None, :]).astype(np.float32),
}
_WKEYS = ["pre_mix_g", "w_in", "conv_w", "lam_re", "lam_im", "log_step", "b_re", "b_im", "c_re", "c_im", "d_skip",
          "w_glu", "b_glu", "gn_conv", "gn_ssm", "w_out", "post_mix_g", "pre_ffn_g", "w_up", "ffn_conv_w",
          "ffn_conv_b", "w_down", "post_ffn_g"]


def _weights_map(inputs):
    m = {}
    for k in _WKEYS:
        a = np.asarray(inputs[k], dtype=np.float32)
        a = a[0]
        if a.ndim == 1:
            a = a[None, :]
        m[k] = np.ascontiguousarray(a)
    m.update(_CONSTS)
    return m


def kernel(**inputs):
    xp = np.asarray(inputs["x_prompt"], dtype=np.float32)
    xs = np.asarray(inputs["x_sample"], dtype=np.float32)
    n = 8
    nc = build(SEQ_LENS)
    wm = _weights_map(inputs)
    in_maps = []
    for i in range(n):
        xc = np.concatenate([xp[2 * i].reshape(-1, D), xp[2 * i + 1].reshape(-1, D),
                             xs[2 * i].reshape(-1, D), xs[2 * i + 1].reshape(-1, D)], axis=0)
        d = dict(wm)
        d["x"] = np.ascontiguousarray(xc)
        in_maps.append(d)
    res = run_bass_kernel_spmd(nc, in_maps, core_ids=list(range(n)))
    yp = np.empty_like(xp)
    ys = np.empty_like(xs)
    for i in range(n):
        y = res.results[i]["y"]
        yp[2 * i] = y[0:2048]
        yp[2 * i + 1] = y[2048:4096]
        ys[2 * i] = y[4096:8192]
        ys[2 * i + 1] = y[8192:12288]
    return (yp, ys)
```
